# Optimizing a Trainium2 kernel written in Bass

```python
import math
import jax
import jax.numpy as jnp
from jax import lax
import numpy as np

D_MODEL = 1024
BATCH = 32
SEQ = 256
DEPTH = 2
DEC_BATCH = 8
DEC_SEQ = 4096
PAST_LEN = 512

GRID_W = 64
HEAD_DIM = 64
D_MIX = D_MODEL
GROUP_W = D_MIX // 4
NORM_EPS = 1e-6
NEG_INF = -1e30
GQA_HEADS = GROUP_W // HEAD_DIM
GQA_KV_HEADS = GQA_HEADS // 2
GQA_GROUP = GQA_HEADS // GQA_KV_HEADS
ROPE_THETA = 10000.0
Q_BLOCK = 128
RW_HEADS = GROUP_W // HEAD_DIM
RW_DECAY_RANK = 32
RW_AAA_RANK = 32
RW_GATE_RANK = 64
RW_LN_EPS = 64e-5
NA_HEADS = GROUP_W // HEAD_DIM
NA_ROWS = 8
NA_COLS = 16
NA_QCOLS = 16
NA_SPAN = NA_QCOLS + NA_COLS
DN_HEADS = GROUP_W // HEAD_DIM
DN_CONV = 5
DN_CHUNK = 64
PEER_HEADS = 8
PEER_NKEYS = 128
PEER_EXPERTS = PEER_NKEYS * PEER_NKEYS
PEER_DQ = 256
PEER_TOPK = 16
PEER_BLOCK = 128
GQA_PROJ = GROUP_W + 2 * GQA_KV_HEADS * HEAD_DIM
RW_SHIFT_COLS = 3 * GROUP_W + RW_DECAY_RANK + RW_AAA_RANK
RW_PROJ = 3 * GROUP_W + 2 * RW_DECAY_RANK + 2 * RW_AAA_RANK + RW_GATE_RANK
NA_PROJ = 3 * GROUP_W
DN_PROJ = 3 * GROUP_W + 4 * DN_HEADS + GROUP_W
IN_COLS = GQA_PROJ + RW_PROJ + NA_PROJ + DN_PROJ

kernel_name = "hybrid_diffusion_prefix_trunk_step"


def rms_norm(x, g):
    xf = x.astype(jnp.float32)
    y = xf * lax.rsqrt(jnp.mean(xf * xf, axis=-1, keepdims=True) + NORM_EPS)
    return (y * g.astype(jnp.float32)).astype(x.dtype)


def l2_norm(x):
    return x * lax.rsqrt(jnp.sum(x * x, axis=-1, keepdims=True) + NORM_EPS)


def axial_rope_tables(T):
    t = np.arange(T)
    row, col = t // GRID_W, t % GRID_W
    n_freq = HEAD_DIM // 4
    inv = ROPE_THETA ** (-2.0 * np.arange(n_freq) / (HEAD_DIM // 2))
    ang = np.concatenate([row[:, None] * inv[None], col[:, None] * inv[None]], axis=1)
    return jnp.asarray(np.cos(ang), jnp.float32), jnp.asarray(np.sin(ang), jnp.float32)


def apply_rope(x, cos, sin):
    xf = x.astype(jnp.float32)
    half = HEAD_DIM // 2
    x1, x2 = xf[..., :half], xf[..., half:]
    c, s = cos[None, :, None, :], sin[None, :, None, :]
    return jnp.concatenate([x1 * c - x2 * s, x2 * c + x1 * s], axis=-1).astype(x.dtype)


def blocked_attention(q, k, v):
    B, KVH, G, T, dh = q.shape
    nb = T // Q_BLOCK
    qb = jnp.moveaxis(q.reshape(B, KVH, G, nb, Q_BLOCK, dh), 3, 0)
    scale = HEAD_DIM ** -0.5

    def one_block(qblk):
        s = jnp.einsum("bkgqd,bkld->bkgql", qblk, k).astype(jnp.float32) * scale
        p = jax.nn.softmax(s, axis=-1).astype(v.dtype)
        return jnp.einsum("bkgql,bkld->bkgqd", p, v)

    o = lax.map(one_block, qb)
    return jnp.moveaxis(o, 0, 3).reshape(B, KVH, G, T, dh)


def neighbourhood_attention(q, k, v, bias, k_ctx, v_ctx):
    B, H, T, dh = q.shape
    rows = T // GRID_W
    wr = min(NA_ROWS, rows)
    r = np.arange(rows)
    row_idx = np.clip(r - wr // 2, 0, rows - wr)[:, None] + np.arange(wr)[None, :]
    dr_idx = row_idx - r[:, None] + (NA_ROWS - 1)
    qg = q.reshape(B, H, rows, GRID_W, dh)
    kg = k.reshape(B, H, rows, GRID_W, dh)[:, :, row_idx]
    vg = v.reshape(B, H, rows, GRID_W, dh)[:, :, row_idx]
    k_ctx = k_ctx.astype(q.dtype)
    v_ctx = v_ctx.astype(v.dtype)
    scale = HEAD_DIM ** -0.5
    n_loc = wr * NA_SPAN
    outs = []
    for qc0 in range(0, GRID_W, NA_QCOLS):
        kc0 = min(max(qc0 - NA_COLS // 2, 0), GRID_W - NA_SPAN)
        qcols = np.arange(qc0, qc0 + NA_QCOLS)
        kcols = np.arange(kc0, kc0 + NA_SPAN)
        cstart = np.clip(qcols - NA_COLS // 2, 0, GRID_W - NA_COLS)
        valid = (kcols[None, :] >= cstart[:, None]) & (kcols[None, :] < cstart[:, None] + NA_COLS)
        dc_idx = np.clip(kcols[None, :] - qcols[:, None], -(NA_COLS - 1), NA_COLS - 1) + (NA_COLS - 1)
        rel = bias[:, dr_idx[:, None, :, None], dc_idx[None, :, None, :]].astype(jnp.float32)
        qj = qg[:, :, :, qc0:qc0 + NA_QCOLS]
        kj = kg[:, :, :, :, kc0:kc0 + NA_SPAN]
        vj = vg[:, :, :, :, kc0:kc0 + NA_SPAN]
        s_loc = jnp.einsum("bhrqd,bhrwcd->bhrqwc", qj, kj).astype(jnp.float32) * scale + rel[None]
        s_loc = jnp.where(valid[:, None, :], s_loc, NEG_INF)
        s_ctx = jnp.einsum("bhrqd,bhpd->bhrqp", qj, k_ctx).astype(jnp.float32) * scale
        s_all = jnp.concatenate([s_loc.reshape(B, H, rows, NA_QCOLS, n_loc), s_ctx], axis=-1)
        prob = jax.nn.softmax(s_all, axis=-1).astype(v.dtype)
        p_loc = prob[..., :n_loc].reshape(B, H, rows, NA_QCOLS, wr, NA_SPAN)
        o = jnp.einsum("bhrqwc,bhrwcd->bhrqd", p_loc, vj) + jnp.einsum("bhrqp,bhpd->bhrqd", prob[..., n_loc:], v_ctx)
        outs.append(o)
    return jnp.concatenate(outs, axis=3).reshape(B, H, T, dh)


def gqa_mixer(p, lp, ctx_kv):
    B, T, _ = p.shape
    kvw = GQA_KV_HEADS * HEAD_DIM
    q = rms_norm(p[..., :GROUP_W].reshape(B, T, GQA_HEADS, HEAD_DIM), lp["gqa_q_norm"])
    k = rms_norm(p[..., GROUP_W:GROUP_W + kvw].reshape(B, T, GQA_KV_HEADS, HEAD_DIM), lp["gqa_k_norm"])
    v = p[..., GROUP_W + kvw:].reshape(B, T, GQA_KV_HEADS, HEAD_DIM)
    if ctx_kv is not None:
        cos, sin = axial_rope_tables(T)
        q = apply_rope(q, cos, sin)
        k = apply_rope(k, cos, sin)
    qh = q.reshape(B, T, GQA_KV_HEADS, GQA_GROUP, HEAD_DIM).transpose(0, 2, 3, 1, 4)
    kh = k.transpose(0, 2, 1, 3)
    vh = v.transpose(0, 2, 1, 3)
    if ctx_kv is None:
        o = blocked_attention(qh, kh, vh)
    else:
        k_ctx, v_ctx = ctx_kv
        o = blocked_attention(qh, jnp.concatenate([kh, k_ctx.astype(kh.dtype)], axis=2),
                              jnp.concatenate([vh, v_ctx.astype(vh.dtype)], axis=2))
    return o.transpose(0, 3, 1, 2, 4).reshape(B, T, GROUP_W), (kh, vh)


def rwkv_scan(r, w, k, v, a, b, S0, reverse):
    def step(S, inp):
        r_t, w_t, k_t, v_t, a_t, b_t = inp
        sa = jnp.einsum("bhvk,bhk->bhv", S, a_t)
        S = S * w_t[:, :, None, :] + sa[..., None] * b_t[:, :, None, :] + v_t[..., None] * k_t[:, :, None, :]
        return S, jnp.einsum("bhvk,bhk->bhv", S, r_t)

    S, o = lax.scan(step, S0, (r, w, k, v, a, b), reverse=reverse)
    return o, S


def rwkv_mixer(p, lp, S0):
    B, T, _ = p.shape
    f32 = jnp.float32
    W, H = GROUP_W, RW_HEADS
    o0 = 3 * W
    o1 = o0 + 2 * RW_DECAY_RANK
    o2 = o1 + 2 * RW_AAA_RANK
    rkv = p[..., :o0]
    wd = p[..., o0:o1].reshape(B, T, 2, RW_DECAY_RANK)
    ad = p[..., o1:o2].reshape(B, T, 2, RW_AAA_RANK)
    gate = jax.nn.sigmoid(p[..., o2:]) @ lp["rw_g2"]
    heads = lambda a: a.astype(f32).reshape(B, T, H, HEAD_DIM)
    tm = lambda a: jnp.swapaxes(a, 0, 1)
    S0 = S0.astype(f32)
    outs, bonuses, finals = [], [], []
    for d in range(2):
        base = jnp.concatenate([rkv, wd[:, :, d], ad[:, :, d]], axis=-1)
        if d == 0:
            shifted = jnp.pad(base[:, :-1], ((0, 0), (1, 0), (0, 0)))
        else:
            shifted = jnp.pad(base[:, 1:], ((0, 0), (0, 1), (0, 0)))
        xd = base + (shifted - base) * lp["rw_mu"][d]
        r, k, v = xd[..., :W], xd[..., W:2 * W], xd[..., 2 * W:3 * W]
        wl, al = xd[..., 3 * W:3 * W + RW_DECAY_RANK], xd[..., 3 * W + RW_DECAY_RANK:]
        w = -jax.nn.softplus(-(lp["rw_w0"][d] + jnp.tanh(wl) @ lp["rw_w2"][d])) - 0.5
        decay = jnp.exp(-jnp.exp(w.astype(f32)))
        a = jax.nn.sigmoid((lp["rw_a0"][d] + al @ lp["rw_a2"][d]).astype(f32))
        kf = k.astype(f32)
        kk = l2_norm(heads(kf * lp["rw_k_k"].astype(f32)))
        k_eff = heads(kf * (1.0 + (a - 1.0) * lp["rw_k_a"].astype(f32)))
        rh, vh, ah = heads(r), heads(v), heads(a)
        o, S = rwkv_scan(tm(rh), tm(heads(decay)), tm(k_eff), tm(vh), tm(-kk), tm(kk * ah), S0[:, d], d == 1)
        outs.append(tm(o))
        bonuses.append(jnp.sum(rh * k_eff * lp["rw_r_k"].astype(f32), axis=-1, keepdims=True) * vh)
        finals.append(S)
    o = outs[0] + outs[1]
    mu = jnp.mean(o, axis=-1, keepdims=True)
    var = jnp.mean(jnp.square(o - mu), axis=-1, keepdims=True)
    y = ((o - mu) * lax.rsqrt(var + RW_LN_EPS)).reshape(B, T, W) * lp["rw_ln_g"].astype(f32) + lp["rw_ln_b"].astype(f32)
    y = y + (bonuses[0] + bonuses[1]).reshape(B, T, W)
    return y.astype(p.dtype) * gate, jnp.stack(finals, axis=1)


def na_mixer(p, lp, ctx_kv):
    B, T, _ = p.shape
    q, k, v = [p[..., i * GROUP_W:(i + 1) * GROUP_W].reshape(B, T, NA_HEADS, HEAD_DIM).transpose(0, 2, 1, 3) for i in range(3)]
    if ctx_kv is None:
        o = blocked_attention(q[:, :, None], k, v)[:, :, 0]
    else:
        o = neighbourhood_attention(q, k, v, lp["na_bias"], ctx_kv[0], ctx_kv[1])
    return o.transpose(0, 2, 1, 3).reshape(B, T, GROUP_W), (k, v)


def gated_delta_chunked(q, k, v, g, beta, S0):
    B, T, H, dk = q.shape
    dv = v.shape[-1]
    n = T // DN_CHUNK

    def to_chunks(a):
        a = a.reshape((B, n, DN_CHUNK, H) + a.shape[3:])
        return jnp.moveaxis(a, (1, 3), (0, 2))

    qc, kc, vc, bc = to_chunks(q), to_chunks(k), to_chunks(v), to_chunks(beta)
    gc = jnp.cumsum(to_chunks(g), axis=-1)
    idx = np.arange(DN_CHUNK)
    incl = idx[:, None] >= idx[None, :]
    strict = idx[:, None] > idx[None, :]
    diff = gc[..., :, None] - gc[..., None, :]
    dmat = jnp.where(incl, jnp.exp(jnp.where(incl, diff, 0.0)), 0.0)
    kb = kc * bc[..., None]
    A = jnp.where(strict, jnp.einsum("...id,...jd->...ij", kb, kc) * dmat, 0.0)
    eye = jnp.eye(DN_CHUNK, dtype=A.dtype)
    Tm = lax.linalg.triangular_solve(A + eye, jnp.broadcast_to(eye, A.shape), left_side=True, lower=True, unit_diagonal=True)
    u = Tm @ (vc * bc[..., None])
    w = Tm @ (kb * jnp.exp(gc)[..., None])
    qk = jnp.where(incl, jnp.einsum("...id,...jd->...ij", qc, kc) * dmat, 0.0)

    def step(S, inp):
        q_i, k_i, u_i, w_i, g_i, qk_i = inp
        v_new = u_i - jnp.einsum("bhck,bhkv->bhcv", w_i, S)
        o = jnp.einsum("bhck,bhkv->bhcv", q_i * jnp.exp(g_i)[..., None], S) + jnp.einsum("bhij,bhjv->bhiv", qk_i, v_new)
        g_last = g_i[..., -1]
        S = S * jnp.exp(g_last)[..., None, None] + jnp.einsum(
            "bhck,bhcv->bhkv", k_i * jnp.exp(g_last[..., None] - g_i)[..., None], v_new)
        return S, o

    S, o = lax.scan(step, S0, (qc, kc, u, w, gc, qk))
    return jnp.moveaxis(o, (0, 2), (1, 3)).reshape(B, T, H, dv), S


def deltanet_mixer(p, lp, S0):
    B, T, _ = p.shape
    f32 = jnp.float32
    W, H = GROUP_W, DN_HEADS
    qkv = lax.conv_general_dilated(p[..., :3 * W], lp["dn_conv"][:, None, :], window_strides=(1,),
                                   padding=[(DN_CONV // 2, DN_CONV // 2)],
                                   dimension_numbers=("NWC", "WIO", "NWC"), feature_group_count=3 * W)
    qkv = jax.nn.silu(qkv)
    heads = lambda a: a.astype(f32).reshape(B, T, H, HEAD_DIM)
    q = l2_norm(heads(qkv[..., :W])) * (HEAD_DIM ** -0.5)
    k = l2_norm(heads(qkv[..., W:2 * W]))
    v = heads(qkv[..., 2 * W:])
    beta = jax.nn.sigmoid(p[..., 3 * W:3 * W + 2 * H].astype(f32)).reshape(B, T, 2, H)
    alpha = p[..., 3 * W + 2 * H:3 * W + 4 * H].astype(f32).reshape(B, T, 2, H)
    g = -jnp.exp(lp["dn_a_log"].astype(f32)) * jax.nn.softplus(alpha + lp["dn_dt_bias"].astype(f32))
    z = heads(p[..., 3 * W + 4 * H:])
    S0 = S0.astype(f32)
    o_f, s_f = gated_delta_chunked(q, k, v, g[:, :, 0], beta[:, :, 0], S0[:, 0])
    rev = lambda a: jnp.flip(a, axis=1)
    o_b, s_b = gated_delta_chunked(rev(q), rev(k), rev(v), rev(g[:, :, 1]), rev(beta[:, :, 1]), S0[:, 1])
    o = rms_norm(o_f + rev(o_b), lp["dn_norm_g"]) * jax.nn.silu(z)
    return o.reshape(B, T, W).astype(p.dtype), jnp.stack([s_f, s_b], axis=1)


def peer_ffn(h, lp):
    B, T, D = h.shape
    nb = (B * T) // PEER_BLOCK
    wq, keys, u, v = lp["peer_wq"], lp["peer_keys"], lp["peer_u"], lp["peer_v"]

    def block(xb):
        q = (xb @ wq).reshape(PEER_BLOCK, PEER_HEADS, 2, PEER_DQ // 2)
        s = jnp.einsum("nhpd,hpkd->nhpk", q, keys).astype(jnp.float32)
        s1, i1 = lax.top_k(s[:, :, 0], PEER_TOPK)
        s2, i2 = lax.top_k(s[:, :, 1], PEER_TOPK)
        cand = (s1[..., :, None] + s2[..., None, :]).reshape(PEER_BLOCK, PEER_HEADS, PEER_TOPK * PEER_TOPK)
        cidx = (i1[..., :, None] * PEER_NKEYS + i2[..., None, :]).reshape(PEER_BLOCK, PEER_HEADS, PEER_TOPK * PEER_TOPK)
        top_s, pos = lax.top_k(cand, PEER_TOPK)
        idx = jnp.take_along_axis(cidx, pos, axis=-1).reshape(PEER_BLOCK, PEER_HEADS * PEER_TOPK)
        gate = jax.nn.softmax(top_s, axis=-1).reshape(PEER_BLOCK, PEER_HEADS * PEER_TOPK).astype(xb.dtype)
        act = jax.nn.gelu(jnp.einsum("nd,ned->ne", xb, u[idx]), approximate=False)
        return jnp.einsum("ne,ned->nd", gate * act, v[idx])

    return lax.map(block, h.reshape(nb, PEER_BLOCK, D)).reshape(B, T, D)


def trunk_layer(x, cond, lp, ctx):
    B, T, _ = x.shape
    mod = jax.nn.silu(cond) @ lp["w_mod"] + lp["b_mod"]
    sh1, sc1, g1, sh2, sc2, g2 = [m[:, None, :] for m in jnp.split(mod, 6, axis=-1)]
    h = rms_norm(x, lp["norm1_g"]) * (1 + sc1) + sh1
    proj = h @ lp["w_in"]
    pa, pb, pc, pd = jnp.split(proj, [GQA_PROJ, GQA_PROJ + RW_PROJ, GQA_PROJ + RW_PROJ + NA_PROJ], axis=-1)
    if ctx is None:
        ctx_a = None
        ctx_c = None
        s_rw0 = jnp.zeros((B, 2, RW_HEADS, HEAD_DIM, HEAD_DIM), jnp.float32)
        s_dn0 = jnp.zeros((B, 2, DN_HEADS, HEAD_DIM, HEAD_DIM), jnp.float32)
    else:
        ka_c, va_c, kc_c, vc_c, s_rw0, s_dn0 = ctx
        ctx_a = (ka_c, va_c)
        ctx_c = (kc_c, vc_c)
    oa, (ka, va) = gqa_mixer(pa, lp, ctx_a)
    ob, s_rw = rwkv_mixer(pb, lp, s_rw0)
    oc, (kc, vc) = na_mixer(pc, lp, ctx_c)
    od, s_dn = deltanet_mixer(pd, lp, s_dn0)
    mix = jnp.concatenate([oa, ob, oc, od], axis=-1) @ lp["w_out"]
    x = x + g1 * mix
    h2 = rms_norm(x, lp["norm2_g"]) * (1 + sc2) + sh2
    x = x + g2 * peer_ffn(h2, lp)
    return x, (ka, va, kc, vc, s_rw, s_dn)


def setup_inputs(seed: int = 0) -> dict:
    key = jax.random.key(seed)
    keys = iter(jax.random.split(key, 64))
    f32 = jnp.float32

    def nrm(shape, scale):
        return jax.random.normal(next(keys), shape, f32) * scale

    def gain(shape):
        return 1.0 + 0.02 * jax.random.normal(next(keys), shape, f32)

    def unif(shape, lo, hi):
        return jax.random.uniform(next(keys), shape, f32, lo, hi)

    W = GROUP_W
    dt = jnp.exp(unif((DEPTH, 2, DN_HEADS), math.log(1e-3), math.log(1e-1)))
    return {
        "x_prompt": nrm((BATCH, SEQ, D_MODEL), 1.0),
        "x_sample": nrm((DEC_BATCH, DEC_SEQ, D_MODEL), 1.0),
        "c": nrm((DEC_BATCH, D_MODEL), 1.0),
        "cache_gqa_k": nrm((DEC_BATCH, DEPTH, GQA_KV_HEADS, PAST_LEN, HEAD_DIM), 1.0),
        "cache_gqa_v": nrm((DEC_BATCH, DEPTH, GQA_KV_HEADS, PAST_LEN, HEAD_DIM), 1.0),
        "cache_na_k": nrm((DEC_BATCH, DEPTH, NA_HEADS, PAST_LEN, HEAD_DIM), 1.0),
        "cache_na_v": nrm((DEC_BATCH, DEPTH, NA_HEADS, PAST_LEN, HEAD_DIM), 1.0),
        "state_rwkv": nrm((DEC_BATCH, DEPTH, 2, RW_HEADS, HEAD_DIM, HEAD_DIM), 0.1),
        "state_delta": nrm((DEC_BATCH, DEPTH, 2, DN_HEADS, HEAD_DIM, HEAD_DIM), 0.1),
        "c_ctx": nrm((D_MODEL,), 1.0),
        "norm1_g": gain((DEPTH, D_MODEL)),
        "norm2_g": gain((DEPTH, D_MODEL)),
        "w_mod": nrm((DEPTH, D_MODEL, 6 * D_MODEL), 0.5 * D_MODEL ** -0.5),
        "b_mod": nrm((DEPTH, 6 * D_MODEL), 0.02),
        "w_in": nrm((DEPTH, D_MODEL, IN_COLS), D_MODEL ** -0.5),
        "w_out": nrm((DEPTH, D_MIX, D_MODEL), D_MIX ** -0.5),
        "gqa_q_norm": gain((DEPTH, HEAD_DIM)),
        "gqa_k_norm": gain((DEPTH, HEAD_DIM)),
        "rw_mu": unif((DEPTH, 2, RW_SHIFT_COLS), 0.0, 1.0),
        "rw_w0": unif((DEPTH, 2, W), -5.0, -0.5),
        "rw_w2": nrm((DEPTH, 2, RW_DECAY_RANK, W), 0.1),
        "rw_a0": nrm((DEPTH, 2, W), 0.1),
        "rw_a2": nrm((DEPTH, 2, RW_AAA_RANK, W), RW_AAA_RANK ** -0.5),
        "rw_g2": nrm((DEPTH, RW_GATE_RANK, W), RW_GATE_RANK ** -0.5),
        "rw_k_k": 0.85 * gain((DEPTH, W)),
        "rw_k_a": gain((DEPTH, W)),
        "rw_r_k": nrm((DEPTH, RW_HEADS, HEAD_DIM), 0.1),
        "rw_ln_g": gain((DEPTH, W)),
        "rw_ln_b": nrm((DEPTH, W), 0.02),
        "na_bias": nrm((DEPTH, NA_HEADS, 2 * NA_ROWS - 1, 2 * NA_COLS - 1), 0.1),
        "dn_conv": nrm((DEPTH, DN_CONV, 3 * W), DN_CONV ** -0.5),
        "dn_a_log": jnp.log(unif((DEPTH, 2, DN_HEADS), 1.0, 16.0)),
        "dn_dt_bias": dt + jnp.log(-jnp.expm1(-dt)),
        "dn_norm_g": gain((DEPTH, HEAD_DIM)),
        "peer_wq": nrm((DEPTH, D_MODEL, PEER_HEADS * PEER_DQ), D_MODEL ** -0.5),
        "peer_keys": nrm((DEPTH, PEER_HEADS, 2, PEER_NKEYS, PEER_DQ // 2), (PEER_DQ // 2) ** -0.5),
        "peer_u": nrm((DEPTH, PEER_EXPERTS, D_MODEL), D_MODEL ** -0.5),
        "peer_v": nrm((DEPTH, PEER_EXPERTS, D_MODEL), (PEER_HEADS * PEER_TOPK) ** -0.5),
        "final_norm_g": gain((D_MODEL,)),
    }


def reference(x_prompt, x_sample, c, cache_gqa_k, cache_gqa_v, cache_na_k, cache_na_v, state_rwkv, state_delta,
              c_ctx, norm1_g, norm2_g, w_mod, b_mod, w_in, w_out, gqa_q_norm, gqa_k_norm,
              rw_mu, rw_w0, rw_w2, rw_a0, rw_a2, rw_g2, rw_k_k, rw_k_a, rw_r_k, rw_ln_g, rw_ln_b,
              na_bias, dn_conv, dn_a_log, dn_dt_bias, dn_norm_g,
              peer_wq, peer_keys, peer_u, peer_v, final_norm_g):
    def layer_params(l):
        return {
            "norm1_g": norm1_g[l], "norm2_g": norm2_g[l], "w_mod": w_mod[l], "b_mod": b_mod[l],
            "w_in": w_in[l], "w_out": w_out[l], "gqa_q_norm": gqa_q_norm[l], "gqa_k_norm": gqa_k_norm[l],
            "rw_mu": rw_mu[l], "rw_w0": rw_w0[l], "rw_w2": rw_w2[l], "rw_a0": rw_a0[l], "rw_a2": rw_a2[l],
            "rw_g2": rw_g2[l], "rw_k_k": rw_k_k[l], "rw_k_a": rw_k_a[l], "rw_r_k": rw_r_k[l],
            "rw_ln_g": rw_ln_g[l], "rw_ln_b": rw_ln_b[l], "na_bias": na_bias[l],
            "dn_conv": dn_conv[l], "dn_a_log": dn_a_log[l], "dn_dt_bias": dn_dt_bias[l], "dn_norm_g": dn_norm_g[l],
            "peer_wq": peer_wq[l], "peer_keys": peer_keys[l], "peer_u": peer_u[l], "peer_v": peer_v[l],
        }

    xp = x_prompt
    cond_ctx = c_ctx[None, :]
    per_layer = []
    for l in range(DEPTH):
        xp, st = trunk_layer(xp, cond_ctx, layer_params(l), None)
        per_layer.append(st)
    y_prompt = rms_norm(xp, final_norm_g)

    xs = x_sample
    for l in range(DEPTH):
        ctx = (cache_gqa_k[:, l], cache_gqa_v[:, l], cache_na_k[:, l], cache_na_v[:, l], state_rwkv[:, l], state_delta[:, l])
        xs, _ = trunk_layer(xs, c, layer_params(l), ctx)
    y_sample = rms_norm(xs, final_norm_g)

    dt = x_prompt.dtype
    new_gqa_k = jnp.stack([st[0] for st in per_layer], axis=1).astype(dt)
    new_gqa_v = jnp.stack([st[1] for st in per_layer], axis=1).astype(dt)
    new_na_k = jnp.stack([st[2] for st in per_layer], axis=1).astype(dt)
    new_na_v = jnp.stack([st[3] for st in per_layer], axis=1).astype(dt)
    new_state_rwkv = jnp.stack([st[4] for st in per_layer], axis=1).astype(dt)
    new_state_delta = jnp.stack([st[5] for st in per_layer], axis=1).astype(dt)
    return (y_prompt, y_sample, new_gqa_k, new_gqa_v, new_na_k, new_na_v, new_state_rwkv, new_state_delta)
```

```python
import math
import numpy as np
from contextlib import ExitStack
import concourse.bass as bass
import concourse.mybir as mybir
from concourse.alu_op_type import AluOpType as ALU
from concourse.bass_utils import run_bass_kernel_spmd

F32 = mybir.dt.float32
BF16 = mybir.dt.bfloat16
I32 = mybir.dt.int32
U32 = mybir.dt.uint32
AF = mybir.ActivationFunctionType
AX = mybir.AxisListType

NCORES = 8
DEPTH = 2
D = 1024
NT = 8
NTOK = 1024
T = 256
EPS = 1e-6
RW_LN_EPS = 64e-5
CA, CC, CB, CD = 0, 512, 1280, 2304
NWC = 3456


class Res:
    __slots__ = ("lw", "rd")

    def __init__(self):
        self.lw = None
        self.rd = []


class V:
    __slots__ = ("ap", "res")

    def __init__(self, ap, res):
        self.ap = ap
        self.res = res

    def __getitem__(self, k):
        return V(self.ap[k], self.res)

    def r(self, pat, **kw):
        return V(self.ap.rearrange(pat, **kw), self.res)

    def bc(self, shape):
        return V(self.ap.broadcast_to(list(shape)), self.res)

    def us(self, ax):
        return V(self.ap.unsqueeze(ax), self.res)


class Eng:
    def __init__(self, name, getter, same_sync):
        self.name = name
        self.getter = getter
        self.ops = []
        self.count = 0
        self.sem = None
        self.seen = {}
        self.same_sync = same_sync


class FW:
    NDMA = 8

    def __init__(self):
        self.nc = bass.Bass("TRN2", target_bir_lowering=False)
        self.es = ExitStack()
        nc = self.nc
        self.pe = Eng("pe", lambda: nc.tensor, False)
        self.act = Eng("act", lambda: nc.scalar, True)
        self.dve = Eng("dve", lambda: nc.vector, True)
        self.pool = Eng("pool", lambda: nc.gpsimd, True)
        self.sp = Eng("sp", lambda: nc.sync, False)
        self.engs = [self.pe, self.act, self.dve, self.pool, self.sp]
        for e in self.engs:
            e.sem = self.es.enter_context(nc.semaphore("s_" + e.name))
        self.dsem = {}
        self.dcnt = {}
        for e in (self.sp, self.pool):
            self.dsem[e.name] = [self.es.enter_context(nc.semaphore(f"d_{e.name}{i}")) for i in range(self.NDMA)]
            self.dcnt[e.name] = 0
        self.ninst = 0
        self.pes = None
        self.sect = None
        self.uid = 0

    def _stack(self, glob):
        if glob:
            return self.sect if self.sect is not None else self.es
        return self.es if self.pes is None else self.pes

    def sb(self, shape, dt, glob=False, name=None):
        self.uid += 1
        t = self._stack(glob).enter_context(self.nc.sbuf_tensor(name or f"t{self.uid}", list(shape), dt))
        return V(t[:], Res())

    def psum(self, shape, dt, name=None):
        self.uid += 1
        t = self.es.enter_context(self.nc.psum_tensor(name or f"p{self.uid}", list(shape), dt))
        return V(t[:], Res())

    def dram(self, name, shape, dt, kind):
        t = self.nc.dram_tensor(name, list(shape), dt, kind=kind)
        return V(t.ap(), Res())

    def _wait(self, E, tok):
        kind, a, b = tok
        if kind == 'e':
            if a is E and not E.same_sync:
                return
            key = a.name
            sem = a.sem
        else:
            key = id(a)
            sem = a
        if E.seen.get(key, 0) >= b:
            return
        E.seen[key] = b
        E.ops.append(('w', sem, b))

    def _deps(self, E, reads, writes):
        for r in reads:
            if r.lw is not None:
                self._wait(E, r.lw)
        for w in writes:
            if w.lw is not None:
                self._wait(E, w.lw)
            for tok in w.rd:
                self._wait(E, tok)

    def _commit(self, tok, reads, writes):
        for r in reads:
            r.rd.append(tok)
            if len(r.rd) > 48:
                best = {}
                for t in r.rd:
                    k = t[1].name if t[0] == 'e' else id(t[1])
                    if k not in best or best[k][2] < t[2]:
                        best[k] = t
                r.rd = list(best.values())
        for w in writes:
            w.lw = tok
            w.rd = []

    def op(self, E, fn, reads, writes):
        reads = [v.res for v in reads]
        writes = [v.res for v in writes]
        self._deps(E, reads, writes)
        E.count += 1
        E.ops.append(('i', fn, E.sem, 1))
        self._commit(('e', E, E.count), reads, writes)
        self.ninst += 1

    def dma(self, E, fn, reads, writes):
        reads = [v.res for v in reads]
        writes = [v.res for v in writes]
        self._deps(E, reads, writes)
        k = self.dcnt[E.name]
        self.dcnt[E.name] = k + 1
        sem = self.dsem[E.name][k % self.NDMA]
        rnd = k // self.NDMA
        if rnd > 0:
            self._wait(E, ('d', sem, 16 * rnd))
        E.ops.append(('i', fn, sem, 16))
        self._commit(('d', sem, 16 * (rnd + 1)), reads, writes)
        self.ninst += 1

    def begin(self):
        self.pes = ExitStack()

    def _barrier(self):
        for E in self.engs:
            for X in self.engs:
                if X is not E and X.count > 0:
                    self._wait(E, ('e', X, X.count))
            for nm, sems in self.dsem.items():
                k = self.dcnt[nm]
                for i, s in enumerate(sems):
                    n = (k - i + self.NDMA - 1) // self.NDMA if k > i else 0
                    if n > 0:
                        self._wait(E, ('d', s, 16 * n))

    def end(self, barrier=True):
        if barrier:
            self._barrier()
        nc = self.nc

        def replay(E, h):
            for o in E.ops:
                if o[0] == 'w':
                    h.wait_ge(o[1], o[2])
                else:
                    o[1](h).then_inc(o[2], o[3])
            E.ops = []

        with nc.Block() as block:
            @block.tensor
            def _(h):
                replay(self.pe, h)

            @block.scalar
            def _(h):
                replay(self.act, h)

            @block.vector
            def _(h):
                replay(self.dve, h)

            @block.gpsimd
            def _(h):
                replay(self.pool, h)

            @block.sync
            def _(h):
                replay(self.sp, h)
        if self.pes is not None:
            self.pes.close()
            self.pes = None

    def close(self):
        self.es.close()

    def mm(self, out, lhsT, rhs, start=True, stop=True):
        self.op(self.pe, lambda h: h.matmul(out.ap, lhsT.ap, rhs.ap, start=start, stop=stop), [lhsT, rhs], [out])

    def tr(self, out, in_, ident):
        self.op(self.pe, lambda h: h.transpose(out.ap, in_.ap, ident.ap), [in_, ident], [out])

    def tt(self, E, out, a, b, op):
        self.op(E, lambda h: h.tensor_tensor(out=out.ap, in0=a.ap, in1=b.ap, op=op), [a, b], [out])

    def ts(self, E, out, a, s1, op0, s2=None, op1=None):
        rd = [a] + [s for s in (s1, s2) if isinstance(s, V)]
        g = lambda s: s.ap if isinstance(s, V) else s
        if op1 is None:
            self.op(E, lambda h: h.tensor_scalar(out=out.ap, in0=a.ap, scalar1=g(s1), scalar2=None, op0=op0), rd, [out])
        else:
            self.op(E, lambda h: h.tensor_scalar(out=out.ap, in0=a.ap, scalar1=g(s1), scalar2=g(s2), op0=op0, op1=op1), rd, [out])

    def stt(self, out, a, s, b, op0, op1):
        rd = [a, b] + ([s] if isinstance(s, V) else [])
        sv = s.ap if isinstance(s, V) else s
        self.op(self.dve, lambda h: h.scalar_tensor_tensor(out=out.ap, in0=a.ap, scalar=sv, in1=b.ap, op0=op0, op1=op1), rd, [out])

    def actf(self, out, in_, func, bias=None, scale=None, accum=None):
        rd = [in_] + ([bias] if isinstance(bias, V) else [])
        wr = [out] + ([accum] if accum is not None else [])
        kw = {}
        if bias is not None:
            kw["bias"] = bias.ap if isinstance(bias, V) else bias
        if scale is not None:
            kw["scale"] = scale
        if accum is not None:
            kw["accum_out"] = accum.ap
        self.op(self.act, lambda h: h.activation(out=out.ap, in_=in_.ap, func=func, **kw), rd, wr)

    def cp(self, E, out, in_):
        if E is self.act:
            self.op(E, lambda h: h.copy(out=out.ap, in_=in_.ap), [in_], [out])
        else:
            self.op(E, lambda h: h.tensor_copy(out=out.ap, in_=in_.ap), [in_], [out])

    def red(self, out, in_, op, axis=AX.X):
        self.op(self.dve, lambda h: h.tensor_reduce(out=out.ap, in_=in_.ap, axis=axis, op=op), [in_], [out])

    def recip(self, out, in_):
        self.op(self.dve, lambda h: h.reciprocal(out=out.ap, in_=in_.ap), [in_], [out])

    def memset(self, E, out, val):
        self.op(E, lambda h: h.memset(out.ap, val), [], [out])

    def ld(self, out, in_, E=None):
        E = E or self.sp
        self.dma(E, lambda h: h.dma_start(out=out.ap, in_=in_.ap), [in_], [out])


def build(stage=99, dbg=False, nl=DEPTH, do_prompt=True, sstage=99):
    fw = FW()
    sp, pe, act, dve, pool = fw.sp, fw.pe, fw.act, fw.dve, fw.pool
    L = DEPTH
    din = {}

    def inp(name, shape, dt=F32):
        din[name] = fw.dram(name, shape, dt, "ExternalInput")
        return din[name]

    xp = inp("xp", [NTOK, D])
    cctx = inp("cctx", [128, 8])
    wmod = inp("wmod", [L, 128, 8, 6 * D])
    bmod = inp("bmod", [L, 6 * D])
    n1g = inp("n1g", [L, D])
    n2g = inp("n2g", [L, D])
    fng = inp("fng", [1, D])
    win = inp("win", [L, 128, 8, NWC])
    wout = inp("wout", [L, 64, 16, D])
    gqn = inp("gqn", [L, 2, 64])
    consts = inp("consts", [128, 1024])
    rwp_d = inp("rwp", [L, 128, 64])
    rwu_d = inp("rwu", [L, 64, 32])
    rww_d = inp("rww", [L, 128, 2, 2, 256])
    rwg_d = inp("rwg", [L, 128, 256])
    dnp_d = inp("dnp", [L, 128, 48])
    expm_d = inp("expm", [128, 2, 4, 128])
    xs_in = inp("xs_in", [4096, D])
    cs_in = inp("cs_in", [128, 8])
    cgk = inp("cgk", [L, 2, 512, 64])
    cgv = inp("cgv", [L, 2, 512, 64])
    cnk = inp("cnk", [L, 4, 512, 64])
    cnv = inp("cnv", [L, 4, 512, 64])
    srw = inp("srw", [L, 2, 4, 64, 64])
    sdn = inp("sdn", [L, 2, 4, 64, 64])
    rcos = inp("rcos", [4096, 32])
    rsin = inp("rsin", [4096, 32])
    nab = inp("nab", [L, 8, 4, 64, 512])
    woutn = inp("woutn", [L, 128, 8, D])
    XS = fw.dram("xs_scr", [4096, D], F32, "Internal")
    MIXD = fw.dram("mixd_scr", [4096, D], BF16, "Internal")
    ODD = [fw.dram(f"odd{i}", [64, 4, 4096], F32, "Internal") for i in range(2)]
    BOND = [fw.dram(f"bond{i}", [64, 4, 4096], F32, "Internal") for i in range(2)]
    GZD = fw.dram("gzd", [64, 4, 4096], F32, "Internal")
    PEER_ON = stage >= 6
    if PEER_ON:
        wq_d = inp("wq", [L, 128, 8, 2048])
        keyt_d = inp("keyt", [L, 128, 16, 128])
        pu_d = [inp(f"pu{i}", [16384, D]) for i in range(L)]
        pv_d = [inp(f"pv{i}", [16384, D]) for i in range(L)]

    y_prompt = fw.dram("y_prompt", [NTOK, D], F32, "ExternalOutput")
    y_sample = fw.dram("y_sample", [4096, D], F32, "ExternalOutput")
    o_gk = fw.dram("o_gk", [4, L, 2, T, 64], F32, "ExternalOutput")
    o_gv = fw.dram("o_gv", [4, L, 2, T, 64], F32, "ExternalOutput")
    o_nk = fw.dram("o_nk", [4, L, 4, T, 64], F32, "ExternalOutput")
    o_nv = fw.dram("o_nv", [4, L, 4, T, 64], F32, "ExternalOutput")
    o_rw = fw.dram("o_rw", [4, L, 2, 4, 64, 64], F32, "ExternalOutput")
    o_dn = fw.dram("o_dn", [4, L, 2, 4, 64, 64], F32, "ExternalOutput")
    dbgo = {}
    if dbg:
        dbgo["x"] = fw.dram("dbg_x", [NTOK, D], F32, "ExternalOutput")
        dbgo["h"] = fw.dram("dbg_h", [NTOK, D], F32, "ExternalOutput")
        dbgo["xs"] = fw.dram("dbg_xs", [4096, D], F32, "ExternalOutput")

    CON = fw.sb([128, 1024], F32, glob=True)
    ID16 = fw.sb([128, 128], BF16, glob=True)
    EPSC = fw.sb([128, 4], F32, glob=True)
    SS = fw.sb([128, 16], F32, glob=True)
    RSTD = fw.sb([128, 16], F32, glob=True)
    IDF = CON[:, 0:128]
    BONES = CON[:, 128:256]
    PS = [fw.psum([128, 512], F32) for _ in range(4)]
    PB = [fw.psum([128, 1024], BF16) for _ in range(2)]
    PSW = fw.psum([128, 1024], F32)
    ZO = CON[:, 320:576]
    IOTA16 = CON[:, 576:592]
    CSs = fw.sb([128, 8], F32, glob=True)
    fw.sect = ExitStack()
    X = fw.sb([128, NT, D], F32, glob=True)
    G1 = fw.sb([128, D], F32, glob=True)
    HT = fw.sb([128, 8, NTOK], BF16, glob=True)
    CS = fw.sb([128, 8], F32, glob=True)

    def mod_chunks(l, n0, dsts, cs=None):
        cs = CS if cs is None else cs
        WM = [fw.sb([128, 8, 512], F32) for _ in range(2)]
        BM = [fw.sb([128, 512], F32) for _ in range(2)]
        for i, dst in enumerate(dsts):
            n = n0 + i
            fw.ld(WM[n % 2], wmod[l, :, :, n * 512:(n + 1) * 512])
            fw.ld(BM[n % 2], V(bmod.ap[l:l + 1, n * 512:(n + 1) * 512].partition_broadcast(128), bmod.res))
            ps = PS[n % 2]
            for c in range(8):
                fw.mm(ps, cs[:, c:c + 1].bc([128, 128]), WM[n % 2][:, c, :], start=(c == 0), stop=(c == 7))
            fw.tt(dve, dst, ps, BM[n % 2], ALU.add)
    outs = [y_prompt, y_sample, o_gk, o_gv, o_nk, o_nv, o_rw, o_dn] + list(dbgo.values())

    fw.begin()
    fw.ld(CON, consts)
    fw.cp(dve, ID16, IDF)
    fw.memset(dve, EPSC[:, 0:1], EPS)
    fw.memset(dve, EPSC[:, 1:2], 1.0)
    fw.memset(dve, EPSC[:, 2:3], RW_LN_EPS)
    fw.ld(X, xp.r("(j p) d -> p j d", p=128))
    CT = fw.sb([128, 8], F32)
    fw.ld(CT, cctx)
    fw.actf(CS, CT, AF.Silu)
    fw.end()

    def rms_tile(j, A, Sh, HB, JK, xt=None):
        xt = X[:, j, :] if xt is None else xt
        k = j % 16
        fw.actf(JK, xt, AF.Square, accum=SS[:, k:k + 1])
        fw.actf(RSTD[:, k:k + 1], SS[:, k:k + 1], AF.Sqrt, bias=EPSC[:, 0:1], scale=1.0 / D)
        fw.recip(RSTD[:, k:k + 1], RSTD[:, k:k + 1])
        fw.stt(JK, xt, RSTD[:, k:k + 1], A, ALU.mult, ALU.mult)
        fw.tt(dve, HB, JK, Sh, ALU.add)

    def to_ht(j, HB, dst=None):
        pb = PB[j % 2]
        for c in range(8):
            fw.tr(pb[:, c * 128:(c + 1) * 128], HB[:, c * 128:(c + 1) * 128], ID16)
        dst = HT[:, :, j * 128:(j + 1) * 128] if dst is None else dst
        fw.cp(act, dst, pb.r("p (c t) -> p c t", c=8))

    def load_w(l, col0, ncols):
        W16 = fw.sb([128, 8, ncols], BF16)
        WS = [fw.sb([128, ncols], F32) for _ in range(2)]
        for c in range(8):
            fw.ld(WS[c % 2], win[l, :, c, col0:col0 + ncols])
            fw.cp(pool, W16[:, c, :], WS[c % 2])
        return W16

    def load_wout(l, ch0):
        WO = fw.sb([64, 4, D], BF16)
        WS = [fw.sb([64, D], F32) for _ in range(2)]
        for c in range(4):
            fw.ld(WS[c % 2], wout[l, :, ch0 + c, :])
            fw.cp(pool, WO[:, c, :], WS[c % 2])
        return WO

    def wout_acc(MIXT, WO, tmp=None):
        TMP = [tmp if tmp is not None else fw.sb([128, 512], F32)] * 2
        for j in range(NT):
            for n in range(2):
                ps = PS[(2 * j + n) % 4]
                for c in range(4):
                    fw.mm(ps, MIXT[:, c, j * 128:(j + 1) * 128], WO[:, c, n * 512:(n + 1) * 512], start=(c == 0), stop=(c == 3))
                tmp = TMP[n]
                fw.tt(dve, tmp, ps, G1[:, n * 512:(n + 1) * 512], ALU.mult)
                fw.tt(pool, X[:, j, n * 512:(n + 1) * 512], X[:, j, n * 512:(n + 1) * 512], tmp, ALU.add)

    def attention(QT, KT, VV, MIXT):
        PF = [fw.sb([128, 256], F32) for _ in range(2)]
        PN = [fw.sb([128, 256], BF16) for _ in range(2)]
        PT = [fw.sb([128, 2, 128], BF16) for _ in range(2)]
        ST = [fw.sb([128, 4], F32) for _ in range(2)]
        u = 0
        for b in range(4):
            for h in range(4):
                for qi in range(2):
                    k = u % 2
                    u += 1
                    tq0 = b * 256 + qi * 128
                    sps = PS[k]
                    fw.mm(sps[:, 0:256], QT(h)[:, tq0:tq0 + 128], KT(h)[:, b * 256:(b + 1) * 256])
                    st = ST[k]
                    fw.red(st[:, 0:1], sps[:, 0:256], ALU.max)
                    fw.ts(dve, st[:, 1:2], st[:, 0:1], -0.125, ALU.mult)
                    fw.actf(PF[k], sps[:, 0:256], AF.Exp, bias=st[:, 1:2], scale=0.125, accum=st[:, 2:3])
                    fw.recip(st[:, 3:4], st[:, 2:3])
                    fw.ts(dve, PN[k], PF[k], st[:, 3:4], ALU.mult)
                    pb = PB[k]
                    for kt in range(2):
                        fw.tr(pb[:, kt * 128:(kt + 1) * 128], PN[k][:, kt * 128:(kt + 1) * 128], ID16)
                    fw.cp(act, PT[k], pb[:, 0:256].r("p (a t) -> p a t", a=2))
                    ops = PS[2 + k]
                    for kt in range(2):
                        fw.mm(ops[0:64, 0:128], VV(h, b, kt), PT[k][:, kt, :], start=(kt == 0), stop=(kt == 1))
                    fw.cp(act, MIXT[:, h, tq0:tq0 + 128], ops[0:64, 0:128])


    def scan(W_s, B_s, K_s, A16, R2, VTM, OS, M):
        streams = []
        for s_ in range(2):
            for d in range(2):
                fw.memset(dve, M[s_][d], 0.0)
                st = dict(d=d, M=M[s_][d], Mb=fw.sb([128, 2, 64], BF16),
                          mw=fw.sb([128, 2, 64], F32), t1=fw.sb([128, 2, 64], F32), t2=fw.sb([128, 2, 64], F32),
                          sap=V(PS[s_].ap[:, d * 128:(d + 1) * 128], Res()),
                          vp=[V(PS[2 + s_].ap[:, (d * 2 + b_) * 128:(d * 2 + b_ + 1) * 128], Res()) for b_ in range(2)],
                          op=V(PSW.ap[0:64, (s_ * 2 + d) * 128:(s_ * 2 + d + 1) * 128].rearrange("p (t j m) -> p t j m", t=32, j=2), Res()))
                fw.memset(dve, st["Mb"], 0.0)
                st["tau"] = (lambda i, s_=s_, d=d: s_ * T + (i if d == 0 else T - 1 - i))
                st["slot"] = (lambda i, d=d: (i % 32) if d == 0 else 31 - (i % 32))

                def evac(i0, s_=s_, d=d, st=st):
                    t0 = i0 if d == 0 else T - 32 - i0
                    dst = OS[:, :, s_ * T + t0:s_ * T + t0 + 32].r("p (j m) t -> p t j m", j=2)
                    if i0 < T // 2:
                        fw.cp(act, dst, st["op"])
                    else:
                        fw.tt(dve, dst, dst, st["op"], ALU.add)
                st["evac"] = evac
                streams.append(st)
        scan_core(streams, T, W_s, B_s, K_s, A16, R2, VTM)

    def scan_core(streams, nsteps, W_s, B_s, K_s, A32, R2, VTM):
        def vb(st, i):
            d, tau = st["d"], st["tau"](i)
            vp = st["vp"][i % 2]
            for m in range(2):
                fw.mm(vp[m * 64:(m + 1) * 64, :], ID16[:, tau % 128:tau % 128 + 1].bc([128, 64]), VTM[:, tau // 128, d, m, :])

        def omm(st, i):
            d, tau = st["d"], st["tau"](i)
            for j in range(2):
                fw.mm(st["op"][:, st["slot"](i), j, :], st["Mb"][:, j, :], R2[:, d * 2 + j, tau, :])
            if i % 32 == 31:
                st["evac"](i - 31)

        for i in range(nsteps):
            for st in streams:
                d, tau = st["d"], st["tau"](i)
                Mb = st["Mb"]
                gs = slice(d * 2, d * 2 + 2)
                for j in range(2):
                    g = d * 2 + j
                    for m in range(2):
                        fw.mm(st["sap"][m * 64:(m + 1) * 64, j * 64:(j + 1) * 64], A32[m * 64:(m + 1) * 64, g, tau:tau + 1].bc([64, 64]),
                              Mb[m * 64:(m + 1) * 64, j, :])
                vb(st, i)
                fw.tt(dve, st["t2"], st["vp"][i % 2].r("p (j v) -> p j v", j=2), K_s[:, gs, tau:tau + 1].bc([128, 2, 64]), ALU.mult)
                fw.tt(dve, st["mw"], st["M"], W_s[:, gs, tau:tau + 1].bc([128, 2, 64]), ALU.mult)
                fw.tt(dve, st["mw"], st["mw"], st["t2"], ALU.add)
                fw.tt(dve, st["t1"], st["sap"].r("p (j v) -> p j v", j=2), B_s[:, gs, tau:tau + 1].bc([128, 2, 64]), ALU.mult)
                fw.tt(dve, st["M"], st["mw"], st["t1"], ALU.add)
                fw.cp(act, st["Mb"], st["M"])
                omm(st, i)

    def proj_fm(ps, W16, col0, ncol, tok0):
        for c in range(8):
            fw.mm(ps[0:ncol, :], W16[:, c, col0:col0 + ncol], HT[:, c, tok0:tok0 + 512], start=(c == 0), stop=(c == 7))

    def peer_phase(l, ntiles, cs, get_x, put_x):
        fw.begin()
        WQ = fw.sb([128, 8, 2048], BF16)
        WS = [fw.sb([128, 1024], F32) for _ in range(2)]
        for c in range(16):
            fw.ld(WS[c % 2], wq_d[l, :, c // 2, (c % 2) * 1024:(c % 2 + 1) * 1024])
            fw.cp(pool, WQ[:, c // 2, (c % 2) * 1024:(c % 2 + 1) * 1024], WS[c % 2])
        KEYS = fw.sb([128, 16, 128], BF16)
        for c in range(2):
            fw.ld(WS[c], keyt_d[l, :, c * 8:(c + 1) * 8, :].r("p a k -> p (a k)"))
            fw.cp(pool, KEYS[:, c * 8:(c + 1) * 8, :], WS[c].r("p (a k) -> p a k", a=8))
        JK = fw.sb([128, D], F32)
        MODB = fw.sb([128, 3 * D], F32)
        S3, S4, G2 = MODB[:, 0:D], MODB[:, D:2 * D], MODB[:, 2 * D:3 * D]
        mod_chunks(l, 6, [MODB[:, n * 512:(n + 1) * 512] for n in range(6)], cs)
        fw.ld(JK, V(n2g.ap[l:l + 1, :].partition_broadcast(128), n2g.res))
        fw.stt(S4, S4, 1.0, JK, ALU.add, ALU.mult)
        H2Bs = [fw.sb([128, D], BF16) for _ in range(2)]
        QPT = fw.sb([128, 16, 128], BF16)
        SC = fw.sb([128, 16, 128], F32)
        S1t = fw.sb([128, 16, 16], F32)
        I1t = fw.sb([128, 16, 16], U32)
        I1f = fw.sb([128, 16, 16], F32)
        WK = fw.sb([128, 256], F32)
        CAND = fw.sb([128, 8, 16, 16], F32)
        EQ = UG[0].r("p (h c i) -> p h c i", h=8, c=16)[:, :, :, 0:16] if False else fw.sb([128, 8, 16, 16], F32)
        TOP = fw.sb([128, 8, 16], F32)
        POS = fw.sb([128, 8, 16], U32)
        PF_ = fw.sb([128, 8, 16], F32)
        PJ = fw.sb([128, 8, 16], F32)
        PI = fw.sb([128, 8, 16], F32)
        SEL = fw.sb([128, 2, 8, 16], F32)
        GT = fw.sb([128, 8, 16], F32)
        G8 = fw.sb([128, 8], F32)
        IDXF = fw.sb([128, 128], F32)
        IDXT = fw.sb([128, 128], I32)
        GTT = fw.sb([128, 128], F32)
        ACT1 = fw.sb([128, 128], F32)
        GA = fw.sb([128, 128], F32)
        GAM = [fw.sb([128, 128], F32) for _ in range(2)]
        UG = [fw.sb([128, D], F32) for _ in range(2)]
        VG = UG
        TMPX = JK[:, 0:512]
        pu_l = pu_d[l]
        pv_l = pv_d[l]
        HTl = fw.sb([128, 8, 128], BF16)
        for j in range(ntiles):
            H2B = H2Bs[j % 2]
            xt = get_x(j)
            rms_tile(j, S4, S3, H2B, JK, xt)
            to_ht(j, H2B, HTl)
            for cc in range(16):
                ps = PS[cc % 2]
                for c in range(8):
                    fw.mm(ps[:, 0:128], WQ[:, c, cc * 128:(cc + 1) * 128], HTl[:, c, :], start=(c == 0), stop=(c == 7))
                fw.cp(act, QPT[:, cc, :], ps[:, 0:128])
            for q in range(4):
                ps = PS[2 + q % 2]
                for i in range(4):
                    cc = q * 4 + i
                    fw.mm(ps[:, i * 128:(i + 1) * 128], QPT[:, cc, :], KEYS[:, cc, :])
                fw.cp(act, SC[:, q * 4:(q + 1) * 4, :], ps.r("p (a k) -> p a k", a=4))

            def top16(vals, idx, src, wk):
                fw.op(dve, lambda h: h.max(out=vals[:, 0:8].ap, in_=src.ap), [src], [vals])
                fw.op(dve, lambda h: h.max_index(out=idx[:, 0:8].ap, in_max=vals[:, 0:8].ap, in_values=src.ap), [src, vals], [idx])
                fw.op(dve, lambda h: h.match_replace(out=wk.ap, in_to_replace=vals[:, 0:8].ap, in_values=src.ap, imm_value=-1e30), [src, vals], [wk])
                fw.op(dve, lambda h: h.max(out=vals[:, 8:16].ap, in_=wk.ap), [wk], [vals])
                fw.op(dve, lambda h: h.max_index(out=idx[:, 8:16].ap, in_max=vals[:, 8:16].ap, in_values=wk.ap), [wk, vals], [idx])

            for cc in range(16):
                top16(S1t[:, cc, :], I1t[:, cc, :], SC[:, cc, :], WK[:, 0:128])
            S1v = S1t.r("p (h two) k -> p h two k", two=2)
            fw.tt(dve, CAND, S1v[:, :, 0, :].us(3).bc([128, 8, 16, 16]), S1v[:, :, 1, :].us(2).bc([128, 8, 16, 16]), ALU.add)
            for hh in range(8):
                top16(TOP[:, hh, :], POS[:, hh, :], CAND[:, hh].r("p i j -> p (i j)"), WK)
            fw.tt(dve, GT, TOP, TOP[:, :, 0:1].bc([128, 8, 16]), ALU.subtract)
            fw.actf(GT, GT, AF.Exp)
            fw.red(G8, GT, ALU.add)
            fw.recip(G8, G8)
            fw.tt(dve, GT, GT, G8.us(2).bc([128, 8, 16]), ALU.mult)
            fw.cp(dve, PF_, POS)
            fw.cp(dve, I1f, I1t)
            fw.ts(dve, PI, PF_, 16.0, ALU.is_ge)
            for i_ in range(2, 16):
                fw.stt(PI, PF_, 16.0 * i_, PI, ALU.is_ge, ALU.add)
            fw.stt(PJ, PI, -16.0, PF_, ALU.mult, ALU.add)
            I1v = I1f.r("p (h two) k -> p h two k", two=2)
            for w_, PP in enumerate((PI, PJ)):
                fw.tt(dve, EQ, PP.us(3).bc([128, 8, 16, 16]), IOTA16.us(1).us(1).bc([128, 8, 16, 16]), ALU.is_equal)
                fw.tt(dve, EQ, EQ, I1v[:, :, w_, :].us(2).bc([128, 8, 16, 16]), ALU.mult)
                fw.red(SEL[:, w_], EQ, ALU.add)
            fw.stt(IDXF.r("p (h k) -> p h k", h=8), SEL[:, 0], 128.0, SEL[:, 1], ALU.mult, ALU.add)
            fw.tr(PS[0][:, 0:128], IDXF, IDF)
            fw.cp(dve, IDXT, PS[0][:, 0:128])
            fw.tr(PS[1][:, 0:128], GT.r("p h k -> p (h k)"), IDF)
            fw.cp(dve, GTT, PS[1][:, 0:128])
            for n in range(128):
                ug = UG[n % 2]
                fw.dma(pool, lambda h, ug=ug, n=n: h.indirect_dma_start(out=ug.ap, out_offset=None, in_=pu_l.ap,
                                                                       in_offset=bass.IndirectOffsetOnAxis(ap=IDXT.ap[:, n:n + 1], axis=0)),
                       [IDXT, pu_l], [ug])
                for n2 in range(2):
                    fw.mm(PSW[:, n2 * 512:(n2 + 1) * 512], ID16[:, n:n + 1].bc([128, 128]), H2B[:, n2 * 512:(n2 + 1) * 512])
                fw.op(dve, lambda h, ug=ug, n=n: h.scalar_tensor_tensor(out=JK.ap, in0=ug.ap, scalar=1.0, in1=PSW.ap,
                                                                        op0=ALU.mult, op1=ALU.mult, accum_out=ACT1.ap[:, n:n + 1]),
                      [ug, PSW], [JK, ACT1])
            fw.actf(GA, ACT1, AF.Gelu)
            fw.tt(dve, GA, GA, GTT, ALU.mult)
            for n in range(128):
                vg = VG[n % 2]
                fw.dma(pool, lambda h, vg=vg, n=n: h.indirect_dma_start(out=vg.ap, out_offset=None, in_=pv_l.ap,
                                                                       in_offset=bass.IndirectOffsetOnAxis(ap=IDXT.ap[:, n:n + 1], axis=0)),
                       [IDXT, pv_l], [vg])
                gm = GAM[n % 2]
                fw.ts(dve, gm, ZO[:, 128 - n:256 - n], GA[:, n:n + 1], ALU.mult)
                for n2 in range(2):
                    fw.mm(PSW[:, n2 * 512:(n2 + 1) * 512], gm, vg[:, n2 * 512:(n2 + 1) * 512], start=(n == 0), stop=(n == 127))
            for n2 in range(2):
                fw.tt(dve, TMPX, PSW[:, n2 * 512:(n2 + 1) * 512], G2[:, n2 * 512:(n2 + 1) * 512], ALU.mult)
                fw.tt(dve, xt[:, n2 * 512:(n2 + 1) * 512], xt[:, n2 * 512:(n2 + 1) * 512], TMPX, ALU.add)
            put_x(j, xt)
        fw.end()


    for l in range(nl if do_prompt else 0):
        fw.begin()
        MODA = fw.sb([128, 2 * D], F32)
        S1, S2 = MODA[:, 0:D], MODA[:, D:2 * D]
        mod_chunks(l, 0, [MODA[:, n * 512:(n + 1) * 512] for n in range(4)] + [G1[:, n * 512:(n + 1) * 512] for n in range(2)])
        GN = fw.sb([128, D], F32)
        fw.ld(GN, V(n1g.ap[l:l + 1, :].partition_broadcast(128), n1g.res))
        fw.stt(S2, S2, 1.0, GN, ALU.add, ALU.mult)
        JK = fw.sb([128, D], F32)
        HBs = [fw.sb([128, D], BF16) for _ in range(2)]
        for j in range(NT):
            rms_tile(j, S2, S1, HBs[j % 2], JK)
            to_ht(j, HBs[j % 2])
        fw.end()
        if stage <= 1:
            break

        fw.begin()
        W16 = load_w(l, CA, 512)
        WO = load_wout(l, 0)
        GQ = fw.sb([128, 2, 64], F32)
        fw.ld(GQ, V(gqn.ap[l:l + 1].partition_broadcast(128), gqn.res))
        QKT = fw.sb([128, 3, NTOK], BF16)
        V16 = fw.sb([128, NT, 128], BF16)
        MIXT = fw.sb([64, 4, NTOK], BF16)
        ATM = [fw.sb([128, 512], F32) for _ in range(2)]
        SQ = fw.sb([128, 384], F32)
        QK16 = [fw.sb([128, 384], BF16) for _ in range(2)]
        s6 = fw.sb([128, 8], F32)
        for j in range(NT):
            ps = PS[j % 2]
            for c in range(8):
                fw.mm(ps, HT[:, c, j * 128:(j + 1) * 128], W16[:, c, :], start=(c == 0), stop=(c == 7))
            a = ATM[j % 2]
            fw.cp(act, a, ps)
            fw.actf(SQ, a[:, 0:384], AF.Square)
            fw.red(s6[:, 0:6], SQ.r("p (h d) -> p h d", d=64), ALU.add)
            fw.actf(s6[:, 0:6], s6[:, 0:6], AF.Sqrt, bias=EPSC[:, 0:1], scale=1.0 / 64)
            fw.recip(s6[:, 0:6], s6[:, 0:6])
            qk = a[:, 0:384].r("p (h d) -> p h d", d=64)
            fw.tt(dve, qk, qk, s6[:, 0:6].us(2).bc([128, 6, 64]), ALU.mult)
            fw.tt(dve, qk[:, 0:4, :], qk[:, 0:4, :], GQ[:, 0:1, :].bc([128, 4, 64]), ALU.mult)
            fw.tt(dve, qk[:, 4:6, :], qk[:, 4:6, :], GQ[:, 1:2, :].bc([128, 2, 64]), ALU.mult)
            b, t0 = j // 2, (j % 2) * 128
            fw.ld(V(o_gk.ap[b, l, :, t0:t0 + 128, :].rearrange("k t d -> t k d"), o_gk.res), a[:, 256:384].r("p (k d) -> p k d", d=64))
            fw.ld(V(o_gv.ap[b, l, :, t0:t0 + 128, :].rearrange("k t d -> t k d"), o_gv.res), a[:, 384:512].r("p (k d) -> p k d", d=64))
            fw.cp(act, QK16[j % 2], a[:, 0:384])
            fw.cp(act, V16[:, j, :], a[:, 384:512])
            pb = PB[j % 2]
            for c in range(3):
                fw.tr(pb[:, c * 128:(c + 1) * 128], QK16[j % 2][:, c * 128:(c + 1) * 128], ID16)
            fw.cp(act, QKT[:, :, j * 128:(j + 1) * 128], pb[:, 0:384].r("p (c t) -> p c t", c=3))
        attention(lambda h: QKT[(h % 2) * 64:(h % 2) * 64 + 64, h // 2, :],
                  lambda h: QKT[(h % 2) * 64:(h % 2) * 64 + 64, 2, :],
                  lambda h, b, kt: V16[:, 2 * b + kt, (h % 2) * 64:(h % 2) * 64 + 64], MIXT)
        wout_acc(MIXT, WO)
        fw.end()
        if stage <= 2:
            break

        fw.begin()
        W16 = load_w(l, CC, 768)
        WO = load_wout(l, 8)
        QKT = fw.sb([128, 4, NTOK], BF16)
        V16 = fw.sb([128, NT, 256], BF16)
        MIXT = fw.sb([64, 4, NTOK], BF16)
        CTM = [fw.sb([128, 768], F32) for _ in range(2)]
        C16 = [fw.sb([128, 512], BF16) for _ in range(2)]
        for j in range(NT):
            a = CTM[j % 2]
            for n, (c0, nn) in enumerate(((0, 512), (512, 256))):
                ps = PS[(2 * j + n) % 4]
                for c in range(8):
                    fw.mm(ps[:, 0:nn], HT[:, c, j * 128:(j + 1) * 128], W16[:, c, c0:c0 + nn], start=(c == 0), stop=(c == 7))
                fw.cp(act, a[:, c0:c0 + nn], ps[:, 0:nn])
            b, t0 = j // 2, (j % 2) * 128
            fw.ld(V(o_nk.ap[b, l, :, t0:t0 + 128, :].rearrange("k t d -> t k d"), o_nk.res), a[:, 256:512].r("p (k d) -> p k d", d=64))
            fw.ld(V(o_nv.ap[b, l, :, t0:t0 + 128, :].rearrange("k t d -> t k d"), o_nv.res), a[:, 512:768].r("p (k d) -> p k d", d=64))
            fw.cp(act, C16[j % 2], a[:, 0:512])
            fw.cp(dve, V16[:, j, :], a[:, 512:768])
            pb = PB[j % 2]
            for c in range(4):
                fw.tr(pb[:, c * 128:(c + 1) * 128], C16[j % 2][:, c * 128:(c + 1) * 128], ID16)
            fw.cp(act, QKT[:, :, j * 128:(j + 1) * 128], pb[:, 0:512].r("p (c t) -> p c t", c=4))
        attention(lambda h: QKT[(h % 2) * 64:(h % 2) * 64 + 64, h // 2, :],
                  lambda h: QKT[(h % 2) * 64:(h % 2) * 64 + 64, 2 + h // 2, :],
                  lambda h, b, kt: V16[:, 2 * b + kt, h * 64:h * 64 + 64], MIXT)
        wout_acc(MIXT, WO)
        fw.end()
        if stage <= 3:
            continue

        if stage >= 4:
            fw.begin()
            W16 = load_w(l, CB, 1024)
            WO = load_wout(l, 4)
            RWP = fw.sb([128, 64], F32)
            fw.ld(RWP, rwp_d[l])
            RWU = fw.sb([64, 32], F32)
            fw.ld(RWU, rwu_d[l])
            WSt = fw.sb([128, 1024], F32)
            W2A = fw.sb([128, 2, 2, 256], BF16)
            fw.ld(WSt, rww_d[l].r("p a b c -> p (a b c)"))
            fw.cp(pool, W2A, WSt.r("p (a b c) -> p a b c", a=2, b=2))
            G2P = fw.sb([128, 256], BF16)
            fw.ld(WSt[:, 0:256], rwg_d[l])
            fw.cp(pool, G2P, WSt[:, 0:256])
            OMKA = fw.sb([128, 2], F32)
            fw.ts(dve, OMKA, RWP[:, 24:26], -1.0, ALU.mult, 1.0, ALU.add)
            MIXT = fw.sb([64, 4, NTOK], BF16)
            GS16 = fw.sb([128, 512], BF16)
            BON = fw.sb([64, 4, 512], F32)
            XR = fw.sb([128, 2, 512], F32)
            XK = fw.sb([128, 2, 512], F32)
            XV = fw.sb([128, 2, 512], F32)
            XL = fw.sb([128, 512], F32)
            BTc = [fw.sb([128, 512], F32) for _ in range(2)]
            TW16 = fw.sb([128, 512], BF16)
            XL16 = fw.sb([128, 512], BF16)
            AG = fw.sb([128, 512], F32)
            KK = fw.sb([128, 512], F32)
            SQ = fw.sb([128, 512], F32)
            RN = fw.sb([128, 512], F32)
            KE = fw.sb([128, 512], F32)
            DF = SQ
            VUt = RN[0:64, :]
            W_s = fw.sb([128, 4, 512], F32)
            B_s = fw.sb([128, 4, 512], BF16)
            K_s = fw.sb([128, 4, 512], BF16)
            A16 = fw.sb([128, 4, 512], BF16)
            R2 = fw.sb([128, 4, 512, 2], BF16)
            VTM = fw.sb([128, 4, 2, 2, 128], BF16)
            OS = fw.sb([64, 4, 512], F32)
            ST = fw.sb([64, 128], F32)
            M = [[fw.sb([128, 2, 64], F32) for d in range(2)] for s_ in range(2)]
            fw.memset(pool, R2, 0.0)
            for sp_ in range(2):
                tok0 = sp_ * 512
                ps = PS[0]
                proj_fm(ps, W16, 896, 128, tok0)
                fw.actf(GS16, ps, AF.Sigmoid)
                for d in range(2):
                    dsts = [XR[:, 0, :], XR[:, 1, :], XK[:, 0, :], XK[:, 1, :], XV[:, 0, :], XV[:, 1, :], XL]
                    for c in range(7):
                        ps = PS[c % 2]
                        bt = BTc[c % 2]
                        proj_fm(ps, W16, c * 128, 128, tok0)
                        fw.cp(act, bt, ps)
                        b3 = bt.r("p (s t) -> p s t", s=2)
                        d3 = DF.r("p (s t) -> p s t", s=2)
                        if d == 0:
                            fw.tt(dve, d3[:, :, 1:T], b3[:, :, 0:T - 1], b3[:, :, 1:T], ALU.subtract)
                            fw.ts(dve, d3[:, :, 0:1], b3[:, :, 0:1], -1.0, ALU.mult)
                        else:
                            fw.tt(dve, d3[:, :, 0:T - 1], b3[:, :, 1:T], b3[:, :, 0:T - 1], ALU.subtract)
                            fw.ts(dve, d3[:, :, T - 1:T], b3[:, :, T - 1:T], -1.0, ALU.mult)
                        fw.stt(dsts[c], DF, RWP[:, d * 7 + c:d * 7 + c + 1], bt, ALU.mult, ALU.add)
                    fw.actf(TW16, XL, AF.Tanh)
                    fw.cp(act, XL16, XL)
                    for j in range(2):
                        g = d * 2 + j
                        fw.mm(PS[2], W2A[:, d, 0, j * 128:(j + 1) * 128], TW16)
                        fw.actf(W_s[:, g, :], PS[2], AF.Sigmoid, bias=RWP[:, 14 + d * 2 + j:15 + d * 2 + j])
                        fw.actf(W_s[:, g, :], W_s[:, g, :], AF.Exp, scale=-math.exp(-0.5))
                        fw.mm(PS[3], W2A[:, d, 1, j * 128:(j + 1) * 128], XL16)
                        fw.actf(AG, PS[3], AF.Sigmoid, bias=RWP[:, 18 + d * 2 + j:19 + d * 2 + j])
                        fw.ts(dve, KK, XK[:, j, :], RWP[:, 22 + j:23 + j], ALU.mult)
                        fw.actf(SQ, KK, AF.Square)
                        fw.mm(PS[2], BONES, SQ)
                        fw.actf(RN, PS[2], AF.Sqrt, bias=EPSC[:, 0:1])
                        fw.recip(RN, RN)
                        fw.tt(dve, KK, KK, RN, ALU.mult)
                        fw.ts(dve, A16[:, g, :], KK, -1.0, ALU.mult)
                        fw.tt(dve, B_s[:, g, :], KK, AG, ALU.mult)
                        fw.ts(dve, SQ, AG, RWP[:, 24 + j:25 + j], ALU.mult, OMKA[:, j:j + 1], ALU.add)
                        fw.tt(dve, KE, XK[:, j, :], SQ, ALU.mult)
                        fw.cp(act, K_s[:, g, :], KE)
                        fw.cp(act, R2[0:64, g, :, 0], XR[0:64, j, :])
                        fw.cp(act, R2[64:128, g, :, 1], XR[64:128, j, :])
                        fw.stt(KE, XR[:, j, :], RWP[:, 28 + j:29 + j], KE, ALU.mult, ALU.mult)
                        for m in range(2):
                            hh = 2 * j + m
                            fw.mm(PS[2][0:64, :], BONES[:, m * 64:(m + 1) * 64], KE)
                            fw.mm(PS[3][0:64, :], IDF[:, m * 64:(m + 1) * 64], XV[:, j, :])
                            fw.cp(act, VUt, PS[3][0:64, :])
                            if d == 0:
                                fw.tt(dve, BON[:, hh, :], PS[2][0:64, :], VUt, ALU.mult)
                            else:
                                fw.tt(dve, VUt, PS[2][0:64, :], VUt, ALU.mult)
                                fw.tt(dve, BON[:, hh, :], BON[:, hh, :], VUt, ALU.add)
                    for tt_ in range(4):
                        ps = PS[tt_ % 2]
                        for j in range(2):
                            fw.tr(ps[:, j * 128:(j + 1) * 128], XV[:, j, tt_ * 128:(tt_ + 1) * 128], IDF)
                        fw.cp(act, VTM[:, tt_, d, :, :].r("p m (j v) -> p j m v", j=2), ps[:, 0:256].r("p (j m v) -> p j m v", j=2, m=2))
                scan(W_s, B_s, K_s, A16, R2, VTM, OS, M)
                for s_ in range(2):
                    b = sp_ * 2 + s_
                    for d in range(2):
                        for j in range(2):
                            fw.tr(PS[0][0:64, 0:128], M[s_][d][:, j, :], IDF)
                            fw.cp(act, ST, PS[0][0:64, 0:128])
                            fw.ld(V(o_rw.ap[b, l, d, 2 * j:2 * j + 2].rearrange("m v k -> v m k"), o_rw.res), ST.r("v (m k) -> v m k", m=2))
                for hh in range(4):
                    ONES = CON[0:64, 256:320]
                    fw.mm(PS[0][0:64, :], ONES, OS[:, hh, :])
                    fw.tt(dve, SQ[0:64, :], OS[:, hh, :], PS[0][0:64, :], ALU.subtract)
                    fw.actf(RN[0:64, :], SQ[0:64, :], AF.Square)
                    fw.mm(PS[1][0:64, :], ONES, RN[0:64, :])
                    fw.actf(RN[0:64, :], PS[1][0:64, :], AF.Sqrt, bias=EPSC[0:64, 2:3])
                    fw.recip(RN[0:64, :], RN[0:64, :])
                    fw.tt(dve, SQ[0:64, :], SQ[0:64, :], RN[0:64, :], ALU.mult)
                    fw.ts(dve, SQ[0:64, :], SQ[0:64, :], RWU[:, 8 + hh:9 + hh], ALU.mult, RWU[:, 12 + hh:13 + hh], ALU.add)
                    fw.tt(dve, SQ[0:64, :], SQ[0:64, :], BON[:, hh, :], ALU.add)
                    fw.mm(PS[2][0:64, :], G2P[:, hh * 64:(hh + 1) * 64], GS16)
                    fw.tt(dve, MIXT[:, hh, tok0:tok0 + 512], SQ[0:64, :], PS[2][0:64, :], ALU.mult)
            wout_acc(MIXT, WO, KE)
            fw.end()

        if stage >= 5:
            fw.begin()
            W16 = load_w(l, CD, 1152)
            WO = load_wout(l, 12)
            DNP = fw.sb([128, 48], F32)
            fw.ld(DNP, dnp_d[l])
            EXPM = fw.sb([128, 2, 4, 128], F32)
            fw.ld(EXPM, expm_d)
            CONVW = DNP[:, 0:30].r("p (c j) -> p c j", j=5)
            NAe = fw.sb([128, 1], F32)
            fw.actf(NAe, DNP[:, 31:32], AF.Exp)
            fw.ts(dve, NAe, NAe, -1.0, ALU.mult)
            MIXT = fw.sb([64, 4, NTOK], BF16)
            BT1 = [fw.sb([128, 512], F32) for _ in range(2)]
            CV = fw.sb([128, 6, 512], F32)
            SQ = fw.sb([128, 512], F32)
            RN = fw.sb([128, 512], F32)
            KB = fw.sb([128, 512], F32)
            SB_ = fw.sb([128, 512], F32)
            GG = fw.sb([128, 512], F32)
            W_s = fw.sb([128, 4, 512], F32)
            B_s = fw.sb([128, 4, 512], BF16)
            K_s = fw.sb([128, 4, 512], BF16)
            A16 = fw.sb([128, 4, 512], BF16)
            R2 = fw.sb([128, 4, 512, 2], BF16)
            VTM = fw.sb([128, 4, 2, 2, 128], BF16)
            ZU = fw.sb([64, 4, 512], F32)
            OS = fw.sb([64, 4, 512], F32)
            M = [[fw.sb([128, 2, 64], F32) for d in range(2)] for s_ in range(2)]
            fw.memset(pool, R2, 0.0)
            for sp_ in range(2):
                tok0 = sp_ * 512
                for ch in range(6):
                    ps = PS[ch % 2]
                    bt = BT1[ch % 2]
                    proj_fm(ps, W16, ch * 128, 128, tok0)
                    fw.cp(act, bt, ps)
                    x3 = bt.r("p (s t) -> p s t", s=2)
                    a3 = CV[:, ch, :].r("p (s t) -> p s t", s=2)
                    fw.ts(dve, CV[:, ch, :], bt, CONVW[:, ch, 2:3], ALU.mult)
                    for jj, off in ((0, -2), (1, -1), (3, 1), (4, 2)):
                        if off < 0:
                            dst, src = a3[:, :, -off:T], x3[:, :, 0:T + off]
                        else:
                            dst, src = a3[:, :, 0:T - off], x3[:, :, off:T]
                        fw.stt(dst, src, CONVW[:, ch, jj:jj + 1], dst, ALU.mult, ALU.add)
                    fw.actf(CV[:, ch, :], CV[:, ch, :], AF.Silu)
                for ch in range(4):
                    fw.actf(SQ, CV[:, ch, :], AF.Square)
                    ps = PS[ch % 2]
                    fw.mm(ps, BONES, SQ)
                    fw.actf(RN, ps, AF.Sqrt, bias=EPSC[:, 0:1])
                    fw.recip(RN, RN)
                    fw.tt(dve, CV[:, ch, :], CV[:, ch, :], RN, ALU.mult)
                for g in range(4):
                    j = g % 2
                    fw.ts(dve, R2[0:64, g, :, 0], CV[0:64, j, :], 0.125, ALU.mult)
                    fw.ts(dve, R2[64:128, g, :, 1], CV[64:128, j, :], 0.125, ALU.mult)
                    fw.cp(act, A16[:, g, :], CV[:, 2 + j, :])
                ps = PS[0]
                proj_fm(ps, W16, 1024, 128, tok0)
                fw.actf(SB_, ps, AF.Sigmoid)
                fw.actf(GG, ps, AF.Exp, bias=DNP[:, 30:31])
                fw.actf(GG, GG, AF.Ln, bias=EPSC[:, 1:2])
                fw.ts(dve, GG, GG, NAe[:, 0:1], ALU.mult)
                for g in range(4):
                    j = g % 2
                    fw.mm(PS[2], EXPM[:, 0, g, :], SB_)
                    fw.mm(PS[3], EXPM[:, 1, g, :], GG)
                    fw.actf(W_s[:, g, :], PS[3], AF.Exp)
                    fw.tt(dve, KB, CV[:, 2 + j, :], PS[2], ALU.mult)
                    fw.cp(act, K_s[:, g, :], KB)
                    fw.stt(B_s[:, g, :], KB, -1.0, W_s[:, g, :], ALU.mult, ALU.mult)
                for tt_ in range(4):
                    ps = PS[tt_ % 2]
                    for j in range(2):
                        fw.tr(ps[:, j * 128:(j + 1) * 128], CV[:, 4 + j, tt_ * 128:(tt_ + 1) * 128], IDF)
                    for d in range(2):
                        fw.cp(act, VTM[:, tt_, d, :, :].r("p m (j v) -> p j m v", j=2), ps[:, 0:256].r("p (j m v) -> p j m v", j=2, m=2))
                for hh in range(4):
                    ps = PS[2 + hh % 2]
                    proj_fm(ps, W16, 768 + hh * 64, 64, tok0)
                    fw.actf(ZU[:, hh, :], ps[0:64, :], AF.Silu)
                scan(W_s, B_s, K_s, A16, R2, VTM, OS, M)
                for s_ in range(2):
                    b = sp_ * 2 + s_
                    for d in range(2):
                        for j in range(2):
                            fw.ld(V(o_dn.ap[b, l, d, 2 * j:2 * j + 2].rearrange("m k v -> (m k) v"), o_dn.res), M[s_][d][:, j, :])
                for hh in range(4):
                    fw.actf(SQ[0:64, :], OS[:, hh, :], AF.Square)
                    ps = PS[hh % 2]
                    fw.mm(ps[0:64, :], CON[0:64, 256:320], SQ[0:64, :])
                    fw.actf(RN[0:64, :], ps[0:64, :], AF.Sqrt, bias=EPSC[0:64, 0:1])
                    fw.recip(RN[0:64, :], RN[0:64, :])
                    fw.tt(dve, RN[0:64, :], RN[0:64, :], OS[:, hh, :], ALU.mult)
                    fw.stt(MIXT[:, hh, tok0:tok0 + 512], RN[0:64, :], DNP[0:64, 32:33], ZU[:, hh, :], ALU.mult, ALU.mult)
            wout_acc(MIXT, WO, KB)
            fw.end()

        if PEER_ON:
            peer_phase(l, NT, CS, lambda j: X[:, j, :], lambda j, xt: None)

    fw.begin()
    if dbg and do_prompt:
        fw.ld(dbgo["x"].r("(j p) d -> p j d", p=128), X)
    FN = fw.sb([128, D], F32)
    fw.ld(FN, V(fng.ap[0:1, :].partition_broadcast(128), fng.res))
    JK = fw.sb([128, D], F32)
    YT = [fw.sb([128, D], F32) for _ in range(2)]
    for j in range(NT if do_prompt else 0):
        fw.actf(JK, X[:, j, :], AF.Square, accum=SS[:, j:j + 1])
        fw.actf(RSTD[:, j:j + 1], SS[:, j:j + 1], AF.Sqrt, bias=EPSC[:, 0:1], scale=1.0 / D)
        fw.recip(RSTD[:, j:j + 1], RSTD[:, j:j + 1])
        fw.stt(YT[j % 2], X[:, j, :], RSTD[:, j:j + 1], FN, ALU.mult, ALU.mult)
        fw.ld(y_prompt[j * 128:(j + 1) * 128, :], YT[j % 2])
    fw.end()
    fw.sect.close()
    fw.sect = None
    TS, NTS = 4096, 32

    def sample_section(sstage):
        H = {}
        fw.begin()
        if sstage < 5:
            ZT = fw.sb([128, D], BF16)
            fw.memset(dve, ZT, 0.0)
            for j in range(NTS):
                fw.ld(MIXD[j * 128:(j + 1) * 128, :], ZT)
        CT = fw.sb([128, 8], F32)
        fw.ld(CT, cs_in)
        fw.actf(CSs, CT, AF.Silu)
        fw.end()

        def proj_s(ps, W16, col0, ncol, tok0, n=512):
            for c in range(8):
                fw.mm(ps[0:ncol, 0:n], W16[:, c, col0:col0 + ncol], H['HTs'][:, c, tok0:tok0 + n], start=(c == 0), stop=(c == 7))

        def scan_s(W_s, B_s, K_s, A16, R2, VTM, OSb, M, M16, T1, SAp, Vp, Op):
            streams = []
            for d in range(2):
                st = dict(d=d, M=M[d], Mb=M16[d], mw=T1[d]["mw"], t1=T1[d]["t1"], t2=T1[d]["t2"],
                          sap=SAp[d], vp=Vp[d], op=Op[d])
                st["tau"] = (lambda i, d=d: i if d == 0 else 511 - i)
                st["slot"] = (lambda i, d=d: (i % 32) if d == 0 else 31 - (i % 32))

                def evac(i0, d=d, st=st):
                    t0 = i0 if d == 0 else 480 - i0
                    fw.cp(act, OSb[d][:, :, t0:t0 + 32].r("p (j m) t -> p t j m", j=2), st["op"])
                st["evac"] = evac
                streams.append(st)
            scan_core(streams, 512, W_s, B_s, K_s, A16, R2, VTM)

        def scan_res():
            SAp = [V(PS[d].ap[:, 0:128], Res()) for d in range(2)]
            Vp = [[V(PS[2 + d].ap[:, b_ * 128:(b_ + 1) * 128], Res()) for b_ in range(2)] for d in range(2)]
            Op = [V(PSW.ap[0:64, d * 128:(d + 1) * 128].rearrange("p (t j m) -> p t j m", t=32, j=2), Res()) for d in range(2)]
            return SAp, Vp, Op

        def y_to_mixd(Y, tok0, col0):
            OTt = [fw.sb([128, 256], BF16) for _ in range(2)]
            for tt_ in range(4):
                pb = PB[tt_ % 2]
                for hh in range(4):
                    fw.tr(pb[:, hh * 64:(hh + 1) * 64], Y[:, hh, tt_ * 128:(tt_ + 1) * 128], ID16[0:64, 0:64])
                fw.cp(act, OTt[tt_ % 2], pb[:, 0:256])
                fw.ld(MIXD[tok0 + tt_ * 128:tok0 + (tt_ + 1) * 128, col0:col0 + 256], OTt[tt_ % 2])

        for l in range(nl):
            xsrc = xs_in if l == 0 else XS
            fw.sect = ExitStack()
            H['HTs'] = fw.sb([128, 8, TS], BF16, glob=True)
            H['G1s'] = fw.sb([128, D], F32, glob=True)
            fw.begin()
            MODA = fw.sb([128, 2 * D], F32)
            S1, S2 = MODA[:, 0:D], MODA[:, D:2 * D]
            mod_chunks(l, 0, [MODA[:, n * 512:(n + 1) * 512] for n in range(4)] + [H['G1s'][:, n * 512:(n + 1) * 512] for n in range(2)], CSs)
            GN = fw.sb([128, D], F32)
            fw.ld(GN, V(n1g.ap[l:l + 1, :].partition_broadcast(128), n1g.res))
            fw.stt(S2, S2, 1.0, GN, ALU.add, ALU.mult)
            JK = fw.sb([128, D], F32)
            HBs = [fw.sb([128, D], BF16) for _ in range(2)]
            XT = [fw.sb([128, D], F32) for _ in range(2)]
            for j in range(NTS):
                fw.ld(XT[j % 2], xsrc[j * 128:(j + 1) * 128, :])
                rms_tile(j, S2, S1, HBs[j % 2], JK, XT[j % 2])
                to_ht(j, HBs[j % 2], H['HTs'][:, :, j * 128:(j + 1) * 128])
            fw.end()

            if sstage >= 2:
                fw.begin()
                W16 = load_w(l, CA, 512)
                GQ = fw.sb([128, 2, 64], F32)
                fw.ld(GQ, V(gqn.ap[l:l + 1].partition_broadcast(128), gqn.res))
                COS = fw.sb([128, NTS, 32], F32)
                SIN = fw.sb([128, NTS, 32], F32)
                fw.ld(COS, rcos.r("(j p) f -> p j f", p=128))
                fw.ld(SIN, rsin.r("(j p) f -> p j f", p=128))
                QT = fw.sb([128, 2, TS], BF16)
                KT = fw.sb([128, TS + 512], BF16)
                Vt = fw.sb([128, 36, 128], BF16)
                ATM = [fw.sb([128, 512], F32) for _ in range(2)]
                SQ = fw.sb([128, 384], F32)
                QK16 = [fw.sb([128, 384], BF16) for _ in range(2)]
                s6 = fw.sb([128, 8], F32)
                R1 = fw.sb([128, 6, 32], F32)
                R2_ = fw.sb([128, 6, 32], F32)
                for j in range(NTS):
                    ps = PS[j % 2]
                    for c in range(8):
                        fw.mm(ps, H['HTs'][:, c, j * 128:(j + 1) * 128], W16[:, c, :], start=(c == 0), stop=(c == 7))
                    a = ATM[j % 2]
                    fw.cp(act, a, ps)
                    fw.actf(SQ, a[:, 0:384], AF.Square)
                    fw.red(s6[:, 0:6], SQ.r("p (h d) -> p h d", d=64), ALU.add)
                    fw.actf(s6[:, 0:6], s6[:, 0:6], AF.Sqrt, bias=EPSC[:, 0:1], scale=1.0 / 64)
                    fw.recip(s6[:, 0:6], s6[:, 0:6])
                    qk = a[:, 0:384].r("p (h d) -> p h d", d=64)
                    fw.tt(dve, qk, qk, s6[:, 0:6].us(2).bc([128, 6, 64]), ALU.mult)
                    fw.tt(dve, qk[:, 0:4, :], qk[:, 0:4, :], GQ[:, 0:1, :].bc([128, 4, 64]), ALU.mult)
                    fw.tt(dve, qk[:, 4:6, :], qk[:, 4:6, :], GQ[:, 1:2, :].bc([128, 2, 64]), ALU.mult)
                    x1, x2 = qk[:, :, 0:32], qk[:, :, 32:64]
                    cb = COS[:, j, :].us(1).bc([128, 6, 32])
                    sb_ = SIN[:, j, :].us(1).bc([128, 6, 32])
                    q16 = QK16[j % 2].r("p (h d) -> p h d", d=64)
                    fw.tt(dve, R1, x1, cb, ALU.mult)
                    fw.tt(pool, R2_, x2, sb_, ALU.mult)
                    fw.tt(dve, q16[:, :, 0:32], R1, R2_, ALU.subtract)
                    fw.tt(dve, R1, x2, cb, ALU.mult)
                    fw.tt(pool, R2_, x1, sb_, ALU.mult)
                    fw.tt(dve, q16[:, :, 32:64], R1, R2_, ALU.add)
                    fw.cp(act, Vt[:, j, :], a[:, 384:512])
                    pb = PB[j % 2]
                    for c in range(3):
                        fw.tr(pb[:, c * 128:(c + 1) * 128], QK16[j % 2][:, c * 128:(c + 1) * 128], ID16)
                    fw.cp(act, QT[:, :, j * 128:(j + 1) * 128], pb[:, 0:256].r("p (c t) -> p c t", c=2))
                    fw.cp(act, KT[:, j * 128:(j + 1) * 128], pb[:, 256:384])
                CK = fw.sb([128, 4, 2, 64], F32)
                CK16 = fw.sb([128, 4, 128], BF16)
                for k_ in range(2):
                    fw.ld(CK[:, :, k_, :], cgk[l, k_].r("(i p) d -> p i d", p=128))
                fw.cp(dve, CK16, CK.r("p i k d -> p i (k d)"))
                for i in range(4):
                    fw.tr(PB[0][:, i * 128:(i + 1) * 128], CK16[:, i, :], ID16)
                fw.cp(act, KT[:, TS:TS + 512], PB[0][:, 0:512])
                for k_ in range(2):
                    fw.ld(CK[:, :, k_, :], cgv[l, k_].r("(i p) d -> p i d", p=128))
                fw.cp(dve, Vt[:, 32:36, :], CK.r("p i k d -> p i (k d)"))
                S = fw.sb([128, 4608], F32)
                P = fw.sb([128, 4608], BF16)
                PT = fw.sb([128, 36, 128], BF16)
                st = fw.sb([128, 4], F32)
                OT = [fw.sb([128, 256], BF16) for _ in range(2)]
                ops = PSW[:, 0:64]
                for j in range(NTS):
                    ot = OT[j % 2]
                    for hh in range(4):
                        pbase = (hh % 2) * 64
                        qv = QT[pbase:pbase + 64, hh // 2, j * 128:(j + 1) * 128]
                        for kc in range(9):
                            ps = PS[kc % 4]
                            fw.mm(ps, qv, KT[pbase:pbase + 64, kc * 512:(kc + 1) * 512])
                            fw.cp(act if kc % 2 else dve, S[:, kc * 512:(kc + 1) * 512], ps)
                        fw.red(st[:, 0:1], S, ALU.max)
                        fw.ts(dve, st[:, 1:2], st[:, 0:1], -0.125, ALU.mult)
                        fw.actf(P, S, AF.Exp, bias=st[:, 1:2], scale=0.125, accum=st[:, 2:3])
                        fw.recip(st[:, 3:4], st[:, 2:3])
                        for grp in range(5):
                            nb = min(8, 36 - grp * 8)
                            pb = PB[grp % 2]
                            for b8 in range(nb):
                                blk = grp * 8 + b8
                                fw.tr(pb[:, b8 * 128:(b8 + 1) * 128], P[:, blk * 128:(blk + 1) * 128], ID16)
                            fw.cp(act if grp % 2 else dve, PT[:, grp * 8:grp * 8 + nb, :], pb[:, 0:nb * 128].r("p (a t) -> p a t", a=nb))
                        kv = hh % 2
                        for blk in range(36):
                            fw.mm(ops, PT[:, blk, :], Vt[:, blk, kv * 64:(kv + 1) * 64], start=(blk == 0), stop=(blk == 35))
                        ho = (0, 2, 1, 3)[hh]
                        fw.ts(dve, ot[:, ho * 64:(ho + 1) * 64], ops, st[:, 3:4], ALU.mult)
                    fw.ld(MIXD[j * 128:(j + 1) * 128, 0:256], ot)
                fw.end()

            if sstage >= 3:
                fw.begin()
                W16 = load_w(l, CC, 768)
                QTc = fw.sb([128, 2, TS], BF16)
                KTc = fw.sb([128, 2, TS + 512], BF16)
                VN = fw.sb([128, NTS, 256], BF16)
                VNs = fw.sb([128, NTS, 256], BF16)
                C16 = [fw.sb([128, 512], BF16) for _ in range(2)]
                for j in range(NTS):
                    pq, pv_ = PS[(2 * j) % 4], PS[(2 * j + 1) % 4]
                    for c in range(8):
                        fw.mm(pq, H['HTs'][:, c, j * 128:(j + 1) * 128], W16[:, c, 0:512], start=(c == 0), stop=(c == 7))
                    for c in range(8):
                        fw.mm(pv_[:, 0:256], H['HTs'][:, c, j * 128:(j + 1) * 128], W16[:, c, 512:768], start=(c == 0), stop=(c == 7))
                    NB_ = 9
                    fw.cp(act, C16[j % 2], pq)
                    if NB_ >= 2:
                        fw.cp(dve, VN[:, j, :], pv_[:, 0:256])
                    pb = PB[j % 2]
                    for c in range(4 if NB_ >= 3 else 0):
                        fw.tr(pb[:, c * 128:(c + 1) * 128], C16[j % 2][:, c * 128:(c + 1) * 128], ID16)
                    if NB_ >= 4:
                        fw.cp(act, QTc[:, :, j * 128:(j + 1) * 128], pb[:, 0:256].r("p (c t) -> p c t", c=2))
                    if NB_ >= 5:
                        fw.cp(act, KTc[:, :, j * 128:(j + 1) * 128], pb[:, 256:512].r("p (c t) -> p c t", c=2))
                NAP = 9
                for j in range(NTS - 1 if NAP >= 2 else 0):
                    ps = PS[j % 4]
                    for c in range(8):
                        fw.mm(ps[:, 0:256], H['HTs'][:, c, 64 + j * 128:64 + (j + 1) * 128], W16[:, c, 512:768], start=(c == 0), stop=(c == 7))
                    fw.cp(act if j % 2 else dve, VNs[:, j, :], ps[:, 0:256])
                CKn = fw.sb([128, 4, 4, 64], F32)
                CK16 = fw.sb([128, 4, 256], BF16)
                Vctx = fw.sb([128, 4, 256], BF16)
                for k_ in range(4 if NAP >= 3 else 0):
                    fw.ld(CKn[:, :, k_, :], cnk[l, k_].r("(i p) d -> p i d", p=128))
                fw.cp(dve, CK16, CKn.r("p i h d -> p i (h d)"))
                for c in range(2 if NAP >= 3 else 0):
                    for i in range(4):
                        fw.tr(PB[c][:, i * 128:(i + 1) * 128], CK16[:, i, c * 128:(c + 1) * 128], ID16)
                    fw.cp(act, KTc[:, c, TS:TS + 512], PB[c][:, 0:512])
                for k_ in range(4):
                    fw.ld(CKn[:, :, k_, :], cnv[l, k_].r("(i p) d -> p i d", p=128))
                fw.cp(dve, Vctx, CKn.r("p i h d -> p i (h d)"))
                NBI = fw.sb([64, 4, 512], F32)
                NBE = fw.sb([64, 4, 512], F32)
                if NAP >= 4:
                    fw.ld(NBI, nab[l, 4].r("h q k -> q h k"))
                Sx = fw.sb([64, 1024], F32)
                Px = fw.sb([64, 1024], BF16)
                PTx = fw.sb([128, 8, 64], BF16)
                stx = fw.sb([64, 4], F32)
                OTx = [fw.sb([64, 256], BF16) for _ in range(2)]
                opx = PSW[0:64, 0:64]
                for r in range(64):
                    r0 = min(max(r - 4, 0), 56)
                    dl = r - r0
                    if dl != 4:
                        fw.ld(NBE, nab[l, dl].r("h q k -> q h k"))
                    NB = NBI if dl == 4 else NBE
                    ot = OTx[r % 2]
                    for h_ in range(4):
                        pbase, c = (h_ % 2) * 64, h_ // 2
                        qv = QTc[pbase:pbase + 64, c, r * 64:(r + 1) * 64]
                        fw.mm(PS[0][0:64, :], qv, KTc[pbase:pbase + 64, c, r0 * 64:r0 * 64 + 512])
                        fw.mm(PS[1][0:64, :], qv, KTc[pbase:pbase + 64, c, TS:TS + 512])
                        fw.stt(Sx[:, 0:512], PS[0][0:64, :], 0.125, NB[:, h_, :], ALU.mult, ALU.add)
                        fw.ts(dve, Sx[:, 512:1024], PS[1][0:64, :], 0.125, ALU.mult)
                        fw.red(stx[:, 0:1], Sx, ALU.max)
                        fw.ts(dve, stx[:, 1:2], stx[:, 0:1], -1.0, ALU.mult)
                        fw.actf(Px, Sx, AF.Exp, bias=stx[:, 1:2], scale=1.0, accum=stx[:, 2:3])
                        fw.recip(stx[:, 3:4], stx[:, 2:3])
                        for blk in range(8):
                            fw.tr(PB[0][:, blk * 64:(blk + 1) * 64], Px[:, blk * 128:(blk + 1) * 128], ID16[0:64, 0:64])
                        fw.cp(act, PTx, PB[0][:, 0:512].r("p (a t) -> p a t", a=8))
                        for blk in range(4):
                            vt = VN[:, r0 // 2 + blk, h_ * 64:(h_ + 1) * 64] if r0 % 2 == 0 else VNs[:, (r0 - 1) // 2 + blk, h_ * 64:(h_ + 1) * 64]
                            fw.mm(opx, PTx[:, blk, :], vt, start=(blk == 0), stop=False)
                        for blk in range(4):
                            fw.mm(opx, PTx[:, 4 + blk, :], Vctx[:, blk, h_ * 64:(h_ + 1) * 64], start=False, stop=(blk == 3))
                        fw.ts(dve, ot[:, h_ * 64:(h_ + 1) * 64], opx, stx[:, 3:4], ALU.mult)
                    fw.ld(MIXD[r * 64:(r + 1) * 64, 512:768], ot)
                fw.end()

            if sstage >= 4:
                fw.begin()
                W16 = load_w(l, CB, 1024)
                RWP = fw.sb([128, 64], F32)
                fw.ld(RWP, rwp_d[l])
                WSt = fw.sb([128, 1024], F32)
                W2A = fw.sb([128, 2, 2, 256], BF16)
                fw.ld(WSt, rww_d[l].r("p a b c -> p (a b c)"))
                fw.cp(pool, W2A, WSt.r("p (a b c) -> p a b c", a=2, b=2))
                G2P = fw.sb([128, 256], BF16)
                fw.ld(WSt[:, 0:256], rwg_d[l])
                fw.cp(pool, G2P, WSt[:, 0:256])
                OMKA = fw.sb([128, 2], F32)
                fw.ts(dve, OMKA, RWP[:, 24:26], -1.0, ALU.mult, 1.0, ALU.add)
                GS16 = fw.sb([128, 512], BF16)
                BON = fw.sb([64, 4, 512], F32)
                XR = fw.sb([128, 2, 512], F32)
                XK = fw.sb([128, 2, 512], F32)
                XV = fw.sb([128, 2, 512], F32)
                XL = fw.sb([128, 512], F32)
                BTc = [fw.sb([128, 513], F32) for _ in range(2)]
                TW16 = fw.sb([128, 512], BF16)
                XL16 = fw.sb([128, 512], BF16)
                AG = fw.sb([128, 512], F32)
                KK = fw.sb([128, 512], F32)
                SQ = fw.sb([128, 512], F32)
                RN = fw.sb([128, 512], F32)
                KE = fw.sb([128, 512], F32)
                DF = SQ
                VUt = RN[0:64, :]
                W_s = fw.sb([128, 4, 512], F32)
                B_s = fw.sb([128, 4, 512], BF16)
                K_s = fw.sb([128, 4, 512], BF16)
                A16 = fw.sb([128, 4, 512], BF16)
                R2 = fw.sb([128, 4, 512, 2], BF16)
                VTM = fw.sb([128, 4, 2, 2, 128], BF16)
                OSb = [fw.sb([64, 4, 512], F32) for _ in range(2)]
                M = [fw.sb([128, 2, 64], F32) for _ in range(2)]
                M16 = [fw.sb([128, 2, 64], BF16) for _ in range(2)]
                T1 = [dict(msh=fw.sb([128, 2, 64], F32), mw=fw.sb([128, 2, 64], F32), t1=fw.sb([128, 2, 64], F32), t2=fw.sb([128, 2, 64], F32)) for _ in range(2)]
                SAp, Vp, Op = scan_res()
                fw.memset(pool, R2, 0.0)
                S0t = fw.sb([64, 2, 64], F32)
                for d in range(2):
                    for j in range(2):
                        fw.ld(S0t, V(srw.ap[l, d, 2 * j:2 * j + 2].rearrange("m v k -> v m k"), srw.res))
                        fw.tr(PS[0][:, 0:64], S0t.r("v m k -> v (m k)"), IDF[0:64, 0:64])
                        fw.cp(act, M[d][:, j, :], PS[0][:, 0:64])
                    fw.cp(act, M16[d], M[d])
                for kb in range(8):
                    for d in range(2):
                        tok0 = (kb if d == 0 else 7 - kb) * 512
                        if d == 0:
                            ps = PS[0]
                            proj_s(ps, W16, 896, 128, tok0)
                            fw.actf(GS16, ps, AF.Sigmoid)
                            for hh in range(4):
                                fw.mm(PS[2][0:64, :], G2P[:, hh * 64:(hh + 1) * 64], GS16)
                                fw.cp(act, BON[:, hh, :], PS[2][0:64, :])
                            fw.ld(GZD[:, :, tok0:tok0 + 512], BON)
                        dsts = [XR[:, 0, :], XR[:, 1, :], XK[:, 0, :], XK[:, 1, :], XV[:, 0, :], XV[:, 1, :], XL]
                        th = tok0 - 1 if d == 0 else tok0 + 512
                        for c in range(7):
                            ps = PS[c % 2]
                            bt = BTc[c % 2]
                            proj_s(ps, W16, c * 128, 128, tok0)
                            main = bt[:, 1:513] if d == 0 else bt[:, 0:512]
                            halo = bt[:, 0:1] if d == 0 else bt[:, 512:513]
                            fw.cp(act, main, ps)
                            if 0 <= th < TS:
                                proj_s(PS[2], W16, c * 128, 128, th, 1)
                                fw.cp(act, halo, PS[2][:, 0:1])
                            else:
                                fw.memset(dve, halo, 0.0)
                            if d == 0:
                                fw.tt(dve, DF, bt[:, 0:512], bt[:, 1:513], ALU.subtract)
                            else:
                                fw.tt(dve, DF, bt[:, 1:513], bt[:, 0:512], ALU.subtract)
                            fw.stt(dsts[c], DF, RWP[:, d * 7 + c:d * 7 + c + 1], main, ALU.mult, ALU.add)
                        fw.actf(TW16, XL, AF.Tanh)
                        fw.cp(act, XL16, XL)
                        for j in range(2):
                            g = d * 2 + j
                            fw.mm(PS[2], W2A[:, d, 0, j * 128:(j + 1) * 128], TW16)
                            fw.actf(W_s[:, g, :], PS[2], AF.Sigmoid, bias=RWP[:, 14 + d * 2 + j:15 + d * 2 + j])
                            fw.actf(W_s[:, g, :], W_s[:, g, :], AF.Exp, scale=-math.exp(-0.5))
                            fw.mm(PS[3], W2A[:, d, 1, j * 128:(j + 1) * 128], XL16)
                            fw.actf(AG, PS[3], AF.Sigmoid, bias=RWP[:, 18 + d * 2 + j:19 + d * 2 + j])
                            fw.ts(dve, KK, XK[:, j, :], RWP[:, 22 + j:23 + j], ALU.mult)
                            fw.actf(SQ, KK, AF.Square)
                            fw.mm(PS[2], BONES, SQ)
                            fw.actf(RN, PS[2], AF.Sqrt, bias=EPSC[:, 0:1])
                            fw.recip(RN, RN)
                            fw.tt(dve, KK, KK, RN, ALU.mult)
                            fw.ts(dve, A16[:, g, :], KK, -1.0, ALU.mult)
                            fw.tt(dve, B_s[:, g, :], KK, AG, ALU.mult)
                            fw.ts(dve, SQ, AG, RWP[:, 24 + j:25 + j], ALU.mult, OMKA[:, j:j + 1], ALU.add)
                            fw.tt(dve, KE, XK[:, j, :], SQ, ALU.mult)
                            fw.cp(act, K_s[:, g, :], KE)
                            fw.cp(act, R2[0:64, g, :, 0], XR[0:64, j, :])
                            fw.cp(act, R2[64:128, g, :, 1], XR[64:128, j, :])
                            fw.stt(KE, XR[:, j, :], RWP[:, 28 + j:29 + j], KE, ALU.mult, ALU.mult)
                            for m in range(2):
                                hh = 2 * j + m
                                fw.mm(PS[2][0:64, :], BONES[:, m * 64:(m + 1) * 64], KE)
                                fw.mm(PS[3][0:64, :], IDF[:, m * 64:(m + 1) * 64], XV[:, j, :])
                                fw.cp(act, VUt, PS[3][0:64, :])
                                fw.tt(dve, BON[:, hh, :], PS[2][0:64, :], VUt, ALU.mult)
                        fw.ld(BOND[d][:, :, tok0:tok0 + 512], BON)
                        for tt_ in range(4):
                            ps = PS[tt_ % 2]
                            for j in range(2):
                                fw.tr(ps[:, j * 128:(j + 1) * 128], XV[:, j, tt_ * 128:(tt_ + 1) * 128], IDF)
                            fw.cp(act, VTM[:, tt_, d, :, :].r("p m (j v) -> p j m v", j=2), ps[:, 0:256].r("p (j m v) -> p j m v", j=2, m=2))
                    scan_s(W_s, B_s, K_s, A16, R2, VTM, OSb, M, M16, T1, SAp, Vp, Op)
                    for d in range(2):
                        tok0 = (kb if d == 0 else 7 - kb) * 512
                        fw.ld(ODD[d][:, :, tok0:tok0 + 512], OSb[d])
                fw.end()
                fw.begin()
                RWU = fw.sb([64, 32], F32)
                fw.ld(RWU, rwu_d[l])
                OF = fw.sb([64, 4, 512], F32)
                OB = fw.sb([64, 4, 512], F32)
                B0 = fw.sb([64, 4, 512], F32)
                B1 = fw.sb([64, 4, 512], F32)
                GZ = fw.sb([64, 4, 512], F32)
                Y = fw.sb([64, 4, 512], BF16)
                SQ = fw.sb([64, 512], F32)
                RN = fw.sb([64, 512], F32)
                ONES = CON[0:64, 256:320]
                for ch in range(8):
                    tok0 = ch * 512
                    fw.ld(OF, ODD[0][:, :, tok0:tok0 + 512])
                    fw.ld(OB, ODD[1][:, :, tok0:tok0 + 512])
                    fw.ld(B0, BOND[0][:, :, tok0:tok0 + 512])
                    fw.ld(B1, BOND[1][:, :, tok0:tok0 + 512])
                    fw.ld(GZ, GZD[:, :, tok0:tok0 + 512])
                    fw.tt(pool, OF, OF, OB, ALU.add)
                    fw.tt(pool, B0, B0, B1, ALU.add)
                    for hh in range(4):
                        fw.mm(PS[0][0:64, :], ONES, OF[:, hh, :])
                        fw.tt(dve, SQ, OF[:, hh, :], PS[0][0:64, :], ALU.subtract)
                        fw.actf(RN, SQ, AF.Square)
                        fw.mm(PS[1][0:64, :], ONES, RN)
                        fw.actf(RN, PS[1][0:64, :], AF.Sqrt, bias=EPSC[0:64, 2:3])
                        fw.recip(RN, RN)
                        fw.tt(dve, SQ, SQ, RN, ALU.mult)
                        fw.ts(dve, SQ, SQ, RWU[:, 8 + hh:9 + hh], ALU.mult, RWU[:, 12 + hh:13 + hh], ALU.add)
                        fw.tt(dve, SQ, SQ, B0[:, hh, :], ALU.add)
                        fw.tt(dve, Y[:, hh, :], SQ, GZ[:, hh, :], ALU.mult)
                    y_to_mixd(Y, tok0, 256)
                fw.end()

            if sstage >= 5:
                fw.begin()
                W16 = load_w(l, CD, 1152)
                DNP = fw.sb([128, 48], F32)
                fw.ld(DNP, dnp_d[l])
                EXPM = fw.sb([128, 2, 4, 128], F32)
                fw.ld(EXPM, expm_d)
                CONVW = DNP[:, 0:30].r("p (c j) -> p c j", j=5)
                NAe = fw.sb([128, 1], F32)
                fw.actf(NAe, DNP[:, 31:32], AF.Exp)
                fw.ts(dve, NAe, NAe, -1.0, ALU.mult)
                BT1 = [fw.sb([128, 516], F32) for _ in range(2)]
                CV = fw.sb([128, 6, 512], F32)
                SQ = fw.sb([128, 512], F32)
                RN = fw.sb([128, 512], F32)
                KB = fw.sb([128, 512], F32)
                SB_ = fw.sb([128, 512], F32)
                GG = fw.sb([128, 512], F32)
                W_s = fw.sb([128, 4, 512], F32)
                B_s = fw.sb([128, 4, 512], BF16)
                K_s = fw.sb([128, 4, 512], BF16)
                A16 = fw.sb([128, 4, 512], BF16)
                R2 = fw.sb([128, 4, 512, 2], BF16)
                VTM = fw.sb([128, 4, 2, 2, 128], BF16)
                ZU = fw.sb([64, 4, 512], F32)
                OSb = [fw.sb([64, 4, 512], F32) for _ in range(2)]
                M = [fw.sb([128, 2, 64], F32) for _ in range(2)]
                M16 = [fw.sb([128, 2, 64], BF16) for _ in range(2)]
                T1 = [dict(msh=fw.sb([128, 2, 64], F32), mw=fw.sb([128, 2, 64], F32), t1=fw.sb([128, 2, 64], F32), t2=fw.sb([128, 2, 64], F32)) for _ in range(2)]
                SAp, Vp, Op = scan_res()
                fw.memset(pool, R2, 0.0)
                for d in range(2):
                    for j in range(2):
                        fw.ld(M[d][:, j, :], V(sdn.ap[l, d, 2 * j:2 * j + 2].rearrange("m k v -> (m k) v"), sdn.res))
                    fw.cp(act, M16[d], M[d])
                for kb in range(8):
                    for d in range(2):
                        tok0 = (kb if d == 0 else 7 - kb) * 512
                        for ch in range(6):
                            ps = PS[ch % 2]
                            bt = BT1[ch % 2]
                            proj_s(ps, W16, ch * 128, 128, tok0)
                            fw.cp(act, bt[:, 2:514], ps)
                            if tok0 > 0:
                                proj_s(PS[2], W16, ch * 128, 128, tok0 - 2, 2)
                                fw.cp(act, bt[:, 0:2], PS[2][:, 0:2])
                            else:
                                fw.memset(dve, bt[:, 0:2], 0.0)
                            if tok0 + 512 < TS:
                                proj_s(PS[3], W16, ch * 128, 128, tok0 + 512, 2)
                                fw.cp(act, bt[:, 514:516], PS[3][:, 0:2])
                            else:
                                fw.memset(dve, bt[:, 514:516], 0.0)
                            fw.ts(dve, CV[:, ch, :], bt[:, 0:512], CONVW[:, ch, 0:1], ALU.mult)
                            for jj in range(1, 5):
                                fw.stt(CV[:, ch, :], bt[:, jj:jj + 512], CONVW[:, ch, jj:jj + 1], CV[:, ch, :], ALU.mult, ALU.add)
                            fw.actf(CV[:, ch, :], CV[:, ch, :], AF.Silu)
                        for ch in range(4):
                            fw.actf(SQ, CV[:, ch, :], AF.Square)
                            ps = PS[ch % 2]
                            fw.mm(ps, BONES, SQ)
                            fw.actf(RN, ps, AF.Sqrt, bias=EPSC[:, 0:1])
                            fw.recip(RN, RN)
                            fw.tt(dve, CV[:, ch, :], CV[:, ch, :], RN, ALU.mult)
                        ps = PS[0]
                        proj_s(ps, W16, 1024, 128, tok0)
                        fw.actf(SB_, ps, AF.Sigmoid)
                        fw.actf(GG, ps, AF.Exp, bias=DNP[:, 30:31])
                        fw.actf(GG, GG, AF.Ln, bias=EPSC[:, 1:2])
                        fw.ts(dve, GG, GG, NAe[:, 0:1], ALU.mult)
                        for j in range(2):
                            g = d * 2 + j
                            fw.ts(dve, R2[0:64, g, :, 0], CV[0:64, j, :], 0.125, ALU.mult)
                            fw.ts(dve, R2[64:128, g, :, 1], CV[64:128, j, :], 0.125, ALU.mult)
                            fw.cp(act, A16[:, g, :], CV[:, 2 + j, :])
                            fw.mm(PS[2], EXPM[:, 0, g, :], SB_)
                            fw.mm(PS[3], EXPM[:, 1, g, :], GG)
                            fw.actf(W_s[:, g, :], PS[3], AF.Exp)
                            fw.tt(dve, KB, CV[:, 2 + j, :], PS[2], ALU.mult)
                            fw.cp(act, K_s[:, g, :], KB)
                            fw.stt(B_s[:, g, :], KB, -1.0, W_s[:, g, :], ALU.mult, ALU.mult)
                        for tt_ in range(4):
                            ps = PS[tt_ % 2]
                            for j in range(2):
                                fw.tr(ps[:, j * 128:(j + 1) * 128], CV[:, 4 + j, tt_ * 128:(tt_ + 1) * 128], IDF)
                            fw.cp(act, VTM[:, tt_, d, :, :].r("p m (j v) -> p j m v", j=2), ps[:, 0:256].r("p (j m v) -> p j m v", j=2, m=2))
                        if d == 0:
                            for hh in range(4):
                                ps = PS[2 + hh % 2]
                                proj_s(ps, W16, 768 + hh * 64, 64, tok0)
                                fw.actf(ZU[:, hh, :], ps[0:64, :], AF.Silu)
                            fw.ld(GZD[:, :, tok0:tok0 + 512], ZU)
                    scan_s(W_s, B_s, K_s, A16, R2, VTM, OSb, M, M16, T1, SAp, Vp, Op)
                    for d in range(2):
                        tok0 = (kb if d == 0 else 7 - kb) * 512
                        fw.ld(ODD[d][:, :, tok0:tok0 + 512], OSb[d])
                fw.end()
                fw.begin()
                DNP = fw.sb([128, 48], F32)
                fw.ld(DNP, dnp_d[l])
                OF = fw.sb([64, 4, 512], F32)
                OB = fw.sb([64, 4, 512], F32)
                GZ = fw.sb([64, 4, 512], F32)
                Y = fw.sb([64, 4, 512], BF16)
                SQ = fw.sb([64, 512], F32)
                RN = fw.sb([64, 512], F32)
                ONES = CON[0:64, 256:320]
                for ch in range(8):
                    tok0 = ch * 512
                    fw.ld(OF, ODD[0][:, :, tok0:tok0 + 512])
                    fw.ld(OB, ODD[1][:, :, tok0:tok0 + 512])
                    fw.ld(GZ, GZD[:, :, tok0:tok0 + 512])
                    fw.tt(pool, OF, OF, OB, ALU.add)
                    for hh in range(4):
                        fw.actf(SQ, OF[:, hh, :], AF.Square)
                        fw.mm(PS[hh % 2][0:64, :], ONES, SQ)
                        fw.actf(RN, PS[hh % 2][0:64, :], AF.Sqrt, bias=EPSC[0:64, 0:1])
                        fw.recip(RN, RN)
                        fw.tt(dve, RN, RN, OF[:, hh, :], ALU.mult)
                        fw.stt(Y[:, hh, :], RN, DNP[0:64, 32:33], GZ[:, hh, :], ALU.mult, ALU.mult)
                    y_to_mixd(Y, tok0, 768)
                fw.end()

            fw.begin()
            WOn = fw.sb([128, 8, D], BF16)
            WS = [fw.sb([128, D], F32) for _ in range(2)]
            for c in range(8):
                fw.ld(WS[c % 2], woutn[l, :, c, :])
                fw.cp(pool, WOn[:, c, :], WS[c % 2])
            MT = [fw.sb([128, D], BF16) for _ in range(2)]
            MF = [fw.sb([128, 8, 128], BF16) for _ in range(2)]
            XT = [fw.sb([128, D], F32) for _ in range(2)]
            TMP = fw.sb([128, 512], F32)
            for j in range(NTS):
                k = j % 2
                fw.ld(MT[k], MIXD[j * 128:(j + 1) * 128, :])
                fw.ld(XT[k], xsrc[j * 128:(j + 1) * 128, :])
                pb = PB[k]
                for c in range(8):
                    fw.tr(pb[:, c * 128:(c + 1) * 128], MT[k][:, c * 128:(c + 1) * 128], ID16)
                fw.cp(act, MF[k], pb.r("p (c t) -> p c t", c=8))
                for n in range(2):
                    ps = PS[(2 * j + n) % 4]
                    for c in range(8):
                        fw.mm(ps, MF[k][:, c, :], WOn[:, c, n * 512:(n + 1) * 512], start=(c == 0), stop=(c == 7))
                    fw.tt(dve, TMP, ps, H['G1s'][:, n * 512:(n + 1) * 512], ALU.mult)
                    fw.tt(pool, XT[k][:, n * 512:(n + 1) * 512], XT[k][:, n * 512:(n + 1) * 512], TMP, ALU.add)
                fw.ld(XS[j * 128:(j + 1) * 128, :], XT[k])
            fw.end()
            fw.sect.close()
            fw.sect = None
            if sstage <= 5:
                continue

            ring = {}

            def get_x(j):
                if "xt" not in ring:
                    ring["xt"] = [fw.sb([128, D], F32) for _ in range(2)]
                xt = ring["xt"][j % 2]
                fw.ld(xt, XS[j * 128:(j + 1) * 128, :])
                return xt

            def put_x(j, xt):
                fw.ld(XS[j * 128:(j + 1) * 128, :], xt)

            peer_phase(l, NTS, CSs, get_x, put_x)

        fw.begin()
        FN = fw.sb([128, D], F32)
        fw.ld(FN, V(fng.ap[0:1, :].partition_broadcast(128), fng.res))
        JK = fw.sb([128, D], F32)
        XT = [fw.sb([128, D], F32) for _ in range(2)]
        YT = [fw.sb([128, D], F32) for _ in range(2)]
        for j in range(NTS):
            k = j % 16
            fw.ld(XT[j % 2], XS[j * 128:(j + 1) * 128, :])
            fw.actf(JK, XT[j % 2], AF.Square, accum=SS[:, k:k + 1])
            fw.actf(RSTD[:, k:k + 1], SS[:, k:k + 1], AF.Sqrt, bias=EPSC[:, 0:1], scale=1.0 / D)
            fw.recip(RSTD[:, k:k + 1], RSTD[:, k:k + 1])
            fw.stt(YT[j % 2], XT[j % 2], RSTD[:, k:k + 1], FN, ALU.mult, ALU.mult)
            fw.ld(y_sample[j * 128:(j + 1) * 128, :], YT[j % 2])
        if dbg:
            for j in range(NTS):
                fw.ld(XT[j % 2], XS[j * 128:(j + 1) * 128, :])
                fw.ld(dbgo["xs"][j * 128:(j + 1) * 128, :], XT[j % 2])
        fw.end()

    if sstage >= 1:
        sample_section(sstage)
    fw.begin()
    for o in outs:
        if o.res.lw is not None:
            fw._wait(fw.sp, o.res.lw)
    fw.end()
    fw.close()
    return fw.nc, fw


def host_inputs(inputs, core, peer=True):
    f = lambda a: np.ascontiguousarray(a, dtype=np.float32)
    L = DEPTH
    m = {}
    m["xp"] = f(inputs["x_prompt"][4 * core:4 * core + 4].reshape(NTOK, D))
    m["cctx"] = f(inputs["c_ctx"].reshape(8, 128).T)
    m["wmod"] = f(inputs["w_mod"].reshape(L, 8, 128, 6 * D).transpose(0, 2, 1, 3))
    m["bmod"] = f(inputs["b_mod"])
    m["n1g"] = f(inputs["norm1_g"])
    m["n2g"] = f(inputs["norm2_g"])
    m["fng"] = f(inputs["final_norm_g"].reshape(1, D))
    w = inputs["w_in"]
    cols = []
    for h in (0, 2, 1, 3):
        cols += list(range(h * 64, h * 64 + 64))
    cols += list(range(256, 512))
    cols += list(range(1472, 2240))
    wperm = w[:, :, cols]
    wfull = np.zeros((L, D, NWC), np.float32)
    wfull[:, :, 0:1280] = wperm
    wfull[:, :, CB:CB + 768] = w[:, :, 512:1280]
    wfull[:, :, CB + 768:CB + 896] = w[:, :, 1280:1408]
    wfull[:, :, CB + 896:CB + 960] = w[:, :, 1408:1472]
    wfull[:, :, CD:CD + 768] = w[:, :, 2240:3008]
    wfull[:, :, CD + 768:CD + 1024] = w[:, :, 3024:3280]
    wfull[:, :, CD + 1024:CD + 1040] = w[:, :, 3008:3024]
    m["win"] = f(wfull.reshape(L, 8, 128, NWC).transpose(0, 2, 1, 3))
    wo = inputs["w_out"]
    rows = []
    for h in (0, 2, 1, 3):
        rows += list(range(h * 64, h * 64 + 64))
    rows += list(range(256, 1024))
    m["wout"] = f(wo[:, rows, :].reshape(L, 16, 64, D).transpose(0, 2, 1, 3))
    m["gqn"] = f(np.stack([inputs["gqa_q_norm"], inputs["gqa_k_norm"]], axis=1))
    con = np.zeros((128, 1024), np.float32)
    con[:, 0:128] = np.eye(128)
    p = np.arange(128)
    con[:, 128:256] = (p[:, None] // 64 == p[None, :] // 64)
    con[0:64, 256:320] = 1.0 / 64
    con[:, 320 + 128] = 1.0
    con[:, 576:592] = np.arange(16)[None, :]
    m["consts"] = con
    rwp = np.zeros((L, 128, 64), np.float32)
    mu = inputs["rw_mu"]
    st2 = lambda v: v.reshape(v.shape[:-1] + (2, 128)).swapaxes(-1, -2)
    for d in range(2):
        rwp[:, :, d * 7:d * 7 + 6] = mu[:, d, 0:768].reshape(L, 6, 128).transpose(0, 2, 1)
        rwp[:, d * 32:d * 32 + 32, d * 7 + 6] = mu[:, d, 768:800]
        rwp[:, 64 + d * 32:96 + d * 32, d * 7 + 6] = mu[:, d, 800:832]
        rwp[:, :, 14 + d * 2:16 + d * 2] = st2(inputs["rw_w0"][:, d])
        rwp[:, :, 18 + d * 2:20 + d * 2] = st2(inputs["rw_a0"][:, d])
    rwp[:, :, 22:24] = st2(inputs["rw_k_k"])
    rwp[:, :, 24:26] = st2(inputs["rw_k_a"])
    rwp[:, :, 28:30] = st2(inputs["rw_r_k"].reshape(L, 256))
    m["rwp"] = rwp
    rwu = np.zeros((L, 64, 32), np.float32)
    rwu[:, :, 8:12] = inputs["rw_ln_g"].reshape(L, 4, 64).transpose(0, 2, 1)
    rwu[:, :, 12:16] = inputs["rw_ln_b"].reshape(L, 4, 64).transpose(0, 2, 1)
    m["rwu"] = rwu
    rww = np.zeros((L, 128, 2, 2, 256), np.float32)
    for d in range(2):
        rww[:, d * 32:d * 32 + 32, d, 0, :] = inputs["rw_w2"][:, d]
        rww[:, 64 + d * 32:96 + d * 32, d, 1, :] = inputs["rw_a2"][:, d]
    m["rww"] = rww
    rwg = np.zeros((L, 128, 256), np.float32)
    rwg[:, 0:64] = inputs["rw_g2"]
    m["rwg"] = rwg
    m["xs_in"] = f(inputs["x_sample"][core])
    m["cs_in"] = f(inputs["c"][core].reshape(8, 128).T)
    m["cgk"] = f(inputs["cache_gqa_k"][core])
    m["cgv"] = f(inputs["cache_gqa_v"][core])
    m["cnk"] = f(inputs["cache_na_k"][core])
    m["cnv"] = f(inputs["cache_na_v"][core])
    m["srw"] = f(inputs["state_rwkv"][core])
    m["sdn"] = f(inputs["state_delta"][core])
    m["rcos"], m["rsin"] = _rope_tables()
    m["nab"] = _na_bias_table(inputs["na_bias"])
    m["woutn"] = f(inputs["w_out"].reshape(L, 8, 128, D).transpose(0, 2, 1, 3))
    dnp = np.zeros((L, 128, 48), np.float32)
    cw = inputs["dn_conv"]
    dnp[:, :, 0:30] = cw.reshape(L, 5, 6, 128).transpose(0, 3, 2, 1).reshape(L, 128, 30)
    dnp[:, 8:16, 30] = inputs["dn_dt_bias"].reshape(L, 8)
    dnp[:, 8:16, 31] = inputs["dn_a_log"].reshape(L, 8)
    dnp[:, 0:64, 32] = inputs["dn_norm_g"]
    m["dnp"] = dnp
    ex = np.zeros((128, 2, 4, 128), np.float32)
    for d in range(2):
        for j in range(2):
            for mm_ in range(2):
                ex[d * 4 + 2 * j + mm_, 0, d * 2 + j, mm_ * 64:(mm_ + 1) * 64] = 1.0
                ex[8 + d * 4 + 2 * j + mm_, 1, d * 2 + j, mm_ * 64:(mm_ + 1) * 64] = 1.0
    m["expm"] = ex
    if not peer:
        return m
    m["wq"] = f(inputs["peer_wq"].reshape(L, 8, 128, 2048).transpose(0, 2, 1, 3))
    m["keyt"] = f(inputs["peer_keys"].reshape(L, 16, 128, 128).transpose(0, 3, 1, 2))
    for i in range(L):
        m[f"pu{i}"] = f(inputs["peer_u"][i])
        m[f"pv{i}"] = f(inputs["peer_v"][i])
    return m


def _rope_tables():
    t = np.arange(4096)
    row, col = t // 64, t % 64
    inv = 10000.0 ** (-2.0 * np.arange(16) / 32)
    ang = np.concatenate([row[:, None] * inv[None], col[:, None] * inv[None]], axis=1)
    return np.cos(ang).astype(np.float32), np.sin(ang).astype(np.float32)


def _na_bias_table(na_bias):
    L = na_bias.shape[0]
    qc = np.arange(64)
    kc = np.arange(64)
    cstart = np.clip(qc - 8, 0, 48)
    valid = (kc[None, :] >= cstart[:, None]) & (kc[None, :] < cstart[:, None] + 16)
    dc = np.clip(kc[None, :] - qc[:, None], -15, 15) + 15
    out = np.empty((L, 8, 4, 64, 8, 64), np.float32)
    for dl in range(8):
        for w in range(8):
            dr = w + 7 - dl
            g = na_bias[:, :, dr, :][:, :, dc]
            out[:, dl, :, :, w, :] = np.where(valid[None, None], g, np.float32(-1e30))
    return out.reshape(L, 8, 4, 64, 512)


_CACHE = {}


def kernel(**inputs):
    inputs = {k: np.asarray(v) for k, v in inputs.items()}
    if "nc" not in _CACHE:
        _CACHE["nc"] = build()[0]
    nc = _CACHE["nc"]
    in_maps = [host_inputs(inputs, c) for c in range(NCORES)]
    res = run_bass_kernel_spmd(nc, in_maps, core_ids=list(range(NCORES)))
    R = res.results
    cat = lambda k: np.concatenate([np.asarray(r[k]) for r in R], axis=0)
    y_prompt = cat("y_prompt").reshape(32, T, D)
    y_sample = np.stack([np.asarray(r["y_sample"]) for r in R], axis=0)
    return (y_prompt, y_sample, cat("o_gk"), cat("o_gv"), cat("o_nk"), cat("o_nv"), cat("o_rw"), cat("o_dn"))
```

```python
import math
import numpy as np
from contextlib import ExitStack
import concourse.bass as bass
import concourse.mybir as mybir
from concourse.alu_op_type import AluOpType as ALU
from concourse.bass_utils import run_bass_kernel_spmd

F32 = mybir.dt.float32
BF16 = mybir.dt.bfloat16
I32 = mybir.dt.int32
U32 = mybir.dt.uint32
AF = mybir.ActivationFunctionType
AX = mybir.AxisListType

NCORES = 8
DEPTH = 2
D = 1024
NT = 8
NTOK = 1024
T = 256
EPS = 1e-6
RW_LN_EPS = 64e-5
CA, CC, CB, CD = 0, 512, 1280, 2304
NWC = 3456


class Res:
    __slots__ = ("lw", "rd")

    def __init__(self):
        self.lw = None
        self.rd = []


class V:
    __slots__ = ("ap", "res")

    def __init__(self, ap, res):
        self.ap = ap
        self.res = res

    def __getitem__(self, k):
        return V(self.ap[k], self.res)

    def r(self, pat, **kw):
        return V(self.ap.rearrange(pat, **kw), self.res)

    def bc(self, shape):
        return V(self.ap.broadcast_to(list(shape)), self.res)

    def us(self, ax):
        return V(self.ap.unsqueeze(ax), self.res)


class Eng:
    def __init__(self, name, getter, same_sync):
        self.name = name
        self.getter = getter
        self.ops = []
        self.count = 0
        self.sem = None
        self.seen = {}
        self.same_sync = same_sync


class FW:
    NDMA = 8

    def __init__(self):
        self.nc = bass.Bass("TRN2", target_bir_lowering=False)
        self.es = ExitStack()
        nc = self.nc
        self.pe = Eng("pe", lambda: nc.tensor, False)
        self.act = Eng("act", lambda: nc.scalar, True)
        self.dve = Eng("dve", lambda: nc.vector, True)
        self.pool = Eng("pool", lambda: nc.gpsimd, True)
        self.sp = Eng("sp", lambda: nc.sync, False)
        self.engs = [self.pe, self.act, self.dve, self.pool, self.sp]
        for e in self.engs:
            e.sem = self.es.enter_context(nc.semaphore("s_" + e.name))
        self.dsem = {}
        self.dcnt = {}
        for e in (self.sp, self.pool):
            self.dsem[e.name] = [self.es.enter_context(nc.semaphore(f"d_{e.name}{i}")) for i in range(self.NDMA)]
            self.dcnt[e.name] = 0
        self.ninst = 0
        self.pes = None
        self.sect = None
        self.uid = 0

    def _stack(self, glob):
        if glob:
            return self.sect if self.sect is not None else self.es
        return self.es if self.pes is None else self.pes

    def sb(self, shape, dt, glob=False, name=None):
        self.uid += 1
        t = self._stack(glob).enter_context(self.nc.sbuf_tensor(name or f"t{self.uid}", list(shape), dt))
        return V(t[:], Res())

    def psum(self, shape, dt, name=None):
        self.uid += 1
        t = self.es.enter_context(self.nc.psum_tensor(name or f"p{self.uid}", list(shape), dt))
        return V(t[:], Res())

    def dram(self, name, shape, dt, kind):
        t = self.nc.dram_tensor(name, list(shape), dt, kind=kind)
        return V(t.ap(), Res())

    def _wait(self, E, tok):
        kind, a, b = tok
        if kind == 'e':
            if a is E and not E.same_sync:
                return
            key = a.name
            sem = a.sem
        else:
            key = id(a)
            sem = a
        if E.seen.get(key, 0) >= b:
            return
        E.seen[key] = b
        E.ops.append(('w', sem, b))

    def _deps(self, E, reads, writes):
        for r in reads:
            if r.lw is not None:
                self._wait(E, r.lw)
        for w in writes:
            if w.lw is not None:
                self._wait(E, w.lw)
            for tok in w.rd:
                self._wait(E, tok)

    def _commit(self, tok, reads, writes):
        for r in reads:
            r.rd.append(tok)
            if len(r.rd) > 48:
                best = {}
                for t in r.rd:
                    k = t[1].name if t[0] == 'e' else id(t[1])
                    if k not in best or best[k][2] < t[2]:
                        best[k] = t
                r.rd = list(best.values())
        for w in writes:
            w.lw = tok
            w.rd = []

    def op(self, E, fn, reads, writes):
        reads = [v.res for v in reads]
        writes = [v.res for v in writes]
        self._deps(E, reads, writes)
        E.count += 1
        E.ops.append(('i', fn, E.sem, 1))
        self._commit(('e', E, E.count), reads, writes)
        self.ninst += 1

    def dma(self, E, fn, reads, writes):
        reads = [v.res for v in reads]
        writes = [v.res for v in writes]
        self._deps(E, reads, writes)
        k = self.dcnt[E.name]
        self.dcnt[E.name] = k + 1
        sem = self.dsem[E.name][k % self.NDMA]
        rnd = k // self.NDMA
        if rnd > 0:
            self._wait(E, ('d', sem, 16 * rnd))
        E.ops.append(('i', fn, sem, 16))
        self._commit(('d', sem, 16 * (rnd + 1)), reads, writes)
        self.ninst += 1

    def begin(self):
        self.pes = ExitStack()

    def _barrier(self):
        for E in self.engs:
            for X in self.engs:
                if X is not E and X.count > 0:
                    self._wait(E, ('e', X, X.count))
            for nm, sems in self.dsem.items():
                k = self.dcnt[nm]
                for i, s in enumerate(sems):
                    n = (k - i + self.NDMA - 1) // self.NDMA if k > i else 0
                    if n > 0:
                        self._wait(E, ('d', s, 16 * n))

    def end(self, barrier=True):
        if barrier:
            self._barrier()
        nc = self.nc

        def replay(E, h):
            for o in E.ops:
                if o[0] == 'w':
                    h.wait_ge(o[1], o[2])
                else:
                    o[1](h).then_inc(o[2], o[3])
            E.ops = []

        with nc.Block() as block:
            @block.tensor
            def _(h):
                replay(self.pe, h)

            @block.scalar
            def _(h):
                replay(self.act, h)

            @block.vector
            def _(h):
                replay(self.dve, h)

            @block.gpsimd
            def _(h):
                replay(self.pool, h)

            @block.sync
            def _(h):
                replay(self.sp, h)
        if self.pes is not None:
            self.pes.close()
            self.pes = None

    def close(self):
        self.es.close()

    def mm(self, out, lhsT, rhs, start=True, stop=True):
        self.op(self.pe, lambda h: h.matmul(out.ap, lhsT.ap, rhs.ap, start=start, stop=stop), [lhsT, rhs], [out])

    def tr(self, out, in_, ident):
        self.op(self.pe, lambda h: h.transpose(out.ap, in_.ap, ident.ap), [in_, ident], [out])

    def tt(self, E, out, a, b, op):
        self.op(E, lambda h: h.tensor_tensor(out=out.ap, in0=a.ap, in1=b.ap, op=op), [a, b], [out])

    def ts(self, E, out, a, s1, op0, s2=None, op1=None):
        rd = [a] + [s for s in (s1, s2) if isinstance(s, V)]
        g = lambda s: s.ap if isinstance(s, V) else s
        if op1 is None:
            self.op(E, lambda h: h.tensor_scalar(out=out.ap, in0=a.ap, scalar1=g(s1), scalar2=None, op0=op0), rd, [out])
        else:
            self.op(E, lambda h: h.tensor_scalar(out=out.ap, in0=a.ap, scalar1=g(s1), scalar2=g(s2), op0=op0, op1=op1), rd, [out])

    def stt(self, out, a, s, b, op0, op1):
        rd = [a, b] + ([s] if isinstance(s, V) else [])
        sv = s.ap if isinstance(s, V) else s
        self.op(self.dve, lambda h: h.scalar_tensor_tensor(out=out.ap, in0=a.ap, scalar=sv, in1=b.ap, op0=op0, op1=op1), rd, [out])

    def actf(self, out, in_, func, bias=None, scale=None, accum=None):
        rd = [in_] + ([bias] if isinstance(bias, V) else [])
        wr = [out] + ([accum] if accum is not None else [])
        kw = {}
        if bias is not None:
            kw["bias"] = bias.ap if isinstance(bias, V) else bias
        if scale is not None:
            kw["scale"] = scale
        if accum is not None:
            kw["accum_out"] = accum.ap
        self.op(self.act, lambda h: h.activation(out=out.ap, in_=in_.ap, func=func, **kw), rd, wr)

    def cp(self, E, out, in_):
        if E is self.act:
            self.op(E, lambda h: h.copy(out=out.ap, in_=in_.ap), [in_], [out])
        else:
            self.op(E, lambda h: h.tensor_copy(out=out.ap, in_=in_.ap), [in_], [out])

    def red(self, out, in_, op, axis=AX.X):
        self.op(self.dve, lambda h: h.tensor_reduce(out=out.ap, in_=in_.ap, axis=axis, op=op), [in_], [out])

    def recip(self, out, in_):
        self.op(self.dve, lambda h: h.reciprocal(out=out.ap, in_=in_.ap), [in_], [out])

    def memset(self, E, out, val):
        self.op(E, lambda h: h.memset(out.ap, val), [], [out])

    def ld(self, out, in_, E=None):
        E = E or self.sp
        self.dma(E, lambda h: h.dma_start(out=out.ap, in_=in_.ap), [in_], [out])


def build(stage=99, dbg=False, nl=DEPTH, do_prompt=True, sstage=99):
    fw = FW()
    sp, pe, act, dve, pool = fw.sp, fw.pe, fw.act, fw.dve, fw.pool
    L = DEPTH
    din = {}

    def inp(name, shape, dt=F32):
        din[name] = fw.dram(name, shape, dt, "ExternalInput")
        return din[name]

    xp = inp("xp", [NTOK, D])
    cctx = inp("cctx", [128, 8])
    wmod = inp("wmod", [L, 128, 8, 6 * D])
    bmod = inp("bmod", [L, 6 * D])
    n1g = inp("n1g", [L, D])
    n2g = inp("n2g", [L, D])
    fng = inp("fng", [1, D])
    win = inp("win", [L, 128, 8, NWC])
    wout = inp("wout", [L, 64, 16, D])
    gqn = inp("gqn", [L, 2, 64])
    consts = inp("consts", [128, 1024])
    rwp_d = inp("rwp", [L, 128, 64])
    rwu_d = inp("rwu", [L, 64, 32])
    rww_d = inp("rww", [L, 128, 2, 2, 256])
    rwg_d = inp("rwg", [L, 128, 256])
    dnp_d = inp("dnp", [L, 128, 48])
    expm_d = inp("expm", [128, 2, 4, 128])
    xs_in = inp("xs_in", [4096, D])
    cs_in = inp("cs_in", [128, 8])
    cgk = inp("cgk", [L, 2, 512, 64])
    cgv = inp("cgv", [L, 2, 512, 64])
    cnk = inp("cnk", [L, 4, 512, 64])
    cnv = inp("cnv", [L, 4, 512, 64])
    srw = inp("srw", [L, 2, 4, 64, 64])
    sdn = inp("sdn", [L, 2, 4, 64, 64])
    rcos = inp("rcos", [4096, 32])
    rsin = inp("rsin", [4096, 32])
    nab = inp("nab", [L, 8, 4, 64, 512])
    woutn = inp("woutn", [L, 128, 8, D])
    XS = fw.dram("xs_scr", [4096, D], F32, "Internal")
    MIXD = fw.dram("mixd_scr", [4096, D], BF16, "Internal")
    ODD = [fw.dram(f"odd{i}", [64, 4, 4096], F32, "Internal") for i in range(2)]
    BOND = [fw.dram(f"bond{i}", [64, 4, 4096], F32, "Internal") for i in range(2)]
    GZD = fw.dram("gzd", [64, 4, 4096], F32, "Internal")
    PEER_ON = stage >= 6
    if PEER_ON:
        wq_d = inp("wq", [L, 128, 8, 2048])
        keyt_d = inp("keyt", [L, 128, 16, 128])
        pu_d = [inp(f"pu{i}", [16384, D]) for i in range(L)]
        pv_d = [inp(f"pv{i}", [16384, D]) for i in range(L)]

    y_prompt = fw.dram("y_prompt", [NTOK, D], F32, "ExternalOutput")
    y_sample = fw.dram("y_sample", [4096, D], F32, "ExternalOutput")
    o_gk = fw.dram("o_gk", [4, L, 2, T, 64], F32, "ExternalOutput")
    o_gv = fw.dram("o_gv", [4, L, 2, T, 64], F32, "ExternalOutput")
    o_nk = fw.dram("o_nk", [4, L, 4, T, 64], F32, "ExternalOutput")
    o_nv = fw.dram("o_nv", [4, L, 4, T, 64], F32, "ExternalOutput")
    o_rw = fw.dram("o_rw", [4, L, 2, 4, 64, 64], F32, "ExternalOutput")
    o_dn = fw.dram("o_dn", [4, L, 2, 4, 64, 64], F32, "ExternalOutput")
    dbgo = {}
    if dbg:
        dbgo["x"] = fw.dram("dbg_x", [NTOK, D], F32, "ExternalOutput")
        dbgo["h"] = fw.dram("dbg_h", [NTOK, D], F32, "ExternalOutput")
        dbgo["xs"] = fw.dram("dbg_xs", [4096, D], F32, "ExternalOutput")

    CON = fw.sb([128, 1024], F32, glob=True)
    ID16 = fw.sb([128, 128], BF16, glob=True)
    EPSC = fw.sb([128, 4], F32, glob=True)
    SS = fw.sb([128, 16], F32, glob=True)
    RSTD = fw.sb([128, 16], F32, glob=True)
    IDF = CON[:, 0:128]
    BONES = CON[:, 128:256]
    PS = [fw.psum([128, 512], F32) for _ in range(4)]
    PB = [fw.psum([128, 1024], BF16) for _ in range(2)]
    PSW = fw.psum([128, 1024], F32)
    ZO = CON[:, 320:576]
    IOTA16 = CON[:, 576:592]
    CSs = fw.sb([128, 8], F32, glob=True)
    fw.sect = ExitStack()
    X = fw.sb([128, NT, D], F32, glob=True)
    G1 = fw.sb([128, D], F32, glob=True)
    HT = fw.sb([128, 8, NTOK], BF16, glob=True)
    CS = fw.sb([128, 8], F32, glob=True)

    def mod_chunks(l, n0, dsts, cs=None):
        cs = CS if cs is None else cs
        WM = [fw.sb([128, 8, 512], F32) for _ in range(2)]
        BM = [fw.sb([128, 512], F32) for _ in range(2)]
        for i, dst in enumerate(dsts):
            n = n0 + i
            fw.ld(WM[n % 2], wmod[l, :, :, n * 512:(n + 1) * 512])
            fw.ld(BM[n % 2], V(bmod.ap[l:l + 1, n * 512:(n + 1) * 512].partition_broadcast(128), bmod.res))
            ps = PS[n % 2]
            for c in range(8):
                fw.mm(ps, cs[:, c:c + 1].bc([128, 128]), WM[n % 2][:, c, :], start=(c == 0), stop=(c == 7))
            fw.tt(dve, dst, ps, BM[n % 2], ALU.add)
    outs = [y_prompt, y_sample, o_gk, o_gv, o_nk, o_nv, o_rw, o_dn] + list(dbgo.values())

    fw.begin()
    fw.ld(CON, consts)
    fw.cp(dve, ID16, IDF)
    fw.memset(dve, EPSC[:, 0:1], EPS)
    fw.memset(dve, EPSC[:, 1:2], 1.0)
    fw.memset(dve, EPSC[:, 2:3], RW_LN_EPS)
    fw.ld(X, xp.r("(j p) d -> p j d", p=128))
    CT = fw.sb([128, 8], F32)
    fw.ld(CT, cctx)
    fw.actf(CS, CT, AF.Silu)
    fw.end()

    def rms_tile(j, A, Sh, HB, JK, xt=None):
        xt = X[:, j, :] if xt is None else xt
        k = j % 16
        fw.actf(JK, xt, AF.Square, accum=SS[:, k:k + 1])
        fw.actf(RSTD[:, k:k + 1], SS[:, k:k + 1], AF.Sqrt, bias=EPSC[:, 0:1], scale=1.0 / D)
        fw.recip(RSTD[:, k:k + 1], RSTD[:, k:k + 1])
        fw.stt(JK, xt, RSTD[:, k:k + 1], A, ALU.mult, ALU.mult)
        fw.tt(dve, HB, JK, Sh, ALU.add)

    def to_ht(j, HB, dst=None):
        pb = PB[j % 2]
        for c in range(8):
            fw.tr(pb[:, c * 128:(c + 1) * 128], HB[:, c * 128:(c + 1) * 128], ID16)
        dst = HT[:, :, j * 128:(j + 1) * 128] if dst is None else dst
        fw.cp(act, dst, pb.r("p (c t) -> p c t", c=8))

    def load_w(l, col0, ncols):
        W16 = fw.sb([128, 8, ncols], BF16)
        WS = [fw.sb([128, ncols], F32) for _ in range(2)]
        for c in range(8):
            fw.ld(WS[c % 2], win[l, :, c, col0:col0 + ncols])
            fw.cp(pool, W16[:, c, :], WS[c % 2])
        return W16

    def load_wout(l, ch0):
        WO = fw.sb([64, 4, D], BF16)
        WS = [fw.sb([64, D], F32) for _ in range(2)]
        for c in range(4):
            fw.ld(WS[c % 2], wout[l, :, ch0 + c, :])
            fw.cp(pool, WO[:, c, :], WS[c % 2])
        return WO

    def wout_acc(MIXT, WO):
        TMP = [fw.sb([128, 512], F32)] * 2
        for j in range(NT):
            for n in range(2):
                ps = PS[(2 * j + n) % 4]
                for c in range(4):
                    fw.mm(ps, MIXT[:, c, j * 128:(j + 1) * 128], WO[:, c, n * 512:(n + 1) * 512], start=(c == 0), stop=(c == 3))
                tmp = TMP[n]
                fw.tt(dve, tmp, ps, G1[:, n * 512:(n + 1) * 512], ALU.mult)
                fw.tt(pool, X[:, j, n * 512:(n + 1) * 512], X[:, j, n * 512:(n + 1) * 512], tmp, ALU.add)

    def attention(QT, KT, VV, MIXT):
        PF = [fw.sb([128, 256], F32) for _ in range(2)]
        PN = [fw.sb([128, 256], BF16) for _ in range(2)]
        PT = [fw.sb([128, 2, 128], BF16) for _ in range(2)]
        ST = [fw.sb([128, 4], F32) for _ in range(2)]
        u = 0
        for b in range(4):
            for h in range(4):
                for qi in range(2):
                    k = u % 2
                    u += 1
                    tq0 = b * 256 + qi * 128
                    sps = PS[k]
                    fw.mm(sps[:, 0:256], QT(h)[:, tq0:tq0 + 128], KT(h)[:, b * 256:(b + 1) * 256])
                    st = ST[k]
                    fw.red(st[:, 0:1], sps[:, 0:256], ALU.max)
                    fw.ts(dve, st[:, 1:2], st[:, 0:1], -0.125, ALU.mult)
                    fw.actf(PF[k], sps[:, 0:256], AF.Exp, bias=st[:, 1:2], scale=0.125, accum=st[:, 2:3])
                    fw.recip(st[:, 3:4], st[:, 2:3])
                    fw.ts(dve, PN[k], PF[k], st[:, 3:4], ALU.mult)
                    pb = PB[k]
                    for kt in range(2):
                        fw.tr(pb[:, kt * 128:(kt + 1) * 128], PN[k][:, kt * 128:(kt + 1) * 128], ID16)
                    fw.cp(act, PT[k], pb[:, 0:256].r("p (a t) -> p a t", a=2))
                    ops = PS[2 + k]
                    for kt in range(2):
                        fw.mm(ops[0:64, 0:128], VV(h, b, kt), PT[k][:, kt, :], start=(kt == 0), stop=(kt == 1))
                    fw.cp(act, MIXT[:, h, tq0:tq0 + 128], ops[0:64, 0:128])


    def scan(W_s, B_s, K_s, A16, R2, VTM, OS, M):
        M16 = [[fw.sb([128, 2, 64], BF16) for d in range(2)] for s_ in range(2)]
        T1 = [[fw.sb([128, 2, 64], F32) for d in range(2)] for s_ in range(2)]
        SAp = [[V(PS[s_].ap[:, d * 128:(d + 1) * 128], Res()) for d in range(2)] for s_ in range(2)]
        Vp = [[V(PS[2 + s_].ap[:, d * 128:(d + 1) * 128], Res()) for d in range(2)] for s_ in range(2)]
        Op = [[V(PSW.ap[0:64, (s_ * 2 + d) * 128:(s_ * 2 + d + 1) * 128].rearrange("p (t j m) -> p t j m", t=32, j=2), Res())
               for d in range(2)] for s_ in range(2)]
        for s_ in range(2):
            for d in range(2):
                fw.memset(dve, M[s_][d], 0.0)
                fw.memset(dve, M16[s_][d], 0.0)
        for i in range(T):
            i0 = (i // 32) * 32
            sl = i - i0
            for s_ in range(2):
                for d in range(2):
                    td = i if d == 0 else T - 1 - i
                    tau = s_ * T + td
                    slot = sl if d == 0 else 31 - sl
                    Mv, Mb, t1 = M[s_][d], M16[s_][d], T1[s_][d]
                    sap, vp = SAp[s_][d], Vp[s_][d]
                    for j in range(2):
                        g = d * 2 + j
                        for m in range(2):
                            fw.mm(sap[m * 64:(m + 1) * 64, j * 64:(j + 1) * 64], A16[m * 64:(m + 1) * 64, g, tau:tau + 1].bc([64, 64]),
                                  Mb[m * 64:(m + 1) * 64, j, :])
                    for m in range(2):
                        fw.mm(vp[m * 64:(m + 1) * 64, :], ID16[:, tau % 128:tau % 128 + 1].bc([128, 64]), VTM[:, tau // 128, d, m, :])
                    gs = slice(d * 2, d * 2 + 2)
                    fw.tt(dve, Mv, Mv, W_s[:, gs, tau:tau + 1].bc([128, 2, 64]), ALU.mult)
                    fw.tt(dve, t1, sap.r("p (j v) -> p j v", j=2), B_s[:, gs, tau:tau + 1].bc([128, 2, 64]), ALU.mult)
                    fw.tt(dve, Mv, Mv, t1, ALU.add)
                    fw.tt(dve, t1, vp.r("p (j v) -> p j v", j=2), K_s[:, gs, tau:tau + 1].bc([128, 2, 64]), ALU.mult)
                    fw.tt(dve, Mv, Mv, t1, ALU.add)
                    fw.cp(act, Mb, Mv)
                    for j in range(2):
                        fw.mm(Op[s_][d][:, slot, j, :], Mb[:, j, :], R2[:, d * 2 + j, tau, :])
                    if sl == 31:
                        t0 = i0 if d == 0 else T - 32 - i0
                        dst = OS[:, :, s_ * T + t0:s_ * T + t0 + 32].r("p (j m) t -> p t j m", j=2)
                        if i0 < T // 2:
                            fw.cp(act, dst, Op[s_][d])
                        else:
                            fw.tt(dve, dst, dst, Op[s_][d], ALU.add)

    def proj_fm(ps, W16, col0, ncol, tok0):
        for c in range(8):
            fw.mm(ps[0:ncol, :], W16[:, c, col0:col0 + ncol], HT[:, c, tok0:tok0 + 512], start=(c == 0), stop=(c == 7))

    def peer_phase(l, ntiles, cs, get_x, put_x):
        fw.begin()
        WQ = fw.sb([128, 8, 2048], BF16)
        WS = [fw.sb([128, 1024], F32) for _ in range(2)]
        for c in range(16):
            fw.ld(WS[c % 2], wq_d[l, :, c // 2, (c % 2) * 1024:(c % 2 + 1) * 1024])
            fw.cp(pool, WQ[:, c // 2, (c % 2) * 1024:(c % 2 + 1) * 1024], WS[c % 2])
        KEYS = fw.sb([128, 16, 128], BF16)
        for c in range(2):
            fw.ld(WS[c], keyt_d[l, :, c * 8:(c + 1) * 8, :].r("p a k -> p (a k)"))
            fw.cp(pool, KEYS[:, c * 8:(c + 1) * 8, :], WS[c].r("p (a k) -> p a k", a=8))
        JK = fw.sb([128, D], F32)
        MODB = fw.sb([128, 3 * D], F32)
        S3, S4, G2 = MODB[:, 0:D], MODB[:, D:2 * D], MODB[:, 2 * D:3 * D]
        mod_chunks(l, 6, [MODB[:, n * 512:(n + 1) * 512] for n in range(6)], cs)
        fw.ld(JK, V(n2g.ap[l:l + 1, :].partition_broadcast(128), n2g.res))
        fw.stt(S4, S4, 1.0, JK, ALU.add, ALU.mult)
        H2Bs = [fw.sb([128, D], BF16) for _ in range(2)]
        QPT = fw.sb([128, 16, 128], BF16)
        SC = fw.sb([128, 16, 128], F32)
        S1t = fw.sb([128, 16, 16], F32)
        I1t = fw.sb([128, 16, 16], U32)
        I1f = fw.sb([128, 16, 16], F32)
        WK = fw.sb([128, 256], F32)
        CAND = fw.sb([128, 8, 16, 16], F32)
        EQ = UG[0].r("p (h c i) -> p h c i", h=8, c=16)[:, :, :, 0:16] if False else fw.sb([128, 8, 16, 16], F32)
        TOP = fw.sb([128, 8, 16], F32)
        POS = fw.sb([128, 8, 16], U32)
        PF_ = fw.sb([128, 8, 16], F32)
        PJ = fw.sb([128, 8, 16], F32)
        PI = fw.sb([128, 8, 16], F32)
        SEL = fw.sb([128, 2, 8, 16], F32)
        GT = fw.sb([128, 8, 16], F32)
        G8 = fw.sb([128, 8], F32)
        IDXF = fw.sb([128, 128], F32)
        IDXT = fw.sb([128, 128], I32)
        GTT = fw.sb([128, 128], F32)
        ACT1 = fw.sb([128, 128], F32)
        GA = fw.sb([128, 128], F32)
        GAM = [fw.sb([128, 128], F32) for _ in range(2)]
        NRING = 2 if ntiles == NT else 6
        UG = [fw.sb([128, D], F32) for _ in range(NRING)]
        VG = UG
        TMPX = JK[:, 0:512]
        pu_l = pu_d[l]
        pv_l = pv_d[l]
        HTl = fw.sb([128, 8, 128], BF16)
        for j in range(ntiles):
            H2B = H2Bs[j % 2]
            xt = get_x(j)
            rms_tile(j, S4, S3, H2B, JK, xt)
            to_ht(j, H2B, HTl)
            for cc in range(16):
                ps = PS[cc % 2]
                for c in range(8):
                    fw.mm(ps[:, 0:128], WQ[:, c, cc * 128:(cc + 1) * 128], HTl[:, c, :], start=(c == 0), stop=(c == 7))
                fw.cp(act, QPT[:, cc, :], ps[:, 0:128])
            for q in range(4):
                ps = PS[2 + q % 2]
                for i in range(4):
                    cc = q * 4 + i
                    fw.mm(ps[:, i * 128:(i + 1) * 128], QPT[:, cc, :], KEYS[:, cc, :])
                fw.cp(act, SC[:, q * 4:(q + 1) * 4, :], ps.r("p (a k) -> p a k", a=4))

            def top16(vals, idx, src, wk):
                fw.op(dve, lambda h: h.max(out=vals[:, 0:8].ap, in_=src.ap), [src], [vals])
                fw.op(dve, lambda h: h.max_index(out=idx[:, 0:8].ap, in_max=vals[:, 0:8].ap, in_values=src.ap), [src, vals], [idx])
                fw.op(dve, lambda h: h.match_replace(out=wk.ap, in_to_replace=vals[:, 0:8].ap, in_values=src.ap, imm_value=-1e30), [src, vals], [wk])
                fw.op(dve, lambda h: h.max(out=vals[:, 8:16].ap, in_=wk.ap), [wk], [vals])
                fw.op(dve, lambda h: h.max_index(out=idx[:, 8:16].ap, in_max=vals[:, 8:16].ap, in_values=wk.ap), [wk, vals], [idx])

            for cc in range(16):
                top16(S1t[:, cc, :], I1t[:, cc, :], SC[:, cc, :], WK[:, 0:128])
            S1v = S1t.r("p (h two) k -> p h two k", two=2)
            fw.tt(dve, CAND, S1v[:, :, 0, :].us(3).bc([128, 8, 16, 16]), S1v[:, :, 1, :].us(2).bc([128, 8, 16, 16]), ALU.add)
            for hh in range(8):
                top16(TOP[:, hh, :], POS[:, hh, :], CAND[:, hh].r("p i j -> p (i j)"), WK)
            fw.tt(dve, GT, TOP, TOP[:, :, 0:1].bc([128, 8, 16]), ALU.subtract)
            fw.actf(GT, GT, AF.Exp)
            fw.red(G8, GT, ALU.add)
            fw.recip(G8, G8)
            fw.tt(dve, GT, GT, G8.us(2).bc([128, 8, 16]), ALU.mult)
            fw.cp(dve, PF_, POS)
            fw.cp(dve, I1f, I1t)
            fw.ts(dve, PI, PF_, 16.0, ALU.is_ge)
            for i_ in range(2, 16):
                fw.stt(PI, PF_, 16.0 * i_, PI, ALU.is_ge, ALU.add)
            fw.stt(PJ, PI, -16.0, PF_, ALU.mult, ALU.add)
            I1v = I1f.r("p (h two) k -> p h two k", two=2)
            for w_, PP in enumerate((PI, PJ)):
                fw.tt(dve, EQ, PP.us(3).bc([128, 8, 16, 16]), IOTA16.us(1).us(1).bc([128, 8, 16, 16]), ALU.is_equal)
                fw.tt(dve, EQ, EQ, I1v[:, :, w_, :].us(2).bc([128, 8, 16, 16]), ALU.mult)
                fw.red(SEL[:, w_], EQ, ALU.add)
            fw.stt(IDXF.r("p (h k) -> p h k", h=8), SEL[:, 0], 128.0, SEL[:, 1], ALU.mult, ALU.add)
            fw.tr(PS[0][:, 0:128], IDXF, IDF)
            fw.cp(dve, IDXT, PS[0][:, 0:128])
            fw.tr(PS[1][:, 0:128], GT.r("p h k -> p (h k)"), IDF)
            fw.cp(dve, GTT, PS[1][:, 0:128])
            for n in range(128):
                ug = UG[n % NRING]
                fw.dma(pool, lambda h, ug=ug, n=n: h.indirect_dma_start(out=ug.ap, out_offset=None, in_=pu_l.ap,
                                                                       in_offset=bass.IndirectOffsetOnAxis(ap=IDXT.ap[:, n:n + 1], axis=0)),
                       [IDXT, pu_l], [ug])
                for n2 in range(2):
                    fw.mm(PSW[:, n2 * 512:(n2 + 1) * 512], ID16[:, n:n + 1].bc([128, 128]), H2B[:, n2 * 512:(n2 + 1) * 512])
                fw.op(dve, lambda h, ug=ug, n=n: h.scalar_tensor_tensor(out=JK.ap, in0=ug.ap, scalar=1.0, in1=PSW.ap,
                                                                        op0=ALU.mult, op1=ALU.mult, accum_out=ACT1.ap[:, n:n + 1]),
                      [ug, PSW], [JK, ACT1])
            fw.actf(GA, ACT1, AF.Gelu)
            fw.tt(dve, GA, GA, GTT, ALU.mult)
            for n in range(128):
                vg = VG[n % NRING]
                fw.dma(pool, lambda h, vg=vg, n=n: h.indirect_dma_start(out=vg.ap, out_offset=None, in_=pv_l.ap,
                                                                       in_offset=bass.IndirectOffsetOnAxis(ap=IDXT.ap[:, n:n + 1], axis=0)),
                       [IDXT, pv_l], [vg])
                gm = GAM[n % 2]
                fw.ts(dve, gm, ZO[:, 128 - n:256 - n], GA[:, n:n + 1], ALU.mult)
                for n2 in range(2):
                    fw.mm(PSW[:, n2 * 512:(n2 + 1) * 512], gm, vg[:, n2 * 512:(n2 + 1) * 512], start=(n == 0), stop=(n == 127))
            for n2 in range(2):
                fw.tt(dve, TMPX, PSW[:, n2 * 512:(n2 + 1) * 512], G2[:, n2 * 512:(n2 + 1) * 512], ALU.mult)
                fw.tt(dve, xt[:, n2 * 512:(n2 + 1) * 512], xt[:, n2 * 512:(n2 + 1) * 512], TMPX, ALU.add)
            put_x(j, xt)
        fw.end()


    for l in range(nl if do_prompt else 0):
        fw.begin()
        MODA = fw.sb([128, 2 * D], F32)
        S1, S2 = MODA[:, 0:D], MODA[:, D:2 * D]
        mod_chunks(l, 0, [MODA[:, n * 512:(n + 1) * 512] for n in range(4)] + [G1[:, n * 512:(n + 1) * 512] for n in range(2)])
        GN = fw.sb([128, D], F32)
        fw.ld(GN, V(n1g.ap[l:l + 1, :].partition_broadcast(128), n1g.res))
        fw.stt(S2, S2, 1.0, GN, ALU.add, ALU.mult)
        JK = fw.sb([128, D], F32)
        HBs = [fw.sb([128, D], BF16) for _ in range(2)]
        for j in range(NT):
            rms_tile(j, S2, S1, HBs[j % 2], JK)
            to_ht(j, HBs[j % 2])
        fw.end()
        if stage <= 1:
            break

        fw.begin()
        W16 = load_w(l, CA, 512)
        WO = load_wout(l, 0)
        GQ = fw.sb([128, 2, 64], F32)
        fw.ld(GQ, V(gqn.ap[l:l + 1].partition_broadcast(128), gqn.res))
        QKT = fw.sb([128, 3, NTOK], BF16)
        V16 = fw.sb([128, NT, 128], BF16)
        MIXT = fw.sb([64, 4, NTOK], BF16)
        ATM = [fw.sb([128, 512], F32) for _ in range(2)]
        SQ = fw.sb([128, 384], F32)
        QK16 = [fw.sb([128, 384], BF16) for _ in range(2)]
        s6 = fw.sb([128, 8], F32)
        for j in range(NT):
            ps = PS[j % 2]
            for c in range(8):
                fw.mm(ps, HT[:, c, j * 128:(j + 1) * 128], W16[:, c, :], start=(c == 0), stop=(c == 7))
            a = ATM[j % 2]
            fw.cp(act, a, ps)
            fw.actf(SQ, a[:, 0:384], AF.Square)
            fw.red(s6[:, 0:6], SQ.r("p (h d) -> p h d", d=64), ALU.add)
            fw.actf(s6[:, 0:6], s6[:, 0:6], AF.Sqrt, bias=EPSC[:, 0:1], scale=1.0 / 64)
            fw.recip(s6[:, 0:6], s6[:, 0:6])
            qk = a[:, 0:384].r("p (h d) -> p h d", d=64)
            fw.tt(dve, qk, qk, s6[:, 0:6].us(2).bc([128, 6, 64]), ALU.mult)
            fw.tt(dve, qk[:, 0:4, :], qk[:, 0:4, :], GQ[:, 0:1, :].bc([128, 4, 64]), ALU.mult)
            fw.tt(dve, qk[:, 4:6, :], qk[:, 4:6, :], GQ[:, 1:2, :].bc([128, 2, 64]), ALU.mult)
            b, t0 = j // 2, (j % 2) * 128
            fw.ld(V(o_gk.ap[b, l, :, t0:t0 + 128, :].rearrange("k t d -> t k d"), o_gk.res), a[:, 256:384].r("p (k d) -> p k d", d=64))
            fw.ld(V(o_gv.ap[b, l, :, t0:t0 + 128, :].rearrange("k t d -> t k d"), o_gv.res), a[:, 384:512].r("p (k d) -> p k d", d=64))
            fw.cp(act, QK16[j % 2], a[:, 0:384])
            fw.cp(act, V16[:, j, :], a[:, 384:512])
            pb = PB[j % 2]
            for c in range(3):
                fw.tr(pb[:, c * 128:(c + 1) * 128], QK16[j % 2][:, c * 128:(c + 1) * 128], ID16)
            fw.cp(act, QKT[:, :, j * 128:(j + 1) * 128], pb[:, 0:384].r("p (c t) -> p c t", c=3))
        attention(lambda h: QKT[(h % 2) * 64:(h % 2) * 64 + 64, h // 2, :],
                  lambda h: QKT[(h % 2) * 64:(h % 2) * 64 + 64, 2, :],
                  lambda h, b, kt: V16[:, 2 * b + kt, (h % 2) * 64:(h % 2) * 64 + 64], MIXT)
        wout_acc(MIXT, WO)
        fw.end()
        if stage <= 2:
            break

        fw.begin()
        W16 = load_w(l, CC, 768)
        WO = load_wout(l, 8)
        QKT = fw.sb([128, 4, NTOK], BF16)
        V16 = fw.sb([128, NT, 256], BF16)
        MIXT = fw.sb([64, 4, NTOK], BF16)
        CTM = [fw.sb([128, 768], F32) for _ in range(2)]
        C16 = [fw.sb([128, 512], BF16) for _ in range(2)]
        for j in range(NT):
            a = CTM[j % 2]
            for n, (c0, nn) in enumerate(((0, 512), (512, 256))):
                ps = PS[(2 * j + n) % 4]
                for c in range(8):
                    fw.mm(ps[:, 0:nn], HT[:, c, j * 128:(j + 1) * 128], W16[:, c, c0:c0 + nn], start=(c == 0), stop=(c == 7))
                fw.cp(act, a[:, c0:c0 + nn], ps[:, 0:nn])
            b, t0 = j // 2, (j % 2) * 128
            fw.ld(V(o_nk.ap[b, l, :, t0:t0 + 128, :].rearrange("k t d -> t k d"), o_nk.res), a[:, 256:512].r("p (k d) -> p k d", d=64))
            fw.ld(V(o_nv.ap[b, l, :, t0:t0 + 128, :].rearrange("k t d -> t k d"), o_nv.res), a[:, 512:768].r("p (k d) -> p k d", d=64))
            fw.cp(act, C16[j % 2], a[:, 0:512])
            fw.cp(dve, V16[:, j, :], a[:, 512:768])
            pb = PB[j % 2]
            for c in range(4):
                fw.tr(pb[:, c * 128:(c + 1) * 128], C16[j % 2][:, c * 128:(c + 1) * 128], ID16)
            fw.cp(act, QKT[:, :, j * 128:(j + 1) * 128], pb[:, 0:512].r("p (c t) -> p c t", c=4))
        attention(lambda h: QKT[(h % 2) * 64:(h % 2) * 64 + 64, h // 2, :],
                  lambda h: QKT[(h % 2) * 64:(h % 2) * 64 + 64, 2 + h // 2, :],
                  lambda h, b, kt: V16[:, 2 * b + kt, h * 64:h * 64 + 64], MIXT)
        wout_acc(MIXT, WO)
        fw.end()
        if stage <= 3:
            continue

        if stage >= 4:
            fw.begin()
            W16 = load_w(l, CB, 1024)
            WO = load_wout(l, 4)
            RWP = fw.sb([128, 64], F32)
            fw.ld(RWP, rwp_d[l])
            RWU = fw.sb([64, 32], F32)
            fw.ld(RWU, rwu_d[l])
            WSt = fw.sb([128, 1024], F32)
            W2A = fw.sb([128, 2, 2, 256], BF16)
            fw.ld(WSt, rww_d[l].r("p a b c -> p (a b c)"))
            fw.cp(pool, W2A, WSt.r("p (a b c) -> p a b c", a=2, b=2))
            G2P = fw.sb([128, 256], BF16)
            fw.ld(WSt[:, 0:256], rwg_d[l])
            fw.cp(pool, G2P, WSt[:, 0:256])
            OMKA = fw.sb([128, 2], F32)
            fw.ts(dve, OMKA, RWP[:, 24:26], -1.0, ALU.mult, 1.0, ALU.add)
            MIXT = fw.sb([64, 4, NTOK], BF16)
            GS16 = fw.sb([128, 512], BF16)
            BON = fw.sb([64, 4, 512], F32)
            XR = fw.sb([128, 2, 512], F32)
            XK = fw.sb([128, 2, 512], F32)
            XV = fw.sb([128, 2, 512], F32)
            XL = fw.sb([128, 512], F32)
            BTc = [fw.sb([128, 512], F32) for _ in range(2)]
            TW16 = fw.sb([128, 512], BF16)
            XL16 = fw.sb([128, 512], BF16)
            SG = fw.sb([128, 512], F32)
            AG = fw.sb([128, 512], F32)
            KK = fw.sb([128, 512], F32)
            SQ = fw.sb([128, 512], F32)
            RN = fw.sb([128, 512], F32)
            KE = fw.sb([128, 512], F32)
            DF = SQ
            VUt = RN[0:64, :]
            W_s = fw.sb([128, 4, 512], F32)
            B_s = fw.sb([128, 4, 512], BF16)
            K_s = fw.sb([128, 4, 512], BF16)
            A16 = fw.sb([128, 4, 512], BF16)
            R2 = fw.sb([128, 4, 512, 2], BF16)
            VTM = fw.sb([128, 4, 2, 2, 128], BF16)
            OS = fw.sb([64, 4, 512], F32)
            ST = fw.sb([64, 128], F32)
            M = [[fw.sb([128, 2, 64], F32) for d in range(2)] for s_ in range(2)]
            fw.memset(pool, R2, 0.0)
            for sp_ in range(2):
                tok0 = sp_ * 512
                ps = PS[0]
                proj_fm(ps, W16, 896, 128, tok0)
                fw.actf(GS16, ps, AF.Sigmoid)
                for d in range(2):
                    dsts = [XR[:, 0, :], XR[:, 1, :], XK[:, 0, :], XK[:, 1, :], XV[:, 0, :], XV[:, 1, :], XL]
                    for c in range(7):
                        ps = PS[c % 2]
                        bt = BTc[c % 2]
                        proj_fm(ps, W16, c * 128, 128, tok0)
                        fw.cp(act, bt, ps)
                        b3 = bt.r("p (s t) -> p s t", s=2)
                        d3 = DF.r("p (s t) -> p s t", s=2)
                        if d == 0:
                            fw.tt(dve, d3[:, :, 1:T], b3[:, :, 0:T - 1], b3[:, :, 1:T], ALU.subtract)
                            fw.ts(dve, d3[:, :, 0:1], b3[:, :, 0:1], -1.0, ALU.mult)
                        else:
                            fw.tt(dve, d3[:, :, 0:T - 1], b3[:, :, 1:T], b3[:, :, 0:T - 1], ALU.subtract)
                            fw.ts(dve, d3[:, :, T - 1:T], b3[:, :, T - 1:T], -1.0, ALU.mult)
                        fw.stt(dsts[c], DF, RWP[:, d * 7 + c:d * 7 + c + 1], bt, ALU.mult, ALU.add)
                    fw.actf(TW16, XL, AF.Tanh)
                    fw.cp(act, XL16, XL)
                    for j in range(2):
                        g = d * 2 + j
                        fw.mm(PS[2], W2A[:, d, 0, j * 128:(j + 1) * 128], TW16)
                        fw.actf(SG, PS[2], AF.Sigmoid, bias=RWP[:, 14 + d * 2 + j:15 + d * 2 + j])
                        fw.actf(W_s[:, g, :], SG, AF.Exp, scale=-math.exp(-0.5))
                        fw.mm(PS[3], W2A[:, d, 1, j * 128:(j + 1) * 128], XL16)
                        fw.actf(AG, PS[3], AF.Sigmoid, bias=RWP[:, 18 + d * 2 + j:19 + d * 2 + j])
                        fw.ts(dve, KK, XK[:, j, :], RWP[:, 22 + j:23 + j], ALU.mult)
                        fw.actf(SQ, KK, AF.Square)
                        fw.mm(PS[2], BONES, SQ)
                        fw.actf(RN, PS[2], AF.Sqrt, bias=EPSC[:, 0:1])
                        fw.recip(RN, RN)
                        fw.tt(dve, KK, KK, RN, ALU.mult)
                        fw.ts(dve, A16[:, g, :], KK, -1.0, ALU.mult)
                        fw.tt(dve, B_s[:, g, :], KK, AG, ALU.mult)
                        fw.ts(dve, SQ, AG, RWP[:, 24 + j:25 + j], ALU.mult, OMKA[:, j:j + 1], ALU.add)
                        fw.tt(dve, KE, XK[:, j, :], SQ, ALU.mult)
                        fw.cp(act, K_s[:, g, :], KE)
                        fw.cp(act, R2[0:64, g, :, 0], XR[0:64, j, :])
                        fw.cp(act, R2[64:128, g, :, 1], XR[64:128, j, :])
                        fw.stt(KE, XR[:, j, :], RWP[:, 28 + j:29 + j], KE, ALU.mult, ALU.mult)
                        for m in range(2):
                            hh = 2 * j + m
                            fw.mm(PS[2][0:64, :], BONES[:, m * 64:(m + 1) * 64], KE)
                            fw.mm(PS[3][0:64, :], IDF[:, m * 64:(m + 1) * 64], XV[:, j, :])
                            fw.cp(act, VUt, PS[3][0:64, :])
                            if d == 0:
                                fw.tt(dve, BON[:, hh, :], PS[2][0:64, :], VUt, ALU.mult)
                            else:
                                fw.tt(dve, VUt, PS[2][0:64, :], VUt, ALU.mult)
                                fw.tt(dve, BON[:, hh, :], BON[:, hh, :], VUt, ALU.add)
                    for tt_ in range(4):
                        ps = PS[tt_ % 2]
                        for j in range(2):
                            fw.tr(ps[:, j * 128:(j + 1) * 128], XV[:, j, tt_ * 128:(tt_ + 1) * 128], IDF)
                        fw.cp(act, VTM[:, tt_, d, :, :].r("p m (j v) -> p j m v", j=2), ps[:, 0:256].r("p (j m v) -> p j m v", j=2, m=2))
                scan(W_s, B_s, K_s, A16, R2, VTM, OS, M)
                for s_ in range(2):
                    b = sp_ * 2 + s_
                    for d in range(2):
                        for j in range(2):
                            fw.tr(PS[0][0:64, 0:128], M[s_][d][:, j, :], IDF)
                            fw.cp(act, ST, PS[0][0:64, 0:128])
                            fw.ld(V(o_rw.ap[b, l, d, 2 * j:2 * j + 2].rearrange("m v k -> v m k"), o_rw.res), ST.r("v (m k) -> v m k", m=2))
                for hh in range(4):
                    ONES = CON[0:64, 256:320]
                    fw.mm(PS[0][0:64, :], ONES, OS[:, hh, :])
                    fw.tt(dve, SQ[0:64, :], OS[:, hh, :], PS[0][0:64, :], ALU.subtract)
                    fw.actf(RN[0:64, :], SQ[0:64, :], AF.Square)
                    fw.mm(PS[1][0:64, :], ONES, RN[0:64, :])
                    fw.actf(RN[0:64, :], PS[1][0:64, :], AF.Sqrt, bias=EPSC[0:64, 2:3])
                    fw.recip(RN[0:64, :], RN[0:64, :])
                    fw.tt(dve, SQ[0:64, :], SQ[0:64, :], RN[0:64, :], ALU.mult)
                    fw.ts(dve, SQ[0:64, :], SQ[0:64, :], RWU[:, 8 + hh:9 + hh], ALU.mult, RWU[:, 12 + hh:13 + hh], ALU.add)
                    fw.tt(dve, SQ[0:64, :], SQ[0:64, :], BON[:, hh, :], ALU.add)
                    fw.mm(PS[2][0:64, :], G2P[:, hh * 64:(hh + 1) * 64], GS16)
                    fw.tt(dve, MIXT[:, hh, tok0:tok0 + 512], SQ[0:64, :], PS[2][0:64, :], ALU.mult)
            wout_acc(MIXT, WO)
            fw.end()

        if stage >= 5:
            fw.begin()
            W16 = load_w(l, CD, 1152)
            WO = load_wout(l, 12)
            DNP = fw.sb([128, 48], F32)
            fw.ld(DNP, dnp_d[l])
            EXPM = fw.sb([128, 2, 4, 128], F32)
            fw.ld(EXPM, expm_d)
            CONVW = DNP[:, 0:30].r("p (c j) -> p c j", j=5)
            NAe = fw.sb([128, 1], F32)
            fw.actf(NAe, DNP[:, 31:32], AF.Exp)
            fw.ts(dve, NAe, NAe, -1.0, ALU.mult)
            MIXT = fw.sb([64, 4, NTOK], BF16)
            BT1 = [fw.sb([128, 512], F32) for _ in range(2)]
            CV = fw.sb([128, 6, 512], F32)
            SQ = fw.sb([128, 512], F32)
            RN = fw.sb([128, 512], F32)
            KB = fw.sb([128, 512], F32)
            SB_ = fw.sb([128, 512], F32)
            GG = fw.sb([128, 512], F32)
            W_s = fw.sb([128, 4, 512], F32)
            B_s = fw.sb([128, 4, 512], BF16)
            K_s = fw.sb([128, 4, 512], BF16)
            A16 = fw.sb([128, 4, 512], BF16)
            R2 = fw.sb([128, 4, 512, 2], BF16)
            VTM = fw.sb([128, 4, 2, 2, 128], BF16)
            ZU = fw.sb([64, 4, 512], F32)
            OS = fw.sb([64, 4, 512], F32)
            M = [[fw.sb([128, 2, 64], F32) for d in range(2)] for s_ in range(2)]
            fw.memset(pool, R2, 0.0)
            for sp_ in range(2):
                tok0 = sp_ * 512
                for ch in range(6):
                    ps = PS[ch % 2]
                    bt = BT1[ch % 2]
                    proj_fm(ps, W16, ch * 128, 128, tok0)
                    fw.cp(act, bt, ps)
                    x3 = bt.r("p (s t) -> p s t", s=2)
                    a3 = CV[:, ch, :].r("p (s t) -> p s t", s=2)
                    fw.ts(dve, CV[:, ch, :], bt, CONVW[:, ch, 2:3], ALU.mult)
                    for jj, off in ((0, -2), (1, -1), (3, 1), (4, 2)):
                        if off < 0:
                            dst, src = a3[:, :, -off:T], x3[:, :, 0:T + off]
                        else:
                            dst, src = a3[:, :, 0:T - off], x3[:, :, off:T]
                        fw.stt(dst, src, CONVW[:, ch, jj:jj + 1], dst, ALU.mult, ALU.add)
                    fw.actf(CV[:, ch, :], CV[:, ch, :], AF.Silu)
                for ch in range(4):
                    fw.actf(SQ, CV[:, ch, :], AF.Square)
                    ps = PS[ch % 2]
                    fw.mm(ps, BONES, SQ)
                    fw.actf(RN, ps, AF.Sqrt, bias=EPSC[:, 0:1])
                    fw.recip(RN, RN)
                    fw.tt(dve, CV[:, ch, :], CV[:, ch, :], RN, ALU.mult)
                for g in range(4):
                    j = g % 2
                    fw.ts(dve, R2[0:64, g, :, 0], CV[0:64, j, :], 0.125, ALU.mult)
                    fw.ts(dve, R2[64:128, g, :, 1], CV[64:128, j, :], 0.125, ALU.mult)
                    fw.cp(act, A16[:, g, :], CV[:, 2 + j, :])
                ps = PS[0]
                proj_fm(ps, W16, 1024, 128, tok0)
                fw.actf(SB_, ps, AF.Sigmoid)
                fw.actf(GG, ps, AF.Exp, bias=DNP[:, 30:31])
                fw.actf(GG, GG, AF.Ln, bias=EPSC[:, 1:2])
                fw.ts(dve, GG, GG, NAe[:, 0:1], ALU.mult)
                for g in range(4):
                    j = g % 2
                    fw.mm(PS[2], EXPM[:, 0, g, :], SB_)
                    fw.mm(PS[3], EXPM[:, 1, g, :], GG)
                    fw.actf(W_s[:, g, :], PS[3], AF.Exp)
                    fw.tt(dve, KB, CV[:, 2 + j, :], PS[2], ALU.mult)
                    fw.cp(act, K_s[:, g, :], KB)
                    fw.stt(B_s[:, g, :], KB, -1.0, W_s[:, g, :], ALU.mult, ALU.mult)
                for tt_ in range(4):
                    ps = PS[tt_ % 2]
                    for j in range(2):
                        fw.tr(ps[:, j * 128:(j + 1) * 128], CV[:, 4 + j, tt_ * 128:(tt_ + 1) * 128], IDF)
                    for d in range(2):
                        fw.cp(act, VTM[:, tt_, d, :, :].r("p m (j v) -> p j m v", j=2), ps[:, 0:256].r("p (j m v) -> p j m v", j=2, m=2))
                for hh in range(4):
                    ps = PS[2 + hh % 2]
                    proj_fm(ps, W16, 768 + hh * 64, 64, tok0)
                    fw.actf(ZU[:, hh, :], ps[0:64, :], AF.Silu)
                scan(W_s, B_s, K_s, A16, R2, VTM, OS, M)
                for s_ in range(2):
                    b = sp_ * 2 + s_
                    for d in range(2):
                        for j in range(2):
                            fw.ld(V(o_dn.ap[b, l, d, 2 * j:2 * j + 2].rearrange("m k v -> (m k) v"), o_dn.res), M[s_][d][:, j, :])
                for hh in range(4):
                    fw.actf(SQ[0:64, :], OS[:, hh, :], AF.Square)
                    ps = PS[hh % 2]
                    fw.mm(ps[0:64, :], CON[0:64, 256:320], SQ[0:64, :])
                    fw.actf(RN[0:64, :], ps[0:64, :], AF.Sqrt, bias=EPSC[0:64, 0:1])
                    fw.recip(RN[0:64, :], RN[0:64, :])
                    fw.tt(dve, RN[0:64, :], RN[0:64, :], OS[:, hh, :], ALU.mult)
                    fw.stt(MIXT[:, hh, tok0:tok0 + 512], RN[0:64, :], DNP[0:64, 32:33], ZU[:, hh, :], ALU.mult, ALU.mult)
            wout_acc(MIXT, WO)
            fw.end()

        if PEER_ON:
            peer_phase(l, NT, CS, lambda j: X[:, j, :], lambda j, xt: None)

    fw.begin()
    if dbg and do_prompt:
        fw.ld(dbgo["x"].r("(j p) d -> p j d", p=128), X)
    FN = fw.sb([128, D], F32)
    fw.ld(FN, V(fng.ap[0:1, :].partition_broadcast(128), fng.res))
    JK = fw.sb([128, D], F32)
    YT = [fw.sb([128, D], F32) for _ in range(2)]
    for j in range(NT if do_prompt else 0):
        fw.actf(JK, X[:, j, :], AF.Square, accum=SS[:, j:j + 1])
        fw.actf(RSTD[:, j:j + 1], SS[:, j:j + 1], AF.Sqrt, bias=EPSC[:, 0:1], scale=1.0 / D)
        fw.recip(RSTD[:, j:j + 1], RSTD[:, j:j + 1])
        fw.stt(YT[j % 2], X[:, j, :], RSTD[:, j:j + 1], FN, ALU.mult, ALU.mult)
        fw.ld(y_prompt[j * 128:(j + 1) * 128, :], YT[j % 2])
    fw.end()
    fw.sect.close()
    fw.sect = None
    TS, NTS = 4096, 32

    def sample_section(sstage):
        H = {}
        fw.begin()
        if sstage < 5:
            ZT = fw.sb([128, D], BF16)
            fw.memset(dve, ZT, 0.0)
            for j in range(NTS):
                fw.ld(MIXD[j * 128:(j + 1) * 128, :], ZT)
        CT = fw.sb([128, 8], F32)
        fw.ld(CT, cs_in)
        fw.actf(CSs, CT, AF.Silu)
        fw.end()

        def proj_s(ps, W16, col0, ncol, tok0, n=512):
            for c in range(8):
                fw.mm(ps[0:ncol, 0:n], W16[:, c, col0:col0 + ncol], H['HTs'][:, c, tok0:tok0 + n], start=(c == 0), stop=(c == 7))

        def scan_s(W_s, B_s, K_s, A16, R2, VTM, OSb, M, M16, T1, SAp, Vp, Op):
            for i in range(512):
                i0 = (i // 32) * 32
                sl = i - i0
                for d in range(2):
                    tau = i if d == 0 else 511 - i
                    slot = sl if d == 0 else 31 - sl
                    Mv, Mb, t1 = M[d], M16[d], T1[d]
                    sap, vp = SAp[d], Vp[d]
                    for j in range(2):
                        g = d * 2 + j
                        for m in range(2):
                            fw.mm(sap[m * 64:(m + 1) * 64, j * 64:(j + 1) * 64], A16[m * 64:(m + 1) * 64, g, tau:tau + 1].bc([64, 64]),
                                  Mb[m * 64:(m + 1) * 64, j, :])
                    for m in range(2):
                        fw.mm(vp[m * 64:(m + 1) * 64, :], ID16[:, tau % 128:tau % 128 + 1].bc([128, 64]), VTM[:, tau // 128, d, m, :])
                    gs = slice(d * 2, d * 2 + 2)
                    fw.tt(dve, Mv, Mv, W_s[:, gs, tau:tau + 1].bc([128, 2, 64]), ALU.mult)
                    fw.tt(dve, t1, sap.r("p (j v) -> p j v", j=2), B_s[:, gs, tau:tau + 1].bc([128, 2, 64]), ALU.mult)
                    fw.tt(dve, Mv, Mv, t1, ALU.add)
                    fw.tt(dve, t1, vp.r("p (j v) -> p j v", j=2), K_s[:, gs, tau:tau + 1].bc([128, 2, 64]), ALU.mult)
                    fw.tt(dve, Mv, Mv, t1, ALU.add)
                    fw.cp(act, Mb, Mv)
                    for j in range(2):
                        fw.mm(Op[d][:, slot, j, :], Mb[:, j, :], R2[:, d * 2 + j, tau, :])
                    if sl == 31:
                        t0 = i0 if d == 0 else 480 - i0
                        fw.cp(act, OSb[d][:, :, t0:t0 + 32].r("p (j m) t -> p t j m", j=2), Op[d])

        def scan_res():
            SAp = [V(PS[d].ap[:, 0:128], Res()) for d in range(2)]
            Vp = [V(PS[2 + d].ap[:, 0:128], Res()) for d in range(2)]
            Op = [V(PSW.ap[0:64, d * 128:(d + 1) * 128].rearrange("p (t j m) -> p t j m", t=32, j=2), Res()) for d in range(2)]
            return SAp, Vp, Op

        def y_to_mixd(Y, tok0, col0):
            OTt = [fw.sb([128, 256], BF16) for _ in range(2)]
            for tt_ in range(4):
                pb = PB[tt_ % 2]
                for hh in range(4):
                    fw.tr(pb[:, hh * 64:(hh + 1) * 64], Y[:, hh, tt_ * 128:(tt_ + 1) * 128], ID16[0:64, 0:64])
                fw.cp(act, OTt[tt_ % 2], pb[:, 0:256])
                fw.ld(MIXD[tok0 + tt_ * 128:tok0 + (tt_ + 1) * 128, col0:col0 + 256], OTt[tt_ % 2])

        for l in range(nl):
            xsrc = xs_in if l == 0 else XS
            fw.sect = ExitStack()
            H['HTs'] = fw.sb([128, 8, TS], BF16, glob=True)
            H['G1s'] = fw.sb([128, D], F32, glob=True)
            fw.begin()
            MODA = fw.sb([128, 2 * D], F32)
            S1, S2 = MODA[:, 0:D], MODA[:, D:2 * D]
            mod_chunks(l, 0, [MODA[:, n * 512:(n + 1) * 512] for n in range(4)] + [H['G1s'][:, n * 512:(n + 1) * 512] for n in range(2)], CSs)
            GN = fw.sb([128, D], F32)
            fw.ld(GN, V(n1g.ap[l:l + 1, :].partition_broadcast(128), n1g.res))
            fw.stt(S2, S2, 1.0, GN, ALU.add, ALU.mult)
            JK = fw.sb([128, D], F32)
            HBs = [fw.sb([128, D], BF16) for _ in range(2)]
            XT = [fw.sb([128, D], F32) for _ in range(2)]
            for j in range(NTS):
                fw.ld(XT[j % 2], xsrc[j * 128:(j + 1) * 128, :])
                rms_tile(j, S2, S1, HBs[j % 2], JK, XT[j % 2])
                to_ht(j, HBs[j % 2], H['HTs'][:, :, j * 128:(j + 1) * 128])
            fw.end()

            if sstage >= 2:
                fw.begin()
                W16 = load_w(l, CA, 512)
                GQ = fw.sb([128, 2, 64], F32)
                fw.ld(GQ, V(gqn.ap[l:l + 1].partition_broadcast(128), gqn.res))
                COS = fw.sb([128, NTS, 32], F32)
                SIN = fw.sb([128, NTS, 32], F32)
                fw.ld(COS, rcos.r("(j p) f -> p j f", p=128))
                fw.ld(SIN, rsin.r("(j p) f -> p j f", p=128))
                QT = fw.sb([128, 2, TS], BF16)
                KT = fw.sb([128, TS + 512], BF16)
                Vt = fw.sb([128, 36, 128], BF16)
                ATM = [fw.sb([128, 512], F32) for _ in range(2)]
                SQ = fw.sb([128, 384], F32)
                QK16 = [fw.sb([128, 384], BF16) for _ in range(2)]
                s6 = fw.sb([128, 8], F32)
                R1 = fw.sb([128, 6, 32], F32)
                R2_ = fw.sb([128, 6, 32], F32)
                for j in range(NTS):
                    ps = PS[j % 2]
                    for c in range(8):
                        fw.mm(ps, H['HTs'][:, c, j * 128:(j + 1) * 128], W16[:, c, :], start=(c == 0), stop=(c == 7))
                    a = ATM[j % 2]
                    fw.cp(act, a, ps)
                    fw.actf(SQ, a[:, 0:384], AF.Square)
                    fw.red(s6[:, 0:6], SQ.r("p (h d) -> p h d", d=64), ALU.add)
                    fw.actf(s6[:, 0:6], s6[:, 0:6], AF.Sqrt, bias=EPSC[:, 0:1], scale=1.0 / 64)
                    fw.recip(s6[:, 0:6], s6[:, 0:6])
                    qk = a[:, 0:384].r("p (h d) -> p h d", d=64)
                    fw.tt(dve, qk, qk, s6[:, 0:6].us(2).bc([128, 6, 64]), ALU.mult)
                    fw.tt(dve, qk[:, 0:4, :], qk[:, 0:4, :], GQ[:, 0:1, :].bc([128, 4, 64]), ALU.mult)
                    fw.tt(dve, qk[:, 4:6, :], qk[:, 4:6, :], GQ[:, 1:2, :].bc([128, 2, 64]), ALU.mult)
                    x1, x2 = qk[:, :, 0:32], qk[:, :, 32:64]
                    cb = COS[:, j, :].us(1).bc([128, 6, 32])
                    sb_ = SIN[:, j, :].us(1).bc([128, 6, 32])
                    q16 = QK16[j % 2].r("p (h d) -> p h d", d=64)
                    fw.tt(dve, R1, x1, cb, ALU.mult)
                    fw.tt(pool, R2_, x2, sb_, ALU.mult)
                    fw.tt(dve, q16[:, :, 0:32], R1, R2_, ALU.subtract)
                    fw.tt(dve, R1, x2, cb, ALU.mult)
                    fw.tt(pool, R2_, x1, sb_, ALU.mult)
                    fw.tt(dve, q16[:, :, 32:64], R1, R2_, ALU.add)
                    fw.cp(act, Vt[:, j, :], a[:, 384:512])
                    pb = PB[j % 2]
                    for c in range(3):
                        fw.tr(pb[:, c * 128:(c + 1) * 128], QK16[j % 2][:, c * 128:(c + 1) * 128], ID16)
                    fw.cp(act, QT[:, :, j * 128:(j + 1) * 128], pb[:, 0:256].r("p (c t) -> p c t", c=2))
                    fw.cp(act, KT[:, j * 128:(j + 1) * 128], pb[:, 256:384])
                CK = fw.sb([128, 4, 2, 64], F32)
                CK16 = fw.sb([128, 4, 128], BF16)
                for k_ in range(2):
                    fw.ld(CK[:, :, k_, :], cgk[l, k_].r("(i p) d -> p i d", p=128))
                fw.cp(dve, CK16, CK.r("p i k d -> p i (k d)"))
                for i in range(4):
                    fw.tr(PB[0][:, i * 128:(i + 1) * 128], CK16[:, i, :], ID16)
                fw.cp(act, KT[:, TS:TS + 512], PB[0][:, 0:512])
                for k_ in range(2):
                    fw.ld(CK[:, :, k_, :], cgv[l, k_].r("(i p) d -> p i d", p=128))
                fw.cp(dve, Vt[:, 32:36, :], CK.r("p i k d -> p i (k d)"))
                S = fw.sb([128, 4608], F32)
                P = fw.sb([128, 4608], BF16)
                PT = fw.sb([128, 36, 128], BF16)
                st = fw.sb([128, 4], F32)
                OT = [fw.sb([128, 256], BF16) for _ in range(2)]
                ops = PSW[:, 0:64]
                for j in range(NTS):
                    ot = OT[j % 2]
                    for hh in range(4):
                        pbase = (hh % 2) * 64
                        qv = QT[pbase:pbase + 64, hh // 2, j * 128:(j + 1) * 128]
                        for kc in range(9):
                            ps = PS[kc % 4]
                            fw.mm(ps, qv, KT[pbase:pbase + 64, kc * 512:(kc + 1) * 512])
                            fw.cp(act if kc % 2 else dve, S[:, kc * 512:(kc + 1) * 512], ps)
                        fw.red(st[:, 0:1], S, ALU.max)
                        fw.ts(dve, st[:, 1:2], st[:, 0:1], -0.125, ALU.mult)
                        fw.actf(P, S, AF.Exp, bias=st[:, 1:2], scale=0.125, accum=st[:, 2:3])
                        fw.recip(st[:, 3:4], st[:, 2:3])
                        for grp in range(5):
                            nb = min(8, 36 - grp * 8)
                            pb = PB[grp % 2]
                            for b8 in range(nb):
                                blk = grp * 8 + b8
                                fw.tr(pb[:, b8 * 128:(b8 + 1) * 128], P[:, blk * 128:(blk + 1) * 128], ID16)
                            fw.cp(act if grp % 2 else dve, PT[:, grp * 8:grp * 8 + nb, :], pb[:, 0:nb * 128].r("p (a t) -> p a t", a=nb))
                        kv = hh % 2
                        for blk in range(36):
                            fw.mm(ops, PT[:, blk, :], Vt[:, blk, kv * 64:(kv + 1) * 64], start=(blk == 0), stop=(blk == 35))
                        ho = (0, 2, 1, 3)[hh]
                        fw.ts(dve, ot[:, ho * 64:(ho + 1) * 64], ops, st[:, 3:4], ALU.mult)
                    fw.ld(MIXD[j * 128:(j + 1) * 128, 0:256], ot)
                fw.end()

            if sstage >= 3:
                fw.begin()
                W16 = load_w(l, CC, 768)
                QTc = fw.sb([128, 2, TS], BF16)
                KTc = fw.sb([128, 2, TS + 512], BF16)
                VN = fw.sb([128, NTS, 256], BF16)
                VNs = fw.sb([128, NTS, 256], BF16)
                C16 = [fw.sb([128, 512], BF16) for _ in range(2)]
                for j in range(NTS):
                    pq, pv_ = PS[(2 * j) % 4], PS[(2 * j + 1) % 4]
                    for c in range(8):
                        fw.mm(pq, H['HTs'][:, c, j * 128:(j + 1) * 128], W16[:, c, 0:512], start=(c == 0), stop=(c == 7))
                    for c in range(8):
                        fw.mm(pv_[:, 0:256], H['HTs'][:, c, j * 128:(j + 1) * 128], W16[:, c, 512:768], start=(c == 0), stop=(c == 7))
                    NB_ = 9
                    fw.cp(act, C16[j % 2], pq)
                    if NB_ >= 2:
                        fw.cp(dve, VN[:, j, :], pv_[:, 0:256])
                    pb = PB[j % 2]
                    for c in range(4 if NB_ >= 3 else 0):
                        fw.tr(pb[:, c * 128:(c + 1) * 128], C16[j % 2][:, c * 128:(c + 1) * 128], ID16)
                    if NB_ >= 4:
                        fw.cp(act, QTc[:, :, j * 128:(j + 1) * 128], pb[:, 0:256].r("p (c t) -> p c t", c=2))
                    if NB_ >= 5:
                        fw.cp(act, KTc[:, :, j * 128:(j + 1) * 128], pb[:, 256:512].r("p (c t) -> p c t", c=2))
                NAP = 9
                for j in range(NTS - 1 if NAP >= 2 else 0):
                    ps = PS[j % 4]
                    for c in range(8):
                        fw.mm(ps[:, 0:256], H['HTs'][:, c, 64 + j * 128:64 + (j + 1) * 128], W16[:, c, 512:768], start=(c == 0), stop=(c == 7))
                    fw.cp(act if j % 2 else dve, VNs[:, j, :], ps[:, 0:256])
                CKn = fw.sb([128, 4, 4, 64], F32)
                CK16 = fw.sb([128, 4, 256], BF16)
                Vctx = fw.sb([128, 4, 256], BF16)
                for k_ in range(4 if NAP >= 3 else 0):
                    fw.ld(CKn[:, :, k_, :], cnk[l, k_].r("(i p) d -> p i d", p=128))
                fw.cp(dve, CK16, CKn.r("p i h d -> p i (h d)"))
                for c in range(2 if NAP >= 3 else 0):
                    for i in range(4):
                        fw.tr(PB[c][:, i * 128:(i + 1) * 128], CK16[:, i, c * 128:(c + 1) * 128], ID16)
                    fw.cp(act, KTc[:, c, TS:TS + 512], PB[c][:, 0:512])
                for k_ in range(4):
                    fw.ld(CKn[:, :, k_, :], cnv[l, k_].r("(i p) d -> p i d", p=128))
                fw.cp(dve, Vctx, CKn.r("p i h d -> p i (h d)"))
                NBI = fw.sb([64, 4, 512], F32)
                NBE = fw.sb([64, 4, 512], F32)
                if NAP >= 4:
                    fw.ld(NBI, nab[l, 4].r("h q k -> q h k"))
                Sx = fw.sb([64, 1024], F32)
                Px = fw.sb([64, 1024], BF16)
                PTx = fw.sb([128, 8, 64], BF16)
                stx = fw.sb([64, 4], F32)
                OTx = [fw.sb([64, 256], BF16) for _ in range(2)]
                opx = PSW[0:64, 0:64]
                for r in range(64):
                    r0 = min(max(r - 4, 0), 56)
                    dl = r - r0
                    if dl != 4:
                        fw.ld(NBE, nab[l, dl].r("h q k -> q h k"))
                    NB = NBI if dl == 4 else NBE
                    ot = OTx[r % 2]
                    for h_ in range(4):
                        pbase, c = (h_ % 2) * 64, h_ // 2
                        qv = QTc[pbase:pbase + 64, c, r * 64:(r + 1) * 64]
                        fw.mm(PS[0][0:64, :], qv, KTc[pbase:pbase + 64, c, r0 * 64:r0 * 64 + 512])
                        fw.mm(PS[1][0:64, :], qv, KTc[pbase:pbase + 64, c, TS:TS + 512])
                        fw.stt(Sx[:, 0:512], PS[0][0:64, :], 0.125, NB[:, h_, :], ALU.mult, ALU.add)
                        fw.ts(dve, Sx[:, 512:1024], PS[1][0:64, :], 0.125, ALU.mult)
                        fw.red(stx[:, 0:1], Sx, ALU.max)
                        fw.ts(dve, stx[:, 1:2], stx[:, 0:1], -1.0, ALU.mult)
                        fw.actf(Px, Sx, AF.Exp, bias=stx[:, 1:2], scale=1.0, accum=stx[:, 2:3])
                        fw.recip(stx[:, 3:4], stx[:, 2:3])
                        for blk in range(8):
                            fw.tr(PB[0][:, blk * 64:(blk + 1) * 64], Px[:, blk * 128:(blk + 1) * 128], ID16[0:64, 0:64])
                        fw.cp(act, PTx, PB[0][:, 0:512].r("p (a t) -> p a t", a=8))
                        for blk in range(4):
                            vt = VN[:, r0 // 2 + blk, h_ * 64:(h_ + 1) * 64] if r0 % 2 == 0 else VNs[:, (r0 - 1) // 2 + blk, h_ * 64:(h_ + 1) * 64]
                            fw.mm(opx, PTx[:, blk, :], vt, start=(blk == 0), stop=False)
                        for blk in range(4):
                            fw.mm(opx, PTx[:, 4 + blk, :], Vctx[:, blk, h_ * 64:(h_ + 1) * 64], start=False, stop=(blk == 3))
                        fw.ts(dve, ot[:, h_ * 64:(h_ + 1) * 64], opx, stx[:, 3:4], ALU.mult)
                    fw.ld(MIXD[r * 64:(r + 1) * 64, 512:768], ot)
                fw.end()

            if sstage >= 4:
                fw.begin()
                W16 = load_w(l, CB, 1024)
                RWP = fw.sb([128, 64], F32)
                fw.ld(RWP, rwp_d[l])
                WSt = fw.sb([128, 1024], F32)
                W2A = fw.sb([128, 2, 2, 256], BF16)
                fw.ld(WSt, rww_d[l].r("p a b c -> p (a b c)"))
                fw.cp(pool, W2A, WSt.r("p (a b c) -> p a b c", a=2, b=2))
                G2P = fw.sb([128, 256], BF16)
                fw.ld(WSt[:, 0:256], rwg_d[l])
                fw.cp(pool, G2P, WSt[:, 0:256])
                OMKA = fw.sb([128, 2], F32)
                fw.ts(dve, OMKA, RWP[:, 24:26], -1.0, ALU.mult, 1.0, ALU.add)
                GS16 = fw.sb([128, 512], BF16)
                BON = fw.sb([64, 4, 512], F32)
                XR = fw.sb([128, 2, 512], F32)
                XK = fw.sb([128, 2, 512], F32)
                XV = fw.sb([128, 2, 512], F32)
                XL = fw.sb([128, 512], F32)
                BTc = [fw.sb([128, 513], F32) for _ in range(2)]
                TW16 = fw.sb([128, 512], BF16)
                XL16 = fw.sb([128, 512], BF16)
                SG = fw.sb([128, 512], F32)
                AG = fw.sb([128, 512], F32)
                KK = fw.sb([128, 512], F32)
                SQ = fw.sb([128, 512], F32)
                RN = fw.sb([128, 512], F32)
                KE = fw.sb([128, 512], F32)
                DF = SQ
                VUt = RN[0:64, :]
                W_s = fw.sb([128, 4, 512], F32)
                B_s = fw.sb([128, 4, 512], BF16)
                K_s = fw.sb([128, 4, 512], BF16)
                A16 = fw.sb([128, 4, 512], BF16)
                R2 = fw.sb([128, 4, 512, 2], BF16)
                VTM = fw.sb([128, 4, 2, 2, 128], BF16)
                OSb = [fw.sb([64, 4, 512], F32) for _ in range(2)]
                M = [fw.sb([128, 2, 64], F32) for _ in range(2)]
                M16 = [fw.sb([128, 2, 64], BF16) for _ in range(2)]
                T1 = [fw.sb([128, 2, 64], F32) for _ in range(2)]
                SAp, Vp, Op = scan_res()
                fw.memset(pool, R2, 0.0)
                S0t = fw.sb([64, 2, 64], F32)
                for d in range(2):
                    for j in range(2):
                        fw.ld(S0t, V(srw.ap[l, d, 2 * j:2 * j + 2].rearrange("m v k -> v m k"), srw.res))
                        fw.tr(PS[0][:, 0:64], S0t.r("v m k -> v (m k)"), IDF[0:64, 0:64])
                        fw.cp(act, M[d][:, j, :], PS[0][:, 0:64])
                    fw.cp(act, M16[d], M[d])
                for kb in range(8):
                    for d in range(2):
                        tok0 = (kb if d == 0 else 7 - kb) * 512
                        if d == 0:
                            ps = PS[0]
                            proj_s(ps, W16, 896, 128, tok0)
                            fw.actf(GS16, ps, AF.Sigmoid)
                            for hh in range(4):
                                fw.mm(PS[2][0:64, :], G2P[:, hh * 64:(hh + 1) * 64], GS16)
                                fw.cp(act, BON[:, hh, :], PS[2][0:64, :])
                            fw.ld(GZD[:, :, tok0:tok0 + 512], BON)
                        dsts = [XR[:, 0, :], XR[:, 1, :], XK[:, 0, :], XK[:, 1, :], XV[:, 0, :], XV[:, 1, :], XL]
                        th = tok0 - 1 if d == 0 else tok0 + 512
                        for c in range(7):
                            ps = PS[c % 2]
                            bt = BTc[c % 2]
                            proj_s(ps, W16, c * 128, 128, tok0)
                            main = bt[:, 1:513] if d == 0 else bt[:, 0:512]
                            halo = bt[:, 0:1] if d == 0 else bt[:, 512:513]
                            fw.cp(act, main, ps)
                            if 0 <= th < TS:
                                proj_s(PS[2], W16, c * 128, 128, th, 1)
                                fw.cp(act, halo, PS[2][:, 0:1])
                            else:
                                fw.memset(dve, halo, 0.0)
                            if d == 0:
                                fw.tt(dve, DF, bt[:, 0:512], bt[:, 1:513], ALU.subtract)
                            else:
                                fw.tt(dve, DF, bt[:, 1:513], bt[:, 0:512], ALU.subtract)
                            fw.stt(dsts[c], DF, RWP[:, d * 7 + c:d * 7 + c + 1], main, ALU.mult, ALU.add)
                        fw.actf(TW16, XL, AF.Tanh)
                        fw.cp(act, XL16, XL)
                        for j in range(2):
                            g = d * 2 + j
                            fw.mm(PS[2], W2A[:, d, 0, j * 128:(j + 1) * 128], TW16)
                            fw.actf(SG, PS[2], AF.Sigmoid, bias=RWP[:, 14 + d * 2 + j:15 + d * 2 + j])
                            fw.actf(W_s[:, g, :], SG, AF.Exp, scale=-math.exp(-0.5))
                            fw.mm(PS[3], W2A[:, d, 1, j * 128:(j + 1) * 128], XL16)
                            fw.actf(AG, PS[3], AF.Sigmoid, bias=RWP[:, 18 + d * 2 + j:19 + d * 2 + j])
                            fw.ts(dve, KK, XK[:, j, :], RWP[:, 22 + j:23 + j], ALU.mult)
                            fw.actf(SQ, KK, AF.Square)
                            fw.mm(PS[2], BONES, SQ)
                            fw.actf(RN, PS[2], AF.Sqrt, bias=EPSC[:, 0:1])
                            fw.recip(RN, RN)
                            fw.tt(dve, KK, KK, RN, ALU.mult)
                            fw.ts(dve, A16[:, g, :], KK, -1.0, ALU.mult)
                            fw.tt(dve, B_s[:, g, :], KK, AG, ALU.mult)
                            fw.ts(dve, SQ, AG, RWP[:, 24 + j:25 + j], ALU.mult, OMKA[:, j:j + 1], ALU.add)
                            fw.tt(dve, KE, XK[:, j, :], SQ, ALU.mult)
                            fw.cp(act, K_s[:, g, :], KE)
                            fw.cp(act, R2[0:64, g, :, 0], XR[0:64, j, :])
                            fw.cp(act, R2[64:128, g, :, 1], XR[64:128, j, :])
                            fw.stt(KE, XR[:, j, :], RWP[:, 28 + j:29 + j], KE, ALU.mult, ALU.mult)
                            for m in range(2):
                                hh = 2 * j + m
                                fw.mm(PS[2][0:64, :], BONES[:, m * 64:(m + 1) * 64], KE)
                                fw.mm(PS[3][0:64, :], IDF[:, m * 64:(m + 1) * 64], XV[:, j, :])
                                fw.cp(act, VUt, PS[3][0:64, :])
                                fw.tt(dve, BON[:, hh, :], PS[2][0:64, :], VUt, ALU.mult)
                        fw.ld(BOND[d][:, :, tok0:tok0 + 512], BON)
                        for tt_ in range(4):
                            ps = PS[tt_ % 2]
                            for j in range(2):
                                fw.tr(ps[:, j * 128:(j + 1) * 128], XV[:, j, tt_ * 128:(tt_ + 1) * 128], IDF)
                            fw.cp(act, VTM[:, tt_, d, :, :].r("p m (j v) -> p j m v", j=2), ps[:, 0:256].r("p (j m v) -> p j m v", j=2, m=2))
                    scan_s(W_s, B_s, K_s, A16, R2, VTM, OSb, M, M16, T1, SAp, Vp, Op)
                    for d in range(2):
                        tok0 = (kb if d == 0 else 7 - kb) * 512
                        fw.ld(ODD[d][:, :, tok0:tok0 + 512], OSb[d])
                fw.end()
                fw.begin()
                RWU = fw.sb([64, 32], F32)
                fw.ld(RWU, rwu_d[l])
                OF = fw.sb([64, 4, 512], F32)
                OB = fw.sb([64, 4, 512], F32)
                B0 = fw.sb([64, 4, 512], F32)
                B1 = fw.sb([64, 4, 512], F32)
                GZ = fw.sb([64, 4, 512], F32)
                Y = fw.sb([64, 4, 512], BF16)
                SQ = fw.sb([64, 512], F32)
                RN = fw.sb([64, 512], F32)
                ONES = CON[0:64, 256:320]
                for ch in range(8):
                    tok0 = ch * 512
                    fw.ld(OF, ODD[0][:, :, tok0:tok0 + 512])
                    fw.ld(OB, ODD[1][:, :, tok0:tok0 + 512])
                    fw.ld(B0, BOND[0][:, :, tok0:tok0 + 512])
                    fw.ld(B1, BOND[1][:, :, tok0:tok0 + 512])
                    fw.ld(GZ, GZD[:, :, tok0:tok0 + 512])
                    fw.tt(pool, OF, OF, OB, ALU.add)
                    fw.tt(pool, B0, B0, B1, ALU.add)
                    for hh in range(4):
                        fw.mm(PS[0][0:64, :], ONES, OF[:, hh, :])
                        fw.tt(dve, SQ, OF[:, hh, :], PS[0][0:64, :], ALU.subtract)
                        fw.actf(RN, SQ, AF.Square)
                        fw.mm(PS[1][0:64, :], ONES, RN)
                        fw.actf(RN, PS[1][0:64, :], AF.Sqrt, bias=EPSC[0:64, 2:3])
                        fw.recip(RN, RN)
                        fw.tt(dve, SQ, SQ, RN, ALU.mult)
                        fw.ts(dve, SQ, SQ, RWU[:, 8 + hh:9 + hh], ALU.mult, RWU[:, 12 + hh:13 + hh], ALU.add)
                        fw.tt(dve, SQ, SQ, B0[:, hh, :], ALU.add)
                        fw.tt(dve, Y[:, hh, :], SQ, GZ[:, hh, :], ALU.mult)
                    y_to_mixd(Y, tok0, 256)
                fw.end()

            if sstage >= 5:
                fw.begin()
                W16 = load_w(l, CD, 1152)
                DNP = fw.sb([128, 48], F32)
                fw.ld(DNP, dnp_d[l])
                EXPM = fw.sb([128, 2, 4, 128], F32)
                fw.ld(EXPM, expm_d)
                CONVW = DNP[:, 0:30].r("p (c j) -> p c j", j=5)
                NAe = fw.sb([128, 1], F32)
                fw.actf(NAe, DNP[:, 31:32], AF.Exp)
                fw.ts(dve, NAe, NAe, -1.0, ALU.mult)
                BT1 = [fw.sb([128, 516], F32) for _ in range(2)]
                CV = fw.sb([128, 6, 512], F32)
                SQ = fw.sb([128, 512], F32)
                RN = fw.sb([128, 512], F32)
                KB = fw.sb([128, 512], F32)
                SB_ = fw.sb([128, 512], F32)
                GG = fw.sb([128, 512], F32)
                W_s = fw.sb([128, 4, 512], F32)
                B_s = fw.sb([128, 4, 512], BF16)
                K_s = fw.sb([128, 4, 512], BF16)
                A16 = fw.sb([128, 4, 512], BF16)
                R2 = fw.sb([128, 4, 512, 2], BF16)
                VTM = fw.sb([128, 4, 2, 2, 128], BF16)
                ZU = fw.sb([64, 4, 512], F32)
                OSb = [fw.sb([64, 4, 512], F32) for _ in range(2)]
                M = [fw.sb([128, 2, 64], F32) for _ in range(2)]
                M16 = [fw.sb([128, 2, 64], BF16) for _ in range(2)]
                T1 = [fw.sb([128, 2, 64], F32) for _ in range(2)]
                SAp, Vp, Op = scan_res()
                fw.memset(pool, R2, 0.0)
                for d in range(2):
                    for j in range(2):
                        fw.ld(M[d][:, j, :], V(sdn.ap[l, d, 2 * j:2 * j + 2].rearrange("m k v -> (m k) v"), sdn.res))
                    fw.cp(act, M16[d], M[d])
                for kb in range(8):
                    for d in range(2):
                        tok0 = (kb if d == 0 else 7 - kb) * 512
                        for ch in range(6):
                            ps = PS[ch % 2]
                            bt = BT1[ch % 2]
                            proj_s(ps, W16, ch * 128, 128, tok0)
                            fw.cp(act, bt[:, 2:514], ps)
                            if tok0 > 0:
                                proj_s(PS[2], W16, ch * 128, 128, tok0 - 2, 2)
                                fw.cp(act, bt[:, 0:2], PS[2][:, 0:2])
                            else:
                                fw.memset(dve, bt[:, 0:2], 0.0)
                            if tok0 + 512 < TS:
                                proj_s(PS[3], W16, ch * 128, 128, tok0 + 512, 2)
                                fw.cp(act, bt[:, 514:516], PS[3][:, 0:2])
                            else:
                                fw.memset(dve, bt[:, 514:516], 0.0)
                            fw.ts(dve, CV[:, ch, :], bt[:, 0:512], CONVW[:, ch, 0:1], ALU.mult)
                            for jj in range(1, 5):
                                fw.stt(CV[:, ch, :], bt[:, jj:jj + 512], CONVW[:, ch, jj:jj + 1], CV[:, ch, :], ALU.mult, ALU.add)
                            fw.actf(CV[:, ch, :], CV[:, ch, :], AF.Silu)
                        for ch in range(4):
                            fw.actf(SQ, CV[:, ch, :], AF.Square)
                            ps = PS[ch % 2]
                            fw.mm(ps, BONES, SQ)
                            fw.actf(RN, ps, AF.Sqrt, bias=EPSC[:, 0:1])
                            fw.recip(RN, RN)
                            fw.tt(dve, CV[:, ch, :], CV[:, ch, :], RN, ALU.mult)
                        ps = PS[0]
                        proj_s(ps, W16, 1024, 128, tok0)
                        fw.actf(SB_, ps, AF.Sigmoid)
                        fw.actf(GG, ps, AF.Exp, bias=DNP[:, 30:31])
                        fw.actf(GG, GG, AF.Ln, bias=EPSC[:, 1:2])
                        fw.ts(dve, GG, GG, NAe[:, 0:1], ALU.mult)
                        for j in range(2):
                            g = d * 2 + j
                            fw.ts(dve, R2[0:64, g, :, 0], CV[0:64, j, :], 0.125, ALU.mult)
                            fw.ts(dve, R2[64:128, g, :, 1], CV[64:128, j, :], 0.125, ALU.mult)
                            fw.cp(act, A16[:, g, :], CV[:, 2 + j, :])
                            fw.mm(PS[2], EXPM[:, 0, g, :], SB_)
                            fw.mm(PS[3], EXPM[:, 1, g, :], GG)
                            fw.actf(W_s[:, g, :], PS[3], AF.Exp)
                            fw.tt(dve, KB, CV[:, 2 + j, :], PS[2], ALU.mult)
                            fw.cp(act, K_s[:, g, :], KB)
                            fw.stt(B_s[:, g, :], KB, -1.0, W_s[:, g, :], ALU.mult, ALU.mult)
                        for tt_ in range(4):
                            ps = PS[tt_ % 2]
                            for j in range(2):
                                fw.tr(ps[:, j * 128:(j + 1) * 128], CV[:, 4 + j, tt_ * 128:(tt_ + 1) * 128], IDF)
                            fw.cp(act, VTM[:, tt_, d, :, :].r("p m (j v) -> p j m v", j=2), ps[:, 0:256].r("p (j m v) -> p j m v", j=2, m=2))
                        if d == 0:
                            for hh in range(4):
                                ps = PS[2 + hh % 2]
                                proj_s(ps, W16, 768 + hh * 64, 64, tok0)
                                fw.actf(ZU[:, hh, :], ps[0:64, :], AF.Silu)
                            fw.ld(GZD[:, :, tok0:tok0 + 512], ZU)
                    scan_s(W_s, B_s, K_s, A16, R2, VTM, OSb, M, M16, T1, SAp, Vp, Op)
                    for d in range(2):
                        tok0 = (kb if d == 0 else 7 - kb) * 512
                        fw.ld(ODD[d][:, :, tok0:tok0 + 512], OSb[d])
                fw.end()
                fw.begin()
                DNP = fw.sb([128, 48], F32)
                fw.ld(DNP, dnp_d[l])
                OF = fw.sb([64, 4, 512], F32)
                OB = fw.sb([64, 4, 512], F32)
                GZ = fw.sb([64, 4, 512], F32)
                Y = fw.sb([64, 4, 512], BF16)
                SQ = fw.sb([64, 512], F32)
                RN = fw.sb([64, 512], F32)
                ONES = CON[0:64, 256:320]
                for ch in range(8):
                    tok0 = ch * 512
                    fw.ld(OF, ODD[0][:, :, tok0:tok0 + 512])
                    fw.ld(OB, ODD[1][:, :, tok0:tok0 + 512])
                    fw.ld(GZ, GZD[:, :, tok0:tok0 + 512])
                    fw.tt(pool, OF, OF, OB, ALU.add)
                    for hh in range(4):
                        fw.actf(SQ, OF[:, hh, :], AF.Square)
                        fw.mm(PS[hh % 2][0:64, :], ONES, SQ)
                        fw.actf(RN, PS[hh % 2][0:64, :], AF.Sqrt, bias=EPSC[0:64, 0:1])
                        fw.recip(RN, RN)
                        fw.tt(dve, RN, RN, OF[:, hh, :], ALU.mult)
                        fw.stt(Y[:, hh, :], RN, DNP[0:64, 32:33], GZ[:, hh, :], ALU.mult, ALU.mult)
                    y_to_mixd(Y, tok0, 768)
                fw.end()

            fw.begin()
            WOn = fw.sb([128, 8, D], BF16)
            WS = [fw.sb([128, D], F32) for _ in range(2)]
            for c in range(8):
                fw.ld(WS[c % 2], woutn[l, :, c, :])
                fw.cp(pool, WOn[:, c, :], WS[c % 2])
            MT = [fw.sb([128, D], BF16) for _ in range(2)]
            MF = [fw.sb([128, 8, 128], BF16) for _ in range(2)]
            XT = [fw.sb([128, D], F32) for _ in range(2)]
            TMP = fw.sb([128, 512], F32)
            for j in range(NTS):
                k = j % 2
                fw.ld(MT[k], MIXD[j * 128:(j + 1) * 128, :])
                fw.ld(XT[k], xsrc[j * 128:(j + 1) * 128, :])
                pb = PB[k]
                for c in range(8):
                    fw.tr(pb[:, c * 128:(c + 1) * 128], MT[k][:, c * 128:(c + 1) * 128], ID16)
                fw.cp(act, MF[k], pb.r("p (c t) -> p c t", c=8))
                for n in range(2):
                    ps = PS[(2 * j + n) % 4]
                    for c in range(8):
                        fw.mm(ps, MF[k][:, c, :], WOn[:, c, n * 512:(n + 1) * 512], start=(c == 0), stop=(c == 7))
                    fw.tt(dve, TMP, ps, H['G1s'][:, n * 512:(n + 1) * 512], ALU.mult)
                    fw.tt(pool, XT[k][:, n * 512:(n + 1) * 512], XT[k][:, n * 512:(n + 1) * 512], TMP, ALU.add)
                fw.ld(XS[j * 128:(j + 1) * 128, :], XT[k])
            fw.end()
            fw.sect.close()
            fw.sect = None
            if sstage <= 5:
                continue

            ring = {}

            def get_x(j):
                if "xt" not in ring:
                    ring["xt"] = [fw.sb([128, D], F32) for _ in range(2)]
                xt = ring["xt"][j % 2]
                fw.ld(xt, XS[j * 128:(j + 1) * 128, :])
                return xt

            def put_x(j, xt):
                fw.ld(XS[j * 128:(j + 1) * 128, :], xt)

            peer_phase(l, NTS, CSs, get_x, put_x)

        fw.begin()
        FN = fw.sb([128, D], F32)
        fw.ld(FN, V(fng.ap[0:1, :].partition_broadcast(128), fng.res))
        JK = fw.sb([128, D], F32)
        XT = [fw.sb([128, D], F32) for _ in range(2)]
        YT = [fw.sb([128, D], F32) for _ in range(2)]
        for j in range(NTS):
            k = j % 16
            fw.ld(XT[j % 2], XS[j * 128:(j + 1) * 128, :])
            fw.actf(JK, XT[j % 2], AF.Square, accum=SS[:, k:k + 1])
            fw.actf(RSTD[:, k:k + 1], SS[:, k:k + 1], AF.Sqrt, bias=EPSC[:, 0:1], scale=1.0 / D)
            fw.recip(RSTD[:, k:k + 1], RSTD[:, k:k + 1])
            fw.stt(YT[j % 2], XT[j % 2], RSTD[:, k:k + 1], FN, ALU.mult, ALU.mult)
            fw.ld(y_sample[j * 128:(j + 1) * 128, :], YT[j % 2])
        if dbg:
            for j in range(NTS):
                fw.ld(XT[j % 2], XS[j * 128:(j + 1) * 128, :])
                fw.ld(dbgo["xs"][j * 128:(j + 1) * 128, :], XT[j % 2])
        fw.end()

    if sstage >= 1:
        sample_section(sstage)
    fw.begin()
    for o in outs:
        if o.res.lw is not None:
            fw._wait(fw.sp, o.res.lw)
    fw.end()
    fw.close()
    return fw.nc, fw


def host_inputs(inputs, core, peer=True):
    f = lambda a: np.ascontiguousarray(a, dtype=np.float32)
    L = DEPTH
    m = {}
    m["xp"] = f(inputs["x_prompt"][4 * core:4 * core + 4].reshape(NTOK, D))
    m["cctx"] = f(inputs["c_ctx"].reshape(8, 128).T)
    m["wmod"] = f(inputs["w_mod"].reshape(L, 8, 128, 6 * D).transpose(0, 2, 1, 3))
    m["bmod"] = f(inputs["b_mod"])
    m["n1g"] = f(inputs["norm1_g"])
    m["n2g"] = f(inputs["norm2_g"])
    m["fng"] = f(inputs["final_norm_g"].reshape(1, D))
    w = inputs["w_in"]
    cols = []
    for h in (0, 2, 1, 3):
        cols += list(range(h * 64, h * 64 + 64))
    cols += list(range(256, 512))
    cols += list(range(1472, 2240))
    wperm = w[:, :, cols]
    wfull = np.zeros((L, D, NWC), np.float32)
    wfull[:, :, 0:1280] = wperm
    wfull[:, :, CB:CB + 768] = w[:, :, 512:1280]
    wfull[:, :, CB + 768:CB + 896] = w[:, :, 1280:1408]
    wfull[:, :, CB + 896:CB + 960] = w[:, :, 1408:1472]
    wfull[:, :, CD:CD + 768] = w[:, :, 2240:3008]
    wfull[:, :, CD + 768:CD + 1024] = w[:, :, 3024:3280]
    wfull[:, :, CD + 1024:CD + 1040] = w[:, :, 3008:3024]
    m["win"] = f(wfull.reshape(L, 8, 128, NWC).transpose(0, 2, 1, 3))
    wo = inputs["w_out"]
    rows = []
    for h in (0, 2, 1, 3):
        rows += list(range(h * 64, h * 64 + 64))
    rows += list(range(256, 1024))
    m["wout"] = f(wo[:, rows, :].reshape(L, 16, 64, D).transpose(0, 2, 1, 3))
    m["gqn"] = f(np.stack([inputs["gqa_q_norm"], inputs["gqa_k_norm"]], axis=1))
    con = np.zeros((128, 1024), np.float32)
    con[:, 0:128] = np.eye(128)
    p = np.arange(128)
    con[:, 128:256] = (p[:, None] // 64 == p[None, :] // 64)
    con[0:64, 256:320] = 1.0 / 64
    con[:, 320 + 128] = 1.0
    con[:, 576:592] = np.arange(16)[None, :]
    m["consts"] = con
    rwp = np.zeros((L, 128, 64), np.float32)
    mu = inputs["rw_mu"]
    st2 = lambda v: v.reshape(v.shape[:-1] + (2, 128)).swapaxes(-1, -2)
    for d in range(2):
        rwp[:, :, d * 7:d * 7 + 6] = mu[:, d, 0:768].reshape(L, 6, 128).transpose(0, 2, 1)
        rwp[:, d * 32:d * 32 + 32, d * 7 + 6] = mu[:, d, 768:800]
        rwp[:, 64 + d * 32:96 + d * 32, d * 7 + 6] = mu[:, d, 800:832]
        rwp[:, :, 14 + d * 2:16 + d * 2] = st2(inputs["rw_w0"][:, d])
        rwp[:, :, 18 + d * 2:20 + d * 2] = st2(inputs["rw_a0"][:, d])
    rwp[:, :, 22:24] = st2(inputs["rw_k_k"])
    rwp[:, :, 24:26] = st2(inputs["rw_k_a"])
    rwp[:, :, 28:30] = st2(inputs["rw_r_k"].reshape(L, 256))
    m["rwp"] = rwp
    rwu = np.zeros((L, 64, 32), np.float32)
    rwu[:, :, 8:12] = inputs["rw_ln_g"].reshape(L, 4, 64).transpose(0, 2, 1)
    rwu[:, :, 12:16] = inputs["rw_ln_b"].reshape(L, 4, 64).transpose(0, 2, 1)
    m["rwu"] = rwu
    rww = np.zeros((L, 128, 2, 2, 256), np.float32)
    for d in range(2):
        rww[:, d * 32:d * 32 + 32, d, 0, :] = inputs["rw_w2"][:, d]
        rww[:, 64 + d * 32:96 + d * 32, d, 1, :] = inputs["rw_a2"][:, d]
    m["rww"] = rww
    rwg = np.zeros((L, 128, 256), np.float32)
    rwg[:, 0:64] = inputs["rw_g2"]
    m["rwg"] = rwg
    m["xs_in"] = f(inputs["x_sample"][core])
    m["cs_in"] = f(inputs["c"][core].reshape(8, 128).T)
    m["cgk"] = f(inputs["cache_gqa_k"][core])
    m["cgv"] = f(inputs["cache_gqa_v"][core])
    m["cnk"] = f(inputs["cache_na_k"][core])
    m["cnv"] = f(inputs["cache_na_v"][core])
    m["srw"] = f(inputs["state_rwkv"][core])
    m["sdn"] = f(inputs["state_delta"][core])
    m["rcos"], m["rsin"] = _rope_tables()
    m["nab"] = _na_bias_table(inputs["na_bias"])
    m["woutn"] = f(inputs["w_out"].reshape(L, 8, 128, D).transpose(0, 2, 1, 3))
    dnp = np.zeros((L, 128, 48), np.float32)
    cw = inputs["dn_conv"]
    dnp[:, :, 0:30] = cw.reshape(L, 5, 6, 128).transpose(0, 3, 2, 1).reshape(L, 128, 30)
    dnp[:, 8:16, 30] = inputs["dn_dt_bias"].reshape(L, 8)
    dnp[:, 8:16, 31] = inputs["dn_a_log"].reshape(L, 8)
    dnp[:, 0:64, 32] = inputs["dn_norm_g"]
    m["dnp"] = dnp
    ex = np.zeros((128, 2, 4, 128), np.float32)
    for d in range(2):
        for j in range(2):
            for mm_ in range(2):
                ex[d * 4 + 2 * j + mm_, 0, d * 2 + j, mm_ * 64:(mm_ + 1) * 64] = 1.0
                ex[8 + d * 4 + 2 * j + mm_, 1, d * 2 + j, mm_ * 64:(mm_ + 1) * 64] = 1.0
    m["expm"] = ex
    if not peer:
        return m
    m["wq"] = f(inputs["peer_wq"].reshape(L, 8, 128, 2048).transpose(0, 2, 1, 3))
    m["keyt"] = f(inputs["peer_keys"].reshape(L, 16, 128, 128).transpose(0, 3, 1, 2))
    for i in range(L):
        m[f"pu{i}"] = f(inputs["peer_u"][i])
        m[f"pv{i}"] = f(inputs["peer_v"][i])
    return m


def _rope_tables():
    t = np.arange(4096)
    row, col = t // 64, t % 64
    inv = 10000.0 ** (-2.0 * np.arange(16) / 32)
    ang = np.concatenate([row[:, None] * inv[None], col[:, None] * inv[None]], axis=1)
    return np.cos(ang).astype(np.float32), np.sin(ang).astype(np.float32)


def _na_bias_table(na_bias):
    L = na_bias.shape[0]
    qc = np.arange(64)
    kc = np.arange(64)
    cstart = np.clip(qc - 8, 0, 48)
    valid = (kc[None, :] >= cstart[:, None]) & (kc[None, :] < cstart[:, None] + 16)
    dc = np.clip(kc[None, :] - qc[:, None], -15, 15) + 15
    out = np.empty((L, 8, 4, 64, 8, 64), np.float32)
    for dl in range(8):
        for w in range(8):
            dr = w + 7 - dl
            g = na_bias[:, :, dr, :][:, :, dc]
            out[:, dl, :, :, w, :] = np.where(valid[None, None], g, np.float32(-1e30))
    return out.reshape(L, 8, 4, 64, 512)


_CACHE = {}


def kernel(**inputs):
    inputs = {k: np.asarray(v) for k, v in inputs.items()}
    if "nc" not in _CACHE:
        _CACHE["nc"] = build()[0]
    nc = _CACHE["nc"]
    in_maps = [host_inputs(inputs, c) for c in range(NCORES)]
    res = run_bass_kernel_spmd(nc, in_maps, core_ids=list(range(NCORES)))
    R = res.results
    cat = lambda k: np.concatenate([np.asarray(r[k]) for r in R], axis=0)
    y_prompt = cat("y_prompt").reshape(32, T, D)
    y_sample = np.stack([np.asarray(r["y_sample"]) for r in R], axis=0)
    return (y_prompt, y_sample, cat("o_gk"), cat("o_gv"), cat("o_nk"), cat("o_nv"), cat("o_rw"), cat("o_dn"))
```

```python
import math
import numpy as np
from contextlib import ExitStack
import concourse.bass as bass
import concourse.mybir as mybir
from concourse.alu_op_type import AluOpType as ALU
from concourse.bass_utils import run_bass_kernel_spmd

F32 = mybir.dt.float32
BF16 = mybir.dt.bfloat16
I32 = mybir.dt.int32
U32 = mybir.dt.uint32
AF = mybir.ActivationFunctionType
AX = mybir.AxisListType

NCORES = 8
DEPTH = 2
D = 1024
NT = 8
NTOK = 1024
T = 256
EPS = 1e-6
RW_LN_EPS = 64e-5
CA, CC, CB, CD = 0, 512, 1280, 2304
NWC = 3456


class Res:
    __slots__ = ("lw", "rd")

    def __init__(self):
        self.lw = None
        self.rd = []


class V:
    __slots__ = ("ap", "res")

    def __init__(self, ap, res):
        self.ap = ap
        self.res = res

    def __getitem__(self, k):
        return V(self.ap[k], self.res)

    def r(self, pat, **kw):
        return V(self.ap.rearrange(pat, **kw), self.res)

    def bc(self, shape):
        return V(self.ap.broadcast_to(list(shape)), self.res)

    def us(self, ax):
        return V(self.ap.unsqueeze(ax), self.res)


class Eng:
    def __init__(self, name, getter, same_sync):
        self.name = name
        self.getter = getter
        self.ops = []
        self.count = 0
        self.sem = None
        self.seen = {}
        self.same_sync = same_sync


class FW:
    NDMA = 12

    def __init__(self):
        self.nc = bass.Bass("TRN2", target_bir_lowering=False)
        self.es = ExitStack()
        nc = self.nc
        self.pe = Eng("pe", lambda: nc.tensor, False)
        self.act = Eng("act", lambda: nc.scalar, True)
        self.dve = Eng("dve", lambda: nc.vector, True)
        self.pool = Eng("pool", lambda: nc.gpsimd, True)
        self.sp = Eng("sp", lambda: nc.sync, False)
        self.engs = [self.pe, self.act, self.dve, self.pool, self.sp]
        for e in self.engs:
            e.sem = self.es.enter_context(nc.semaphore("s_" + e.name))
        self.dsem = {}
        self.dcnt = {}
        for e in (self.sp, self.pool):
            self.dsem[e.name] = [self.es.enter_context(nc.semaphore(f"d_{e.name}{i}")) for i in range(self.NDMA)]
            self.dcnt[e.name] = 0
        self.ninst = 0
        self.pes = None
        self.sect = None
        self.uid = 0

    def _stack(self, glob):
        if glob:
            return self.sect if self.sect is not None else self.es
        return self.es if self.pes is None else self.pes

    def sb(self, shape, dt, glob=False, name=None):
        self.uid += 1
        t = self._stack(glob).enter_context(self.nc.sbuf_tensor(name or f"t{self.uid}", list(shape), dt))
        return V(t[:], Res())

    def psum(self, shape, dt, name=None):
        self.uid += 1
        t = self.es.enter_context(self.nc.psum_tensor(name or f"p{self.uid}", list(shape), dt))
        return V(t[:], Res())

    def dram(self, name, shape, dt, kind):
        t = self.nc.dram_tensor(name, list(shape), dt, kind=kind)
        return V(t.ap(), Res())

    def _wait(self, E, tok):
        kind, a, b = tok
        if kind == 'e':
            if a is E and not E.same_sync:
                return
            key = a.name
            sem = a.sem
        else:
            key = id(a)
            sem = a
        if E.seen.get(key, 0) >= b:
            return
        E.seen[key] = b
        E.ops.append(('w', sem, b))

    def _deps(self, E, reads, writes):
        for r in reads:
            if r.lw is not None:
                self._wait(E, r.lw)
        for w in writes:
            if w.lw is not None:
                self._wait(E, w.lw)
            for tok in w.rd:
                self._wait(E, tok)

    def _commit(self, tok, reads, writes):
        for r in reads:
            r.rd.append(tok)
            if len(r.rd) > 48:
                best = {}
                for t in r.rd:
                    k = t[1].name if t[0] == 'e' else id(t[1])
                    if k not in best or best[k][2] < t[2]:
                        best[k] = t
                r.rd = list(best.values())
        for w in writes:
            w.lw = tok
            w.rd = []

    def op(self, E, fn, reads, writes):
        reads = [v.res for v in reads]
        writes = [v.res for v in writes]
        self._deps(E, reads, writes)
        E.count += 1
        E.ops.append(('i', fn, E.sem, 1))
        self._commit(('e', E, E.count), reads, writes)
        self.ninst += 1

    def dma(self, E, fn, reads, writes):
        reads = [v.res for v in reads]
        writes = [v.res for v in writes]
        self._deps(E, reads, writes)
        k = self.dcnt[E.name]
        self.dcnt[E.name] = k + 1
        sem = self.dsem[E.name][k % self.NDMA]
        rnd = k // self.NDMA
        if rnd > 0:
            self._wait(E, ('d', sem, 16 * rnd))
        E.ops.append(('i', fn, sem, 16))
        self._commit(('d', sem, 16 * (rnd + 1)), reads, writes)
        self.ninst += 1

    def begin(self):
        self.pes = ExitStack()

    def _barrier(self):
        for E in self.engs:
            for X in self.engs:
                if X is not E and X.count > 0:
                    self._wait(E, ('e', X, X.count))
            for nm, sems in self.dsem.items():
                k = self.dcnt[nm]
                for i, s in enumerate(sems):
                    n = (k - i + self.NDMA - 1) // self.NDMA if k > i else 0
                    if n > 0:
                        self._wait(E, ('d', s, 16 * n))

    def end(self, barrier=True):
        if barrier:
            self._barrier()
        nc = self.nc

        def replay(E, h):
            for o in E.ops:
                if o[0] == 'w':
                    h.wait_ge(o[1], o[2])
                else:
                    o[1](h).then_inc(o[2], o[3])
            E.ops = []

        with nc.Block() as block:
            @block.tensor
            def _(h):
                replay(self.pe, h)

            @block.scalar
            def _(h):
                replay(self.act, h)

            @block.vector
            def _(h):
                replay(self.dve, h)

            @block.gpsimd
            def _(h):
                replay(self.pool, h)

            @block.sync
            def _(h):
                replay(self.sp, h)
        if self.pes is not None:
            self.pes.close()
            self.pes = None

    def close(self):
        self.es.close()

    def mm(self, out, lhsT, rhs, start=True, stop=True):
        self.op(self.pe, lambda h: h.matmul(out.ap, lhsT.ap, rhs.ap, start=start, stop=stop), [lhsT, rhs], [out])

    def tr(self, out, in_, ident):
        self.op(self.pe, lambda h: h.transpose(out.ap, in_.ap, ident.ap), [in_, ident], [out])

    def tt(self, E, out, a, b, op):
        self.op(E, lambda h: h.tensor_tensor(out=out.ap, in0=a.ap, in1=b.ap, op=op), [a, b], [out])

    def ts(self, E, out, a, s1, op0, s2=None, op1=None):
        rd = [a] + [s for s in (s1, s2) if isinstance(s, V)]
        g = lambda s: s.ap if isinstance(s, V) else s
        if op1 is None:
            self.op(E, lambda h: h.tensor_scalar(out=out.ap, in0=a.ap, scalar1=g(s1), scalar2=None, op0=op0), rd, [out])
        else:
            self.op(E, lambda h: h.tensor_scalar(out=out.ap, in0=a.ap, scalar1=g(s1), scalar2=g(s2), op0=op0, op1=op1), rd, [out])

    def stt(self, out, a, s, b, op0, op1):
        rd = [a, b] + ([s] if isinstance(s, V) else [])
        sv = s.ap if isinstance(s, V) else s
        self.op(self.dve, lambda h: h.scalar_tensor_tensor(out=out.ap, in0=a.ap, scalar=sv, in1=b.ap, op0=op0, op1=op1), rd, [out])

    def actf(self, out, in_, func, bias=None, scale=None, accum=None):
        rd = [in_] + ([bias] if isinstance(bias, V) else [])
        wr = [out] + ([accum] if accum is not None else [])
        kw = {}
        if bias is not None:
            kw["bias"] = bias.ap if isinstance(bias, V) else bias
        if scale is not None:
            kw["scale"] = scale
        if accum is not None:
            kw["accum_out"] = accum.ap
        self.op(self.act, lambda h: h.activation(out=out.ap, in_=in_.ap, func=func, **kw), rd, wr)

    def cp(self, E, out, in_):
        if E is self.act:
            self.op(E, lambda h: h.copy(out=out.ap, in_=in_.ap), [in_], [out])
        else:
            self.op(E, lambda h: h.tensor_copy(out=out.ap, in_=in_.ap), [in_], [out])

    def red(self, out, in_, op, axis=AX.X):
        self.op(self.dve, lambda h: h.tensor_reduce(out=out.ap, in_=in_.ap, axis=axis, op=op), [in_], [out])

    def recip(self, out, in_):
        self.op(self.dve, lambda h: h.reciprocal(out=out.ap, in_=in_.ap), [in_], [out])

    def memset(self, E, out, val):
        self.op(E, lambda h: h.memset(out.ap, val), [], [out])

    def ld(self, out, in_, E=None):
        E = E or self.sp
        self.dma(E, lambda h: h.dma_start(out=out.ap, in_=in_.ap), [in_], [out])


def build(stage=99, dbg=False, nl=DEPTH, do_prompt=True, sstage=99):
    fw = FW()
    sp, pe, act, dve, pool = fw.sp, fw.pe, fw.act, fw.dve, fw.pool
    L = DEPTH
    din = {}

    def inp(name, shape, dt=F32):
        din[name] = fw.dram(name, shape, dt, "ExternalInput")
        return din[name]

    xp = inp("xp", [NTOK, D])
    cctx = inp("cctx", [128, 8])
    wmod = inp("wmod", [L, 128, 8, 6 * D])
    bmod = inp("bmod", [L, 6 * D])
    n1g = inp("n1g", [L, D])
    n2g = inp("n2g", [L, D])
    fng = inp("fng", [1, D])
    win = inp("win", [L, 128, 8, NWC])
    wout = inp("wout", [L, 64, 16, D])
    gqn = inp("gqn", [L, 2, 64])
    consts = inp("consts", [128, 1024])
    rwp_d = inp("rwp", [L, 128, 64])
    rwu_d = inp("rwu", [L, 64, 32])
    rww_d = inp("rww", [L, 128, 2, 2, 256])
    rwg_d = inp("rwg", [L, 128, 256])
    dnp_d = inp("dnp", [L, 128, 48])
    expm_d = inp("expm", [128, 2, 4, 128])
    xs_in = inp("xs_in", [4096, D])
    cs_in = inp("cs_in", [128, 8])
    cgk = inp("cgk", [L, 2, 512, 64])
    cgv = inp("cgv", [L, 2, 512, 64])
    cnk = inp("cnk", [L, 4, 512, 64])
    cnv = inp("cnv", [L, 4, 512, 64])
    srw = inp("srw", [L, 2, 4, 64, 64])
    sdn = inp("sdn", [L, 2, 4, 64, 64])
    rcos = inp("rcos", [4096, 32])
    rsin = inp("rsin", [4096, 32])
    nab = inp("nab", [L, 8, 4, 64, 512])
    woutn = inp("woutn", [L, 128, 8, D])
    XS = fw.dram("xs_scr", [4096, D], F32, "Internal")
    MIXD = fw.dram("mixd_scr", [4096, D], BF16, "Internal")
    ODD = [fw.dram(f"odd{i}", [64, 4, 4096], F32, "Internal") for i in range(2)]
    BOND = [fw.dram(f"bond{i}", [64, 4, 4096], F32, "Internal") for i in range(2)]
    GZD = fw.dram("gzd", [64, 4, 4096], F32, "Internal")
    PEER_ON = stage >= 6
    if PEER_ON:
        wq_d = inp("wq", [L, 128, 8, 2048])
        keyt_d = inp("keyt", [L, 128, 16, 128])
        pu_d = [inp(f"pu{i}", [16384, D]) for i in range(L)]
        pv_d = [inp(f"pv{i}", [16384, D]) for i in range(L)]

    y_prompt = fw.dram("y_prompt", [NTOK, D], F32, "ExternalOutput")
    y_sample = fw.dram("y_sample", [4096, D], F32, "ExternalOutput")
    o_gk = fw.dram("o_gk", [4, L, 2, T, 64], F32, "ExternalOutput")
    o_gv = fw.dram("o_gv", [4, L, 2, T, 64], F32, "ExternalOutput")
    o_nk = fw.dram("o_nk", [4, L, 4, T, 64], F32, "ExternalOutput")
    o_nv = fw.dram("o_nv", [4, L, 4, T, 64], F32, "ExternalOutput")
    o_rw = fw.dram("o_rw", [4, L, 2, 4, 64, 64], F32, "ExternalOutput")
    o_dn = fw.dram("o_dn", [4, L, 2, 4, 64, 64], F32, "ExternalOutput")
    dbgo = {}
    if dbg:
        dbgo["x"] = fw.dram("dbg_x", [NTOK, D], F32, "ExternalOutput")
        dbgo["h"] = fw.dram("dbg_h", [NTOK, D], F32, "ExternalOutput")
        dbgo["xs"] = fw.dram("dbg_xs", [4096, D], F32, "ExternalOutput")

    CON = fw.sb([128, 1024], F32, glob=True)
    ID16 = fw.sb([128, 128], BF16, glob=True)
    EPSC = fw.sb([128, 4], F32, glob=True)
    SS = fw.sb([128, 16], F32, glob=True)
    RSTD = fw.sb([128, 16], F32, glob=True)
    IDF = CON[:, 0:128]
    BONES = CON[:, 128:256]
    PS = [fw.psum([128, 512], F32) for _ in range(4)]
    PB = [fw.psum([128, 1024], BF16) for _ in range(2)]
    PSW = fw.psum([128, 1024], F32)
    ZO = CON[:, 320:576]
    IOTA16 = CON[:, 576:592]
    CSs = fw.sb([128, 8], F32, glob=True)
    fw.sect = ExitStack()
    X = fw.sb([128, NT, D], F32, glob=True)
    G1 = fw.sb([128, D], F32, glob=True)
    HT = fw.sb([128, 8, NTOK], BF16, glob=True)
    CS = fw.sb([128, 8], F32, glob=True)

    def mod_chunks(l, n0, dsts, cs=None):
        cs = CS if cs is None else cs
        WM = [fw.sb([128, 8, 512], F32) for _ in range(2)]
        BM = [fw.sb([128, 512], F32) for _ in range(2)]
        for i, dst in enumerate(dsts):
            n = n0 + i
            fw.ld(WM[n % 2], wmod[l, :, :, n * 512:(n + 1) * 512])
            fw.ld(BM[n % 2], V(bmod.ap[l:l + 1, n * 512:(n + 1) * 512].partition_broadcast(128), bmod.res))
            ps = PS[n % 2]
            for c in range(8):
                fw.mm(ps, cs[:, c:c + 1].bc([128, 128]), WM[n % 2][:, c, :], start=(c == 0), stop=(c == 7))
            fw.tt(dve, dst, ps, BM[n % 2], ALU.add)
    outs = [y_prompt, y_sample, o_gk, o_gv, o_nk, o_nv, o_rw, o_dn] + list(dbgo.values())

    fw.begin()
    fw.ld(CON, consts)
    fw.cp(dve, ID16, IDF)
    fw.memset(dve, EPSC[:, 0:1], EPS)
    fw.memset(dve, EPSC[:, 1:2], 1.0)
    fw.memset(dve, EPSC[:, 2:3], RW_LN_EPS)
    fw.ld(X, xp.r("(j p) d -> p j d", p=128))
    CT = fw.sb([128, 8], F32)
    fw.ld(CT, cctx)
    fw.actf(CS, CT, AF.Silu)
    fw.end()

    def rms_tile(j, A, Sh, HB, JK, xt=None):
        xt = X[:, j, :] if xt is None else xt
        k = j % 16
        fw.actf(JK, xt, AF.Square, accum=SS[:, k:k + 1])
        fw.actf(RSTD[:, k:k + 1], SS[:, k:k + 1], AF.Sqrt, bias=EPSC[:, 0:1], scale=1.0 / D)
        fw.recip(RSTD[:, k:k + 1], RSTD[:, k:k + 1])
        fw.stt(JK, xt, RSTD[:, k:k + 1], A, ALU.mult, ALU.mult)
        fw.tt(dve, HB, JK, Sh, ALU.add)

    def to_ht(j, HB, dst=None):
        pb = PB[j % 2]
        for c in range(8):
            fw.tr(pb[:, c * 128:(c + 1) * 128], HB[:, c * 128:(c + 1) * 128], ID16)
        dst = HT[:, :, j * 128:(j + 1) * 128] if dst is None else dst
        fw.cp(act, dst, pb.r("p (c t) -> p c t", c=8))

    def load_w(l, col0, ncols):
        W16 = fw.sb([128, 8, ncols], BF16)
        WS = [fw.sb([128, ncols], F32) for _ in range(2)]
        for c in range(8):
            fw.ld(WS[c % 2], win[l, :, c, col0:col0 + ncols])
            fw.cp(pool, W16[:, c, :], WS[c % 2])
        return W16

    def load_wout(l, ch0):
        WO = fw.sb([64, 4, D], BF16)
        WS = [fw.sb([64, D], F32) for _ in range(2)]
        for c in range(4):
            fw.ld(WS[c % 2], wout[l, :, ch0 + c, :])
            fw.cp(pool, WO[:, c, :], WS[c % 2])
        return WO

    def wout_acc(MIXT, WO):
        TMP = [fw.sb([128, 512], F32)] * 2
        for j in range(NT):
            for n in range(2):
                ps = PS[(2 * j + n) % 4]
                for c in range(4):
                    fw.mm(ps, MIXT[:, c, j * 128:(j + 1) * 128], WO[:, c, n * 512:(n + 1) * 512], start=(c == 0), stop=(c == 3))
                tmp = TMP[n]
                fw.tt(dve, tmp, ps, G1[:, n * 512:(n + 1) * 512], ALU.mult)
                fw.tt(pool, X[:, j, n * 512:(n + 1) * 512], X[:, j, n * 512:(n + 1) * 512], tmp, ALU.add)

    def attention(QT, KT, VV, MIXT):
        PF = [fw.sb([128, 256], F32) for _ in range(2)]
        PN = [fw.sb([128, 256], BF16) for _ in range(2)]
        PT = [fw.sb([128, 2, 128], BF16) for _ in range(2)]
        ST = [fw.sb([128, 4], F32) for _ in range(2)]
        u = 0
        for b in range(4):
            for h in range(4):
                for qi in range(2):
                    k = u % 2
                    u += 1
                    tq0 = b * 256 + qi * 128
                    sps = PS[k]
                    fw.mm(sps[:, 0:256], QT(h)[:, tq0:tq0 + 128], KT(h)[:, b * 256:(b + 1) * 256])
                    st = ST[k]
                    fw.red(st[:, 0:1], sps[:, 0:256], ALU.max)
                    fw.ts(dve, st[:, 1:2], st[:, 0:1], -0.125, ALU.mult)
                    fw.actf(PF[k], sps[:, 0:256], AF.Exp, bias=st[:, 1:2], scale=0.125, accum=st[:, 2:3])
                    fw.recip(st[:, 3:4], st[:, 2:3])
                    fw.ts(dve, PN[k], PF[k], st[:, 3:4], ALU.mult)
                    pb = PB[k]
                    for kt in range(2):
                        fw.tr(pb[:, kt * 128:(kt + 1) * 128], PN[k][:, kt * 128:(kt + 1) * 128], ID16)
                    fw.cp(act, PT[k], pb[:, 0:256].r("p (a t) -> p a t", a=2))
                    ops = PS[2 + k]
                    for kt in range(2):
                        fw.mm(ops[0:64, 0:128], VV(h, b, kt), PT[k][:, kt, :], start=(kt == 0), stop=(kt == 1))
                    fw.cp(act, MIXT[:, h, tq0:tq0 + 128], ops[0:64, 0:128])


    def scan(W_s, B_s, K_s, A16, R2, VTM, OS, M):
        M16 = [[fw.sb([128, 2, 64], BF16) for d in range(2)] for s_ in range(2)]
        T1 = [[fw.sb([128, 2, 64], F32) for d in range(2)] for s_ in range(2)]
        SAp = [[V(PS[s_].ap[:, d * 128:(d + 1) * 128], Res()) for d in range(2)] for s_ in range(2)]
        Vp = [[V(PS[2 + s_].ap[:, d * 128:(d + 1) * 128], Res()) for d in range(2)] for s_ in range(2)]
        Op = [[V(PSW.ap[0:64, (s_ * 2 + d) * 128:(s_ * 2 + d + 1) * 128].rearrange("p (t j m) -> p t j m", t=32, j=2), Res())
               for d in range(2)] for s_ in range(2)]
        fw._barrier()
        for s_ in range(2):
            for d in range(2):
                fw.memset(dve, M[s_][d], 0.0)
                fw.memset(dve, M16[s_][d], 0.0)
        for i in range(T):
            i0 = (i // 32) * 32
            sl = i - i0
            for s_ in range(2):
                for d in range(2):
                    td = i if d == 0 else T - 1 - i
                    tau = s_ * T + td
                    slot = sl if d == 0 else 31 - sl
                    Mv, Mb, t1 = M[s_][d], M16[s_][d], T1[s_][d]
                    sap, vp = SAp[s_][d], Vp[s_][d]
                    for j in range(2):
                        g = d * 2 + j
                        for m in range(2):
                            fw.mm(sap[m * 64:(m + 1) * 64, j * 64:(j + 1) * 64], A16[m * 64:(m + 1) * 64, g, tau:tau + 1].bc([64, 64]),
                                  Mb[m * 64:(m + 1) * 64, j, :])
                    for m in range(2):
                        fw.mm(vp[m * 64:(m + 1) * 64, :], ID16[:, tau % 128:tau % 128 + 1].bc([128, 64]), VTM[:, tau // 128, d, m, :])
                    gs = slice(d * 2, d * 2 + 2)
                    fw.tt(dve, Mv, Mv, W_s[:, gs, tau:tau + 1].bc([128, 2, 64]), ALU.mult)
                    fw.tt(dve, t1, sap.r("p (j v) -> p j v", j=2), B_s[:, gs, tau:tau + 1].bc([128, 2, 64]), ALU.mult)
                    fw.tt(dve, Mv, Mv, t1, ALU.add)
                    fw.tt(dve, t1, vp.r("p (j v) -> p j v", j=2), K_s[:, gs, tau:tau + 1].bc([128, 2, 64]), ALU.mult)
                    fw.tt(dve, Mv, Mv, t1, ALU.add)
                    fw.cp(act, Mb, Mv)
                    for j in range(2):
                        fw.mm(Op[s_][d][:, slot, j, :], Mb[:, j, :], R2[:, d * 2 + j, tau, :])
                    if sl == 31:
                        t0 = i0 if d == 0 else T - 32 - i0
                        dst = OS[:, :, s_ * T + t0:s_ * T + t0 + 32].r("p (j m) t -> p t j m", j=2)
                        if i0 < T // 2:
                            fw.cp(act, dst, Op[s_][d])
                        else:
                            fw.tt(dve, dst, dst, Op[s_][d], ALU.add)
        fw._barrier()

    def proj_fm(ps, W16, col0, ncol, tok0):
        for c in range(8):
            fw.mm(ps[0:ncol, :], W16[:, c, col0:col0 + ncol], HT[:, c, tok0:tok0 + 512], start=(c == 0), stop=(c == 7))

    def peer_phase(l, ntiles, cs, get_x, put_x):
        fw.begin()
        WQ = fw.sb([128, 8, 2048], BF16)
        WS = [fw.sb([128, 1024], F32) for _ in range(2)]
        for c in range(16):
            fw.ld(WS[c % 2], wq_d[l, :, c // 2, (c % 2) * 1024:(c % 2 + 1) * 1024])
            fw.cp(pool, WQ[:, c // 2, (c % 2) * 1024:(c % 2 + 1) * 1024], WS[c % 2])
        KEYS = fw.sb([128, 16, 128], BF16)
        for c in range(2):
            fw.ld(WS[c], keyt_d[l, :, c * 8:(c + 1) * 8, :].r("p a k -> p (a k)"))
            fw.cp(pool, KEYS[:, c * 8:(c + 1) * 8, :], WS[c].r("p (a k) -> p a k", a=8))
        JK = fw.sb([128, D], F32)
        MODB = fw.sb([128, 3 * D], F32)
        S3, S4, G2 = MODB[:, 0:D], MODB[:, D:2 * D], MODB[:, 2 * D:3 * D]
        mod_chunks(l, 6, [MODB[:, n * 512:(n + 1) * 512] for n in range(6)], cs)
        fw.ld(JK, V(n2g.ap[l:l + 1, :].partition_broadcast(128), n2g.res))
        fw.stt(S4, S4, 1.0, JK, ALU.add, ALU.mult)
        H2Bs = [fw.sb([128, D], BF16) for _ in range(2)]
        QPT = fw.sb([128, 16, 128], BF16)
        SC = fw.sb([128, 16, 128], F32)
        S1t = fw.sb([128, 16, 16], F32)
        I1t = fw.sb([128, 16, 16], U32)
        I1f = fw.sb([128, 16, 16], F32)
        WK = fw.sb([128, 256], F32)
        CAND = fw.sb([128, 8, 16, 16], F32)
        EQ = UG[0].r("p (h c i) -> p h c i", h=8, c=16)[:, :, :, 0:16] if False else fw.sb([128, 8, 16, 16], F32)
        TOP = fw.sb([128, 8, 16], F32)
        POS = fw.sb([128, 8, 16], U32)
        PF_ = fw.sb([128, 8, 16], F32)
        PJ = fw.sb([128, 8, 16], F32)
        PI = fw.sb([128, 8, 16], F32)
        SEL = fw.sb([128, 2, 8, 16], F32)
        GT = fw.sb([128, 8, 16], F32)
        G8 = fw.sb([128, 8], F32)
        IDXF = fw.sb([128, 128], F32)
        IDXT = fw.sb([128, 128], I32)
        GTT = fw.sb([128, 128], F32)
        ACT1 = fw.sb([128, 128], F32)
        GA = fw.sb([128, 128], F32)
        GAM = [fw.sb([128, 128], F32) for _ in range(2)]
        NRING = 2 if ntiles == NT else 10
        UG = [fw.sb([128, D], F32) for _ in range(NRING)]
        VG = UG
        TMPX = JK[:, 0:512]
        pu_l = pu_d[l]
        pv_l = pv_d[l]
        HTl = fw.sb([128, 8, 128], BF16)
        for j in range(ntiles):
            H2B = H2Bs[j % 2]
            xt = get_x(j)
            rms_tile(j, S4, S3, H2B, JK, xt)
            to_ht(j, H2B, HTl)
            for cc in range(16):
                ps = PS[cc % 2]
                for c in range(8):
                    fw.mm(ps[:, 0:128], WQ[:, c, cc * 128:(cc + 1) * 128], HTl[:, c, :], start=(c == 0), stop=(c == 7))
                fw.cp(act, QPT[:, cc, :], ps[:, 0:128])
            for q in range(4):
                ps = PS[2 + q % 2]
                for i in range(4):
                    cc = q * 4 + i
                    fw.mm(ps[:, i * 128:(i + 1) * 128], QPT[:, cc, :], KEYS[:, cc, :])
                fw.cp(act, SC[:, q * 4:(q + 1) * 4, :], ps.r("p (a k) -> p a k", a=4))

            def top16(vals, idx, src, wk):
                fw.op(dve, lambda h: h.max(out=vals[:, 0:8].ap, in_=src.ap), [src], [vals])
                fw.op(dve, lambda h: h.max_index(out=idx[:, 0:8].ap, in_max=vals[:, 0:8].ap, in_values=src.ap), [src, vals], [idx])
                fw.op(dve, lambda h: h.match_replace(out=wk.ap, in_to_replace=vals[:, 0:8].ap, in_values=src.ap, imm_value=-1e30), [src, vals], [wk])
                fw.op(dve, lambda h: h.max(out=vals[:, 8:16].ap, in_=wk.ap), [wk], [vals])
                fw.op(dve, lambda h: h.max_index(out=idx[:, 8:16].ap, in_max=vals[:, 8:16].ap, in_values=wk.ap), [wk, vals], [idx])

            for cc in range(16):
                top16(S1t[:, cc, :], I1t[:, cc, :], SC[:, cc, :], WK[:, 0:128])
            S1v = S1t.r("p (h two) k -> p h two k", two=2)
            fw.tt(dve, CAND, S1v[:, :, 0, :].us(3).bc([128, 8, 16, 16]), S1v[:, :, 1, :].us(2).bc([128, 8, 16, 16]), ALU.add)
            for hh in range(8):
                top16(TOP[:, hh, :], POS[:, hh, :], CAND[:, hh].r("p i j -> p (i j)"), WK)
            fw.tt(dve, GT, TOP, TOP[:, :, 0:1].bc([128, 8, 16]), ALU.subtract)
            fw.actf(GT, GT, AF.Exp)
            fw.red(G8, GT, ALU.add)
            fw.recip(G8, G8)
            fw.tt(dve, GT, GT, G8.us(2).bc([128, 8, 16]), ALU.mult)
            fw.cp(dve, PF_, POS)
            fw.cp(dve, I1f, I1t)
            fw.ts(dve, PI, PF_, 16.0, ALU.is_ge)
            for i_ in range(2, 16):
                fw.stt(PI, PF_, 16.0 * i_, PI, ALU.is_ge, ALU.add)
            fw.stt(PJ, PI, -16.0, PF_, ALU.mult, ALU.add)
            I1v = I1f.r("p (h two) k -> p h two k", two=2)
            for w_, PP in enumerate((PI, PJ)):
                fw.tt(dve, EQ, PP.us(3).bc([128, 8, 16, 16]), IOTA16.us(1).us(1).bc([128, 8, 16, 16]), ALU.is_equal)
                fw.tt(dve, EQ, EQ, I1v[:, :, w_, :].us(2).bc([128, 8, 16, 16]), ALU.mult)
                fw.red(SEL[:, w_], EQ, ALU.add)
            fw.stt(IDXF.r("p (h k) -> p h k", h=8), SEL[:, 0], 128.0, SEL[:, 1], ALU.mult, ALU.add)
            fw.tr(PS[0][:, 0:128], IDXF, IDF)
            fw.cp(dve, IDXT, PS[0][:, 0:128])
            fw.tr(PS[1][:, 0:128], GT.r("p h k -> p (h k)"), IDF)
            fw.cp(dve, GTT, PS[1][:, 0:128])
            for n in range(128):
                ug = UG[n % NRING]
                fw.dma(pool, lambda h, ug=ug, n=n: h.indirect_dma_start(out=ug.ap, out_offset=None, in_=pu_l.ap,
                                                                       in_offset=bass.IndirectOffsetOnAxis(ap=IDXT.ap[:, n:n + 1], axis=0)),
                       [IDXT, pu_l], [ug])
                for n2 in range(2):
                    fw.mm(PSW[:, n2 * 512:(n2 + 1) * 512], ID16[:, n:n + 1].bc([128, 128]), H2B[:, n2 * 512:(n2 + 1) * 512])
                fw.op(dve, lambda h, ug=ug, n=n: h.scalar_tensor_tensor(out=JK.ap, in0=ug.ap, scalar=1.0, in1=PSW.ap,
                                                                        op0=ALU.mult, op1=ALU.mult, accum_out=ACT1.ap[:, n:n + 1]),
                      [ug, PSW], [JK, ACT1])
            fw.actf(GA, ACT1, AF.Gelu)
            fw.tt(dve, GA, GA, GTT, ALU.mult)
            for n in range(128):
                vg = VG[n % NRING]
                fw.dma(pool, lambda h, vg=vg, n=n: h.indirect_dma_start(out=vg.ap, out_offset=None, in_=pv_l.ap,
                                                                       in_offset=bass.IndirectOffsetOnAxis(ap=IDXT.ap[:, n:n + 1], axis=0)),
                       [IDXT, pv_l], [vg])
                gm = GAM[n % 2]
                fw.ts(dve, gm, ZO[:, 128 - n:256 - n], GA[:, n:n + 1], ALU.mult)
                for n2 in range(2):
                    fw.mm(PSW[:, n2 * 512:(n2 + 1) * 512], gm, vg[:, n2 * 512:(n2 + 1) * 512], start=(n == 0), stop=(n == 127))
            for n2 in range(2):
                fw.tt(dve, TMPX, PSW[:, n2 * 512:(n2 + 1) * 512], G2[:, n2 * 512:(n2 + 1) * 512], ALU.mult)
                fw.tt(dve, xt[:, n2 * 512:(n2 + 1) * 512], xt[:, n2 * 512:(n2 + 1) * 512], TMPX, ALU.add)
            put_x(j, xt)
        fw.end()


    for l in range(nl if do_prompt else 0):
        fw.begin()
        MODA = fw.sb([128, 2 * D], F32)
        S1, S2 = MODA[:, 0:D], MODA[:, D:2 * D]
        mod_chunks(l, 0, [MODA[:, n * 512:(n + 1) * 512] for n in range(4)] + [G1[:, n * 512:(n + 1) * 512] for n in range(2)])
        GN = fw.sb([128, D], F32)
        fw.ld(GN, V(n1g.ap[l:l + 1, :].partition_broadcast(128), n1g.res))
        fw.stt(S2, S2, 1.0, GN, ALU.add, ALU.mult)
        JK = fw.sb([128, D], F32)
        HBs = [fw.sb([128, D], BF16) for _ in range(2)]
        for j in range(NT):
            rms_tile(j, S2, S1, HBs[j % 2], JK)
            to_ht(j, HBs[j % 2])
        fw.end()
        if stage <= 1:
            break

        fw.begin()
        W16 = load_w(l, CA, 512)
        WO = load_wout(l, 0)
        GQ = fw.sb([128, 2, 64], F32)
        fw.ld(GQ, V(gqn.ap[l:l + 1].partition_broadcast(128), gqn.res))
        QKT = fw.sb([128, 3, NTOK], BF16)
        V16 = fw.sb([128, NT, 128], BF16)
        MIXT = fw.sb([64, 4, NTOK], BF16)
        ATM = [fw.sb([128, 512], F32) for _ in range(2)]
        SQ = fw.sb([128, 384], F32)
        QK16 = [fw.sb([128, 384], BF16) for _ in range(2)]
        s6 = fw.sb([128, 8], F32)
        for j in range(NT):
            ps = PS[j % 2]
            for c in range(8):
                fw.mm(ps, HT[:, c, j * 128:(j + 1) * 128], W16[:, c, :], start=(c == 0), stop=(c == 7))
            a = ATM[j % 2]
            fw.cp(act, a, ps)
            fw.actf(SQ, a[:, 0:384], AF.Square)
            fw.red(s6[:, 0:6], SQ.r("p (h d) -> p h d", d=64), ALU.add)
            fw.actf(s6[:, 0:6], s6[:, 0:6], AF.Sqrt, bias=EPSC[:, 0:1], scale=1.0 / 64)
            fw.recip(s6[:, 0:6], s6[:, 0:6])
            qk = a[:, 0:384].r("p (h d) -> p h d", d=64)
            fw.tt(dve, qk, qk, s6[:, 0:6].us(2).bc([128, 6, 64]), ALU.mult)
            fw.tt(dve, qk[:, 0:4, :], qk[:, 0:4, :], GQ[:, 0:1, :].bc([128, 4, 64]), ALU.mult)
            fw.tt(dve, qk[:, 4:6, :], qk[:, 4:6, :], GQ[:, 1:2, :].bc([128, 2, 64]), ALU.mult)
            b, t0 = j // 2, (j % 2) * 128
            fw.ld(V(o_gk.ap[b, l, :, t0:t0 + 128, :].rearrange("k t d -> t k d"), o_gk.res), a[:, 256:384].r("p (k d) -> p k d", d=64))
            fw.ld(V(o_gv.ap[b, l, :, t0:t0 + 128, :].rearrange("k t d -> t k d"), o_gv.res), a[:, 384:512].r("p (k d) -> p k d", d=64))
            fw.cp(act, QK16[j % 2], a[:, 0:384])
            fw.cp(act, V16[:, j, :], a[:, 384:512])
            pb = PB[j % 2]
            for c in range(3):
                fw.tr(pb[:, c * 128:(c + 1) * 128], QK16[j % 2][:, c * 128:(c + 1) * 128], ID16)
            fw.cp(act, QKT[:, :, j * 128:(j + 1) * 128], pb[:, 0:384].r("p (c t) -> p c t", c=3))
        attention(lambda h: QKT[(h % 2) * 64:(h % 2) * 64 + 64, h // 2, :],
                  lambda h: QKT[(h % 2) * 64:(h % 2) * 64 + 64, 2, :],
                  lambda h, b, kt: V16[:, 2 * b + kt, (h % 2) * 64:(h % 2) * 64 + 64], MIXT)
        wout_acc(MIXT, WO)
        fw.end()
        if stage <= 2:
            break

        fw.begin()
        W16 = load_w(l, CC, 768)
        WO = load_wout(l, 8)
        QKT = fw.sb([128, 4, NTOK], BF16)
        V16 = fw.sb([128, NT, 256], BF16)
        MIXT = fw.sb([64, 4, NTOK], BF16)
        CTM = [fw.sb([128, 768], F32) for _ in range(2)]
        C16 = [fw.sb([128, 512], BF16) for _ in range(2)]
        for j in range(NT):
            a = CTM[j % 2]
            for n, (c0, nn) in enumerate(((0, 512), (512, 256))):
                ps = PS[(2 * j + n) % 4]
                for c in range(8):
                    fw.mm(ps[:, 0:nn], HT[:, c, j * 128:(j + 1) * 128], W16[:, c, c0:c0 + nn], start=(c == 0), stop=(c == 7))
                fw.cp(act, a[:, c0:c0 + nn], ps[:, 0:nn])
            b, t0 = j // 2, (j % 2) * 128
            fw.ld(V(o_nk.ap[b, l, :, t0:t0 + 128, :].rearrange("k t d -> t k d"), o_nk.res), a[:, 256:512].r("p (k d) -> p k d", d=64))
            fw.ld(V(o_nv.ap[b, l, :, t0:t0 + 128, :].rearrange("k t d -> t k d"), o_nv.res), a[:, 512:768].r("p (k d) -> p k d", d=64))
            fw.cp(act, C16[j % 2], a[:, 0:512])
            fw.cp(dve, V16[:, j, :], a[:, 512:768])
            pb = PB[j % 2]
            for c in range(4):
                fw.tr(pb[:, c * 128:(c + 1) * 128], C16[j % 2][:, c * 128:(c + 1) * 128], ID16)
            fw.cp(act, QKT[:, :, j * 128:(j + 1) * 128], pb[:, 0:512].r("p (c t) -> p c t", c=4))
        attention(lambda h: QKT[(h % 2) * 64:(h % 2) * 64 + 64, h // 2, :],
                  lambda h: QKT[(h % 2) * 64:(h % 2) * 64 + 64, 2 + h // 2, :],
                  lambda h, b, kt: V16[:, 2 * b + kt, h * 64:h * 64 + 64], MIXT)
        wout_acc(MIXT, WO)
        fw.end()
        if stage <= 3:
            continue

        if stage >= 4:
            fw.begin()
            W16 = load_w(l, CB, 1024)
            WO = load_wout(l, 4)
            RWP = fw.sb([128, 64], F32)
            fw.ld(RWP, rwp_d[l])
            RWU = fw.sb([64, 32], F32)
            fw.ld(RWU, rwu_d[l])
            WSt = fw.sb([128, 1024], F32)
            W2A = fw.sb([128, 2, 2, 256], BF16)
            fw.ld(WSt, rww_d[l].r("p a b c -> p (a b c)"))
            fw.cp(pool, W2A, WSt.r("p (a b c) -> p a b c", a=2, b=2))
            G2P = fw.sb([128, 256], BF16)
            fw.ld(WSt[:, 0:256], rwg_d[l])
            fw.cp(pool, G2P, WSt[:, 0:256])
            OMKA = fw.sb([128, 2], F32)
            fw.ts(dve, OMKA, RWP[:, 24:26], -1.0, ALU.mult, 1.0, ALU.add)
            MIXT = fw.sb([64, 4, NTOK], BF16)
            GS16 = fw.sb([128, 512], BF16)
            BON = fw.sb([64, 4, 512], F32)
            XR = fw.sb([128, 2, 512], F32)
            XK = fw.sb([128, 2, 512], F32)
            XV = fw.sb([128, 2, 512], F32)
            XL = fw.sb([128, 512], F32)
            BTc = [fw.sb([128, 512], F32) for _ in range(2)]
            TW16 = fw.sb([128, 512], BF16)
            XL16 = fw.sb([128, 512], BF16)
            SG = fw.sb([128, 512], F32)
            AG = fw.sb([128, 512], F32)
            KK = fw.sb([128, 512], F32)
            SQ = fw.sb([128, 512], F32)
            RN = fw.sb([128, 512], F32)
            KE = fw.sb([128, 512], F32)
            DF = SQ
            VUt = RN[0:64, :]
            W_s = fw.sb([128, 4, 512], F32)
            B_s = fw.sb([128, 4, 512], BF16)
            K_s = fw.sb([128, 4, 512], BF16)
            A16 = fw.sb([128, 4, 512], BF16)
            R2 = fw.sb([128, 4, 512, 2], BF16)
            VTM = fw.sb([128, 4, 2, 2, 128], BF16)
            OS = fw.sb([64, 4, 512], F32)
            ST = fw.sb([64, 128], F32)
            M = [[fw.sb([128, 2, 64], F32) for d in range(2)] for s_ in range(2)]
            fw.memset(pool, R2, 0.0)
            for sp_ in range(2):
                tok0 = sp_ * 512
                ps = PS[0]
                proj_fm(ps, W16, 896, 128, tok0)
                fw.actf(GS16, ps, AF.Sigmoid)
                for d in range(2):
                    dsts = [XR[:, 0, :], XR[:, 1, :], XK[:, 0, :], XK[:, 1, :], XV[:, 0, :], XV[:, 1, :], XL]
                    for c in range(7):
                        ps = PS[c % 2]
                        bt = BTc[c % 2]
                        proj_fm(ps, W16, c * 128, 128, tok0)
                        fw.cp(act, bt, ps)
                        b3 = bt.r("p (s t) -> p s t", s=2)
                        d3 = DF.r("p (s t) -> p s t", s=2)
                        if d == 0:
                            fw.tt(dve, d3[:, :, 1:T], b3[:, :, 0:T - 1], b3[:, :, 1:T], ALU.subtract)
                            fw.ts(dve, d3[:, :, 0:1], b3[:, :, 0:1], -1.0, ALU.mult)
                        else:
                            fw.tt(dve, d3[:, :, 0:T - 1], b3[:, :, 1:T], b3[:, :, 0:T - 1], ALU.subtract)
                            fw.ts(dve, d3[:, :, T - 1:T], b3[:, :, T - 1:T], -1.0, ALU.mult)
                        fw.stt(dsts[c], DF, RWP[:, d * 7 + c:d * 7 + c + 1], bt, ALU.mult, ALU.add)
                    fw.actf(TW16, XL, AF.Tanh)
                    fw.cp(act, XL16, XL)
                    for j in range(2):
                        g = d * 2 + j
                        fw.mm(PS[2], W2A[:, d, 0, j * 128:(j + 1) * 128], TW16)
                        fw.actf(SG, PS[2], AF.Sigmoid, bias=RWP[:, 14 + d * 2 + j:15 + d * 2 + j])
                        fw.actf(W_s[:, g, :], SG, AF.Exp, scale=-math.exp(-0.5))
                        fw.mm(PS[3], W2A[:, d, 1, j * 128:(j + 1) * 128], XL16)
                        fw.actf(AG, PS[3], AF.Sigmoid, bias=RWP[:, 18 + d * 2 + j:19 + d * 2 + j])
                        fw.ts(dve, KK, XK[:, j, :], RWP[:, 22 + j:23 + j], ALU.mult)
                        fw.actf(SQ, KK, AF.Square)
                        fw.mm(PS[2], BONES, SQ)
                        fw.actf(RN, PS[2], AF.Sqrt, bias=EPSC[:, 0:1])
                        fw.recip(RN, RN)
                        fw.tt(dve, KK, KK, RN, ALU.mult)
                        fw.ts(dve, A16[:, g, :], KK, -1.0, ALU.mult)
                        fw.tt(dve, B_s[:, g, :], KK, AG, ALU.mult)
                        fw.ts(dve, SQ, AG, RWP[:, 24 + j:25 + j], ALU.mult, OMKA[:, j:j + 1], ALU.add)
                        fw.tt(dve, KE, XK[:, j, :], SQ, ALU.mult)
                        fw.cp(act, K_s[:, g, :], KE)
                        fw.cp(act, R2[0:64, g, :, 0], XR[0:64, j, :])
                        fw.cp(act, R2[64:128, g, :, 1], XR[64:128, j, :])
                        fw.stt(KE, XR[:, j, :], RWP[:, 28 + j:29 + j], KE, ALU.mult, ALU.mult)
                        for m in range(2):
                            hh = 2 * j + m
                            fw.mm(PS[2][0:64, :], BONES[:, m * 64:(m + 1) * 64], KE)
                            fw.mm(PS[3][0:64, :], IDF[:, m * 64:(m + 1) * 64], XV[:, j, :])
                            fw.cp(act, VUt, PS[3][0:64, :])
                            if d == 0:
                                fw.tt(dve, BON[:, hh, :], PS[2][0:64, :], VUt, ALU.mult)
                            else:
                                fw.tt(dve, VUt, PS[2][0:64, :], VUt, ALU.mult)
                                fw.tt(dve, BON[:, hh, :], BON[:, hh, :], VUt, ALU.add)
                    for tt_ in range(4):
                        ps = PS[tt_ % 2]
                        for j in range(2):
                            fw.tr(ps[:, j * 128:(j + 1) * 128], XV[:, j, tt_ * 128:(tt_ + 1) * 128], IDF)
                        fw.cp(act, VTM[:, tt_, d, :, :].r("p m (j v) -> p j m v", j=2), ps[:, 0:256].r("p (j m v) -> p j m v", j=2, m=2))
                scan(W_s, B_s, K_s, A16, R2, VTM, OS, M)
                for s_ in range(2):
                    b = sp_ * 2 + s_
                    for d in range(2):
                        for j in range(2):
                            fw.tr(PS[0][0:64, 0:128], M[s_][d][:, j, :], IDF)
                            fw.cp(act, ST, PS[0][0:64, 0:128])
                            fw.ld(V(o_rw.ap[b, l, d, 2 * j:2 * j + 2].rearrange("m v k -> v m k"), o_rw.res), ST.r("v (m k) -> v m k", m=2))
                for hh in range(4):
                    ONES = CON[0:64, 256:320]
                    fw.mm(PS[0][0:64, :], ONES, OS[:, hh, :])
                    fw.tt(dve, SQ[0:64, :], OS[:, hh, :], PS[0][0:64, :], ALU.subtract)
                    fw.actf(RN[0:64, :], SQ[0:64, :], AF.Square)
                    fw.mm(PS[1][0:64, :], ONES, RN[0:64, :])
                    fw.actf(RN[0:64, :], PS[1][0:64, :], AF.Sqrt, bias=EPSC[0:64, 2:3])
                    fw.recip(RN[0:64, :], RN[0:64, :])
                    fw.tt(dve, SQ[0:64, :], SQ[0:64, :], RN[0:64, :], ALU.mult)
                    fw.ts(dve, SQ[0:64, :], SQ[0:64, :], RWU[:, 8 + hh:9 + hh], ALU.mult, RWU[:, 12 + hh:13 + hh], ALU.add)
                    fw.tt(dve, SQ[0:64, :], SQ[0:64, :], BON[:, hh, :], ALU.add)
                    fw.mm(PS[2][0:64, :], G2P[:, hh * 64:(hh + 1) * 64], GS16)
                    fw.tt(dve, MIXT[:, hh, tok0:tok0 + 512], SQ[0:64, :], PS[2][0:64, :], ALU.mult)
            wout_acc(MIXT, WO)
            fw.end()

        if stage >= 5:
            fw.begin()
            W16 = load_w(l, CD, 1152)
            WO = load_wout(l, 12)
            DNP = fw.sb([128, 48], F32)
            fw.ld(DNP, dnp_d[l])
            EXPM = fw.sb([128, 2, 4, 128], F32)
            fw.ld(EXPM, expm_d)
            CONVW = DNP[:, 0:30].r("p (c j) -> p c j", j=5)
            NAe = fw.sb([128, 1], F32)
            fw.actf(NAe, DNP[:, 31:32], AF.Exp)
            fw.ts(dve, NAe, NAe, -1.0, ALU.mult)
            MIXT = fw.sb([64, 4, NTOK], BF16)
            BT1 = [fw.sb([128, 512], F32) for _ in range(2)]
            CV = fw.sb([128, 6, 512], F32)
            SQ = fw.sb([128, 512], F32)
            RN = fw.sb([128, 512], F32)
            KB = fw.sb([128, 512], F32)
            SB_ = fw.sb([128, 512], F32)
            GG = fw.sb([128, 512], F32)
            W_s = fw.sb([128, 4, 512], F32)
            B_s = fw.sb([128, 4, 512], BF16)
            K_s = fw.sb([128, 4, 512], BF16)
            A16 = fw.sb([128, 4, 512], BF16)
            R2 = fw.sb([128, 4, 512, 2], BF16)
            VTM = fw.sb([128, 4, 2, 2, 128], BF16)
            ZU = fw.sb([64, 4, 512], F32)
            OS = fw.sb([64, 4, 512], F32)
            M = [[fw.sb([128, 2, 64], F32) for d in range(2)] for s_ in range(2)]
            fw.memset(pool, R2, 0.0)
            for sp_ in range(2):
                tok0 = sp_ * 512
                for ch in range(6):
                    ps = PS[ch % 2]
                    bt = BT1[ch % 2]
                    proj_fm(ps, W16, ch * 128, 128, tok0)
                    fw.cp(act, bt, ps)
                    x3 = bt.r("p (s t) -> p s t", s=2)
                    a3 = CV[:, ch, :].r("p (s t) -> p s t", s=2)
                    fw.ts(dve, CV[:, ch, :], bt, CONVW[:, ch, 2:3], ALU.mult)
                    for jj, off in ((0, -2), (1, -1), (3, 1), (4, 2)):
                        if off < 0:
                            dst, src = a3[:, :, -off:T], x3[:, :, 0:T + off]
                        else:
                            dst, src = a3[:, :, 0:T - off], x3[:, :, off:T]
                        fw.stt(dst, src, CONVW[:, ch, jj:jj + 1], dst, ALU.mult, ALU.add)
                    fw.actf(CV[:, ch, :], CV[:, ch, :], AF.Silu)
                for ch in range(4):
                    fw.actf(SQ, CV[:, ch, :], AF.Square)
                    ps = PS[ch % 2]
                    fw.mm(ps, BONES, SQ)
                    fw.actf(RN, ps, AF.Sqrt, bias=EPSC[:, 0:1])
                    fw.recip(RN, RN)
                    fw.tt(dve, CV[:, ch, :], CV[:, ch, :], RN, ALU.mult)
                for g in range(4):
                    j = g % 2
                    fw.ts(dve, R2[0:64, g, :, 0], CV[0:64, j, :], 0.125, ALU.mult)
                    fw.ts(dve, R2[64:128, g, :, 1], CV[64:128, j, :], 0.125, ALU.mult)
                    fw.cp(act, A16[:, g, :], CV[:, 2 + j, :])
                ps = PS[0]
                proj_fm(ps, W16, 1024, 128, tok0)
                fw.actf(SB_, ps, AF.Sigmoid)
                fw.actf(GG, ps, AF.Exp, bias=DNP[:, 30:31])
                fw.actf(GG, GG, AF.Ln, bias=EPSC[:, 1:2])
                fw.ts(dve, GG, GG, NAe[:, 0:1], ALU.mult)
                for g in range(4):
                    j = g % 2
                    fw.mm(PS[2], EXPM[:, 0, g, :], SB_)
                    fw.mm(PS[3], EXPM[:, 1, g, :], GG)
                    fw.actf(W_s[:, g, :], PS[3], AF.Exp)
                    fw.tt(dve, KB, CV[:, 2 + j, :], PS[2], ALU.mult)
                    fw.cp(act, K_s[:, g, :], KB)
                    fw.stt(B_s[:, g, :], KB, -1.0, W_s[:, g, :], ALU.mult, ALU.mult)
                for tt_ in range(4):
                    ps = PS[tt_ % 2]
                    for j in range(2):
                        fw.tr(ps[:, j * 128:(j + 1) * 128], CV[:, 4 + j, tt_ * 128:(tt_ + 1) * 128], IDF)
                    for d in range(2):
                        fw.cp(act, VTM[:, tt_, d, :, :].r("p m (j v) -> p j m v", j=2), ps[:, 0:256].r("p (j m v) -> p j m v", j=2, m=2))
                for hh in range(4):
                    ps = PS[2 + hh % 2]
                    proj_fm(ps, W16, 768 + hh * 64, 64, tok0)
                    fw.actf(ZU[:, hh, :], ps[0:64, :], AF.Silu)
                scan(W_s, B_s, K_s, A16, R2, VTM, OS, M)
                for s_ in range(2):
                    b = sp_ * 2 + s_
                    for d in range(2):
                        for j in range(2):
                            fw.ld(V(o_dn.ap[b, l, d, 2 * j:2 * j + 2].rearrange("m k v -> (m k) v"), o_dn.res), M[s_][d][:, j, :])
                for hh in range(4):
                    fw.actf(SQ[0:64, :], OS[:, hh, :], AF.Square)
                    ps = PS[hh % 2]
                    fw.mm(ps[0:64, :], CON[0:64, 256:320], SQ[0:64, :])
                    fw.actf(RN[0:64, :], ps[0:64, :], AF.Sqrt, bias=EPSC[0:64, 0:1])
                    fw.recip(RN[0:64, :], RN[0:64, :])
                    fw.tt(dve, RN[0:64, :], RN[0:64, :], OS[:, hh, :], ALU.mult)
                    fw.stt(MIXT[:, hh, tok0:tok0 + 512], RN[0:64, :], DNP[0:64, 32:33], ZU[:, hh, :], ALU.mult, ALU.mult)
            wout_acc(MIXT, WO)
            fw.end()

        if PEER_ON:
            peer_phase(l, NT, CS, lambda j: X[:, j, :], lambda j, xt: None)

    fw.begin()
    if dbg and do_prompt:
        fw.ld(dbgo["x"].r("(j p) d -> p j d", p=128), X)
    FN = fw.sb([128, D], F32)
    fw.ld(FN, V(fng.ap[0:1, :].partition_broadcast(128), fng.res))
    JK = fw.sb([128, D], F32)
    YT = [fw.sb([128, D], F32) for _ in range(2)]
    for j in range(NT if do_prompt else 0):
        fw.actf(JK, X[:, j, :], AF.Square, accum=SS[:, j:j + 1])
        fw.actf(RSTD[:, j:j + 1], SS[:, j:j + 1], AF.Sqrt, bias=EPSC[:, 0:1], scale=1.0 / D)
        fw.recip(RSTD[:, j:j + 1], RSTD[:, j:j + 1])
        fw.stt(YT[j % 2], X[:, j, :], RSTD[:, j:j + 1], FN, ALU.mult, ALU.mult)
        fw.ld(y_prompt[j * 128:(j + 1) * 128, :], YT[j % 2])
    fw.end()
    fw.sect.close()
    fw.sect = None
    TS, NTS = 4096, 32

    def sample_section(sstage):
        H = {}
        fw.begin()
        if sstage < 5:
            ZT = fw.sb([128, D], BF16)
            fw.memset(dve, ZT, 0.0)
            for j in range(NTS):
                fw.ld(MIXD[j * 128:(j + 1) * 128, :], ZT)
        CT = fw.sb([128, 8], F32)
        fw.ld(CT, cs_in)
        fw.actf(CSs, CT, AF.Silu)
        fw.end()

        def proj_s(ps, W16, col0, ncol, tok0, n=512):
            for c in range(8):
                fw.mm(ps[0:ncol, 0:n], W16[:, c, col0:col0 + ncol], H['HTs'][:, c, tok0:tok0 + n], start=(c == 0), stop=(c == 7))

        def scan_s(W_s, B_s, K_s, A16, R2, VTM, OSb, M, M16, T1, SAp, Vp, Op):
            fw._barrier()
            for i in range(512):
                i0 = (i // 32) * 32
                sl = i - i0
                for d in range(2):
                    tau = i if d == 0 else 511 - i
                    slot = sl if d == 0 else 31 - sl
                    Mv, Mb, t1 = M[d], M16[d], T1[d]
                    sap, vp = SAp[d], Vp[d]
                    for j in range(2):
                        g = d * 2 + j
                        for m in range(2):
                            fw.mm(sap[m * 64:(m + 1) * 64, j * 64:(j + 1) * 64], A16[m * 64:(m + 1) * 64, g, tau:tau + 1].bc([64, 64]),
                                  Mb[m * 64:(m + 1) * 64, j, :])
                    for m in range(2):
                        fw.mm(vp[m * 64:(m + 1) * 64, :], ID16[:, tau % 128:tau % 128 + 1].bc([128, 64]), VTM[:, tau // 128, d, m, :])
                    gs = slice(d * 2, d * 2 + 2)
                    fw.tt(dve, Mv, Mv, W_s[:, gs, tau:tau + 1].bc([128, 2, 64]), ALU.mult)
                    fw.tt(dve, t1, sap.r("p (j v) -> p j v", j=2), B_s[:, gs, tau:tau + 1].bc([128, 2, 64]), ALU.mult)
                    fw.tt(dve, Mv, Mv, t1, ALU.add)
                    fw.tt(dve, t1, vp.r("p (j v) -> p j v", j=2), K_s[:, gs, tau:tau + 1].bc([128, 2, 64]), ALU.mult)
                    fw.tt(dve, Mv, Mv, t1, ALU.add)
                    fw.cp(act, Mb, Mv)
                    for j in range(2):
                        fw.mm(Op[d][:, slot, j, :], Mb[:, j, :], R2[:, d * 2 + j, tau, :])
                    if sl == 31:
                        t0 = i0 if d == 0 else 480 - i0
                        fw.cp(act, OSb[d][:, :, t0:t0 + 32].r("p (j m) t -> p t j m", j=2), Op[d])
            fw._barrier()

        def scan_res():
            SAp = [V(PS[d].ap[:, 0:128], Res()) for d in range(2)]
            Vp = [V(PS[2 + d].ap[:, 0:128], Res()) for d in range(2)]
            Op = [V(PSW.ap[0:64, d * 128:(d + 1) * 128].rearrange("p (t j m) -> p t j m", t=32, j=2), Res()) for d in range(2)]
            return SAp, Vp, Op

        def y_to_mixd(Y, tok0, col0):
            OTt = [fw.sb([128, 256], BF16) for _ in range(2)]
            for tt_ in range(4):
                pb = PB[tt_ % 2]
                for hh in range(4):
                    fw.tr(pb[:, hh * 64:(hh + 1) * 64], Y[:, hh, tt_ * 128:(tt_ + 1) * 128], ID16[0:64, 0:64])
                fw.cp(act, OTt[tt_ % 2], pb[:, 0:256])
                fw.ld(MIXD[tok0 + tt_ * 128:tok0 + (tt_ + 1) * 128, col0:col0 + 256], OTt[tt_ % 2])

        for l in range(nl):
            xsrc = xs_in if l == 0 else XS
            fw.sect = ExitStack()
            H['HTs'] = fw.sb([128, 8, TS], BF16, glob=True)
            H['G1s'] = fw.sb([128, D], F32, glob=True)
            fw.begin()
            MODA = fw.sb([128, 2 * D], F32)
            S1, S2 = MODA[:, 0:D], MODA[:, D:2 * D]
            mod_chunks(l, 0, [MODA[:, n * 512:(n + 1) * 512] for n in range(4)] + [H['G1s'][:, n * 512:(n + 1) * 512] for n in range(2)], CSs)
            GN = fw.sb([128, D], F32)
            fw.ld(GN, V(n1g.ap[l:l + 1, :].partition_broadcast(128), n1g.res))
            fw.stt(S2, S2, 1.0, GN, ALU.add, ALU.mult)
            JK = fw.sb([128, D], F32)
            HBs = [fw.sb([128, D], BF16) for _ in range(2)]
            XT = [fw.sb([128, D], F32) for _ in range(2)]
            for j in range(NTS):
                fw.ld(XT[j % 2], xsrc[j * 128:(j + 1) * 128, :])
                rms_tile(j, S2, S1, HBs[j % 2], JK, XT[j % 2])
                to_ht(j, HBs[j % 2], H['HTs'][:, :, j * 128:(j + 1) * 128])
            fw.end()

            if sstage >= 2:
                fw.begin()
                W16 = load_w(l, CA, 512)
                GQ = fw.sb([128, 2, 64], F32)
                fw.ld(GQ, V(gqn.ap[l:l + 1].partition_broadcast(128), gqn.res))
                COS = fw.sb([128, NTS, 32], F32)
                SIN = fw.sb([128, NTS, 32], F32)
                fw.ld(COS, rcos.r("(j p) f -> p j f", p=128))
                fw.ld(SIN, rsin.r("(j p) f -> p j f", p=128))
                QT = fw.sb([128, 2, TS], BF16)
                KT = fw.sb([128, TS + 512], BF16)
                Vt = fw.sb([128, 36, 128], BF16)
                ATM = [fw.sb([128, 512], F32) for _ in range(2)]
                SQ = fw.sb([128, 384], F32)
                QK16 = [fw.sb([128, 384], BF16) for _ in range(2)]
                s6 = fw.sb([128, 8], F32)
                R1 = fw.sb([128, 6, 32], F32)
                R2_ = fw.sb([128, 6, 32], F32)
                for j in range(NTS):
                    ps = PS[j % 2]
                    for c in range(8):
                        fw.mm(ps, H['HTs'][:, c, j * 128:(j + 1) * 128], W16[:, c, :], start=(c == 0), stop=(c == 7))
                    a = ATM[j % 2]
                    fw.cp(act, a, ps)
                    fw.actf(SQ, a[:, 0:384], AF.Square)
                    fw.red(s6[:, 0:6], SQ.r("p (h d) -> p h d", d=64), ALU.add)
                    fw.actf(s6[:, 0:6], s6[:, 0:6], AF.Sqrt, bias=EPSC[:, 0:1], scale=1.0 / 64)
                    fw.recip(s6[:, 0:6], s6[:, 0:6])
                    qk = a[:, 0:384].r("p (h d) -> p h d", d=64)
                    fw.tt(dve, qk, qk, s6[:, 0:6].us(2).bc([128, 6, 64]), ALU.mult)
                    fw.tt(dve, qk[:, 0:4, :], qk[:, 0:4, :], GQ[:, 0:1, :].bc([128, 4, 64]), ALU.mult)
                    fw.tt(dve, qk[:, 4:6, :], qk[:, 4:6, :], GQ[:, 1:2, :].bc([128, 2, 64]), ALU.mult)
                    x1, x2 = qk[:, :, 0:32], qk[:, :, 32:64]
                    cb = COS[:, j, :].us(1).bc([128, 6, 32])
                    sb_ = SIN[:, j, :].us(1).bc([128, 6, 32])
                    q16 = QK16[j % 2].r("p (h d) -> p h d", d=64)
                    fw.tt(dve, R1, x1, cb, ALU.mult)
                    fw.tt(pool, R2_, x2, sb_, ALU.mult)
                    fw.tt(dve, q16[:, :, 0:32], R1, R2_, ALU.subtract)
                    fw.tt(dve, R1, x2, cb, ALU.mult)
                    fw.tt(pool, R2_, x1, sb_, ALU.mult)
                    fw.tt(dve, q16[:, :, 32:64], R1, R2_, ALU.add)
                    fw.cp(act, Vt[:, j, :], a[:, 384:512])
                    pb = PB[j % 2]
                    for c in range(3):
                        fw.tr(pb[:, c * 128:(c + 1) * 128], QK16[j % 2][:, c * 128:(c + 1) * 128], ID16)
                    fw.cp(act, QT[:, :, j * 128:(j + 1) * 128], pb[:, 0:256].r("p (c t) -> p c t", c=2))
                    fw.cp(act, KT[:, j * 128:(j + 1) * 128], pb[:, 256:384])
                CK = fw.sb([128, 4, 2, 64], F32)
                CK16 = fw.sb([128, 4, 128], BF16)
                for k_ in range(2):
                    fw.ld(CK[:, :, k_, :], cgk[l, k_].r("(i p) d -> p i d", p=128))
                fw.cp(dve, CK16, CK.r("p i k d -> p i (k d)"))
                for i in range(4):
                    fw.tr(PB[0][:, i * 128:(i + 1) * 128], CK16[:, i, :], ID16)
                fw.cp(act, KT[:, TS:TS + 512], PB[0][:, 0:512])
                for k_ in range(2):
                    fw.ld(CK[:, :, k_, :], cgv[l, k_].r("(i p) d -> p i d", p=128))
                fw.cp(dve, Vt[:, 32:36, :], CK.r("p i k d -> p i (k d)"))
                S = fw.sb([128, 4608], F32)
                P = fw.sb([128, 4608], BF16)
                PT = fw.sb([128, 36, 128], BF16)
                st = fw.sb([128, 4], F32)
                OT = [fw.sb([128, 256], BF16) for _ in range(2)]
                ops = PSW[:, 0:64]
                for j in range(NTS):
                    ot = OT[j % 2]
                    for hh in range(4):
                        pbase = (hh % 2) * 64
                        qv = QT[pbase:pbase + 64, hh // 2, j * 128:(j + 1) * 128]
                        for kc in range(9):
                            ps = PS[kc % 4]
                            fw.mm(ps, qv, KT[pbase:pbase + 64, kc * 512:(kc + 1) * 512])
                            fw.cp(act if kc % 2 else dve, S[:, kc * 512:(kc + 1) * 512], ps)
                        fw.red(st[:, 0:1], S, ALU.max)
                        fw.ts(dve, st[:, 1:2], st[:, 0:1], -0.125, ALU.mult)
                        fw.actf(P, S, AF.Exp, bias=st[:, 1:2], scale=0.125, accum=st[:, 2:3])
                        fw.recip(st[:, 3:4], st[:, 2:3])
                        for grp in range(5):
                            nb = min(8, 36 - grp * 8)
                            pb = PB[grp % 2]
                            for b8 in range(nb):
                                blk = grp * 8 + b8
                                fw.tr(pb[:, b8 * 128:(b8 + 1) * 128], P[:, blk * 128:(blk + 1) * 128], ID16)
                            fw.cp(act if grp % 2 else dve, PT[:, grp * 8:grp * 8 + nb, :], pb[:, 0:nb * 128].r("p (a t) -> p a t", a=nb))
                        kv = hh % 2
                        for blk in range(36):
                            fw.mm(ops, PT[:, blk, :], Vt[:, blk, kv * 64:(kv + 1) * 64], start=(blk == 0), stop=(blk == 35))
                        ho = (0, 2, 1, 3)[hh]
                        fw.ts(dve, ot[:, ho * 64:(ho + 1) * 64], ops, st[:, 3:4], ALU.mult)
                    fw.ld(MIXD[j * 128:(j + 1) * 128, 0:256], ot)
                fw.end()

            if sstage >= 3:
                fw.begin()
                W16 = load_w(l, CC, 768)
                QTc = fw.sb([128, 2, TS], BF16)
                KTc = fw.sb([128, 2, TS + 512], BF16)
                VN = fw.sb([128, NTS, 256], BF16)
                VNs = fw.sb([128, NTS, 256], BF16)
                C16 = [fw.sb([128, 512], BF16) for _ in range(2)]
                for j in range(NTS):
                    pq, pv_ = PS[(2 * j) % 4], PS[(2 * j + 1) % 4]
                    for c in range(8):
                        fw.mm(pq, H['HTs'][:, c, j * 128:(j + 1) * 128], W16[:, c, 0:512], start=(c == 0), stop=(c == 7))
                    for c in range(8):
                        fw.mm(pv_[:, 0:256], H['HTs'][:, c, j * 128:(j + 1) * 128], W16[:, c, 512:768], start=(c == 0), stop=(c == 7))
                    NB_ = 9
                    fw.cp(act, C16[j % 2], pq)
                    if NB_ >= 2:
                        fw.cp(dve, VN[:, j, :], pv_[:, 0:256])
                    pb = PB[j % 2]
                    for c in range(4 if NB_ >= 3 else 0):
                        fw.tr(pb[:, c * 128:(c + 1) * 128], C16[j % 2][:, c * 128:(c + 1) * 128], ID16)
                    if NB_ >= 4:
                        fw.cp(act, QTc[:, :, j * 128:(j + 1) * 128], pb[:, 0:256].r("p (c t) -> p c t", c=2))
                    if NB_ >= 5:
                        fw.cp(act, KTc[:, :, j * 128:(j + 1) * 128], pb[:, 256:512].r("p (c t) -> p c t", c=2))
                NAP = 9
                for j in range(NTS - 1 if NAP >= 2 else 0):
                    ps = PS[j % 4]
                    for c in range(8):
                        fw.mm(ps[:, 0:256], H['HTs'][:, c, 64 + j * 128:64 + (j + 1) * 128], W16[:, c, 512:768], start=(c == 0), stop=(c == 7))
                    fw.cp(act if j % 2 else dve, VNs[:, j, :], ps[:, 0:256])
                CKn = fw.sb([128, 4, 4, 64], F32)
                CK16 = fw.sb([128, 4, 256], BF16)
                Vctx = fw.sb([128, 4, 256], BF16)
                for k_ in range(4 if NAP >= 3 else 0):
                    fw.ld(CKn[:, :, k_, :], cnk[l, k_].r("(i p) d -> p i d", p=128))
                fw.cp(dve, CK16, CKn.r("p i h d -> p i (h d)"))
                for c in range(2 if NAP >= 3 else 0):
                    for i in range(4):
                        fw.tr(PB[c][:, i * 128:(i + 1) * 128], CK16[:, i, c * 128:(c + 1) * 128], ID16)
                    fw.cp(act, KTc[:, c, TS:TS + 512], PB[c][:, 0:512])
                for k_ in range(4):
                    fw.ld(CKn[:, :, k_, :], cnv[l, k_].r("(i p) d -> p i d", p=128))
                fw.cp(dve, Vctx, CKn.r("p i h d -> p i (h d)"))
                NBI = fw.sb([64, 4, 512], F32)
                NBE = fw.sb([64, 4, 512], F32)
                if NAP >= 4:
                    fw.ld(NBI, nab[l, 4].r("h q k -> q h k"))
                Sx = fw.sb([64, 1024], F32)
                Px = fw.sb([64, 1024], BF16)
                PTx = fw.sb([128, 8, 64], BF16)
                stx = fw.sb([64, 4], F32)
                OTx = [fw.sb([64, 256], BF16) for _ in range(2)]
                opx = PSW[0:64, 0:64]
                for r in range(64):
                    r0 = min(max(r - 4, 0), 56)
                    dl = r - r0
                    if dl != 4:
                        fw.ld(NBE, nab[l, dl].r("h q k -> q h k"))
                    NB = NBI if dl == 4 else NBE
                    ot = OTx[r % 2]
                    for h_ in range(4):
                        pbase, c = (h_ % 2) * 64, h_ // 2
                        qv = QTc[pbase:pbase + 64, c, r * 64:(r + 1) * 64]
                        fw.mm(PS[0][0:64, :], qv, KTc[pbase:pbase + 64, c, r0 * 64:r0 * 64 + 512])
                        fw.mm(PS[1][0:64, :], qv, KTc[pbase:pbase + 64, c, TS:TS + 512])
                        fw.stt(Sx[:, 0:512], PS[0][0:64, :], 0.125, NB[:, h_, :], ALU.mult, ALU.add)
                        fw.ts(dve, Sx[:, 512:1024], PS[1][0:64, :], 0.125, ALU.mult)
                        fw.red(stx[:, 0:1], Sx, ALU.max)
                        fw.ts(dve, stx[:, 1:2], stx[:, 0:1], -1.0, ALU.mult)
                        fw.actf(Px, Sx, AF.Exp, bias=stx[:, 1:2], scale=1.0, accum=stx[:, 2:3])
                        fw.recip(stx[:, 3:4], stx[:, 2:3])
                        for blk in range(8):
                            fw.tr(PB[0][:, blk * 64:(blk + 1) * 64], Px[:, blk * 128:(blk + 1) * 128], ID16[0:64, 0:64])
                        fw.cp(act, PTx, PB[0][:, 0:512].r("p (a t) -> p a t", a=8))
                        for blk in range(4):
                            vt = VN[:, r0 // 2 + blk, h_ * 64:(h_ + 1) * 64] if r0 % 2 == 0 else VNs[:, (r0 - 1) // 2 + blk, h_ * 64:(h_ + 1) * 64]
                            fw.mm(opx, PTx[:, blk, :], vt, start=(blk == 0), stop=False)
                        for blk in range(4):
                            fw.mm(opx, PTx[:, 4 + blk, :], Vctx[:, blk, h_ * 64:(h_ + 1) * 64], start=False, stop=(blk == 3))
                        fw.ts(dve, ot[:, h_ * 64:(h_ + 1) * 64], opx, stx[:, 3:4], ALU.mult)
                    fw.ld(MIXD[r * 64:(r + 1) * 64, 512:768], ot)
                fw.end()

            if sstage >= 4:
                fw.begin()
                W16 = load_w(l, CB, 1024)
                RWP = fw.sb([128, 64], F32)
                fw.ld(RWP, rwp_d[l])
                WSt = fw.sb([128, 1024], F32)
                W2A = fw.sb([128, 2, 2, 256], BF16)
                fw.ld(WSt, rww_d[l].r("p a b c -> p (a b c)"))
                fw.cp(pool, W2A, WSt.r("p (a b c) -> p a b c", a=2, b=2))
                G2P = fw.sb([128, 256], BF16)
                fw.ld(WSt[:, 0:256], rwg_d[l])
                fw.cp(pool, G2P, WSt[:, 0:256])
                OMKA = fw.sb([128, 2], F32)
                fw.ts(dve, OMKA, RWP[:, 24:26], -1.0, ALU.mult, 1.0, ALU.add)
                GS16 = fw.sb([128, 512], BF16)
                BON = fw.sb([64, 4, 512], F32)
                XR = fw.sb([128, 2, 512], F32)
                XK = fw.sb([128, 2, 512], F32)
                XV = fw.sb([128, 2, 512], F32)
                XL = fw.sb([128, 512], F32)
                BTc = [fw.sb([128, 513], F32) for _ in range(2)]
                TW16 = fw.sb([128, 512], BF16)
                XL16 = fw.sb([128, 512], BF16)
                SG = fw.sb([128, 512], F32)
                AG = fw.sb([128, 512], F32)
                KK = fw.sb([128, 512], F32)
                SQ = fw.sb([128, 512], F32)
                RN = fw.sb([128, 512], F32)
                KE = fw.sb([128, 512], F32)
                DF = SQ
                VUt = RN[0:64, :]
                W_s = fw.sb([128, 4, 512], F32)
                B_s = fw.sb([128, 4, 512], BF16)
                K_s = fw.sb([128, 4, 512], BF16)
                A16 = fw.sb([128, 4, 512], BF16)
                R2 = fw.sb([128, 4, 512, 2], BF16)
                VTM = fw.sb([128, 4, 2, 2, 128], BF16)
                OSb = [fw.sb([64, 4, 512], F32) for _ in range(2)]
                M = [fw.sb([128, 2, 64], F32) for _ in range(2)]
                M16 = [fw.sb([128, 2, 64], BF16) for _ in range(2)]
                T1 = [fw.sb([128, 2, 64], F32) for _ in range(2)]
                SAp, Vp, Op = scan_res()
                fw.memset(pool, R2, 0.0)
                S0t = fw.sb([64, 2, 64], F32)
                for d in range(2):
                    for j in range(2):
                        fw.ld(S0t, V(srw.ap[l, d, 2 * j:2 * j + 2].rearrange("m v k -> v m k"), srw.res))
                        fw.tr(PS[0][:, 0:64], S0t.r("v m k -> v (m k)"), IDF[0:64, 0:64])
                        fw.cp(act, M[d][:, j, :], PS[0][:, 0:64])
                    fw.cp(act, M16[d], M[d])
                for kb in range(8):
                    for d in range(2):
                        tok0 = (kb if d == 0 else 7 - kb) * 512
                        if d == 0:
                            ps = PS[0]
                            proj_s(ps, W16, 896, 128, tok0)
                            fw.actf(GS16, ps, AF.Sigmoid)
                            for hh in range(4):
                                fw.mm(PS[2][0:64, :], G2P[:, hh * 64:(hh + 1) * 64], GS16)
                                fw.cp(act, BON[:, hh, :], PS[2][0:64, :])
                            fw.ld(GZD[:, :, tok0:tok0 + 512], BON)
                        dsts = [XR[:, 0, :], XR[:, 1, :], XK[:, 0, :], XK[:, 1, :], XV[:, 0, :], XV[:, 1, :], XL]
                        th = tok0 - 1 if d == 0 else tok0 + 512
                        for c in range(7):
                            ps = PS[c % 2]
                            bt = BTc[c % 2]
                            proj_s(ps, W16, c * 128, 128, tok0)
                            main = bt[:, 1:513] if d == 0 else bt[:, 0:512]
                            halo = bt[:, 0:1] if d == 0 else bt[:, 512:513]
                            fw.cp(act, main, ps)
                            if 0 <= th < TS:
                                proj_s(PS[2], W16, c * 128, 128, th, 1)
                                fw.cp(act, halo, PS[2][:, 0:1])
                            else:
                                fw.memset(dve, halo, 0.0)
                            if d == 0:
                                fw.tt(dve, DF, bt[:, 0:512], bt[:, 1:513], ALU.subtract)
                            else:
                                fw.tt(dve, DF, bt[:, 1:513], bt[:, 0:512], ALU.subtract)
                            fw.stt(dsts[c], DF, RWP[:, d * 7 + c:d * 7 + c + 1], main, ALU.mult, ALU.add)
                        fw.actf(TW16, XL, AF.Tanh)
                        fw.cp(act, XL16, XL)
                        for j in range(2):
                            g = d * 2 + j
                            fw.mm(PS[2], W2A[:, d, 0, j * 128:(j + 1) * 128], TW16)
                            fw.actf(SG, PS[2], AF.Sigmoid, bias=RWP[:, 14 + d * 2 + j:15 + d * 2 + j])
                            fw.actf(W_s[:, g, :], SG, AF.Exp, scale=-math.exp(-0.5))
                            fw.mm(PS[3], W2A[:, d, 1, j * 128:(j + 1) * 128], XL16)
                            fw.actf(AG, PS[3], AF.Sigmoid, bias=RWP[:, 18 + d * 2 + j:19 + d * 2 + j])
                            fw.ts(dve, KK, XK[:, j, :], RWP[:, 22 + j:23 + j], ALU.mult)
                            fw.actf(SQ, KK, AF.Square)
                            fw.mm(PS[2], BONES, SQ)
                            fw.actf(RN, PS[2], AF.Sqrt, bias=EPSC[:, 0:1])
                            fw.recip(RN, RN)
                            fw.tt(dve, KK, KK, RN, ALU.mult)
                            fw.ts(dve, A16[:, g, :], KK, -1.0, ALU.mult)
                            fw.tt(dve, B_s[:, g, :], KK, AG, ALU.mult)
                            fw.ts(dve, SQ, AG, RWP[:, 24 + j:25 + j], ALU.mult, OMKA[:, j:j + 1], ALU.add)
                            fw.tt(dve, KE, XK[:, j, :], SQ, ALU.mult)
                            fw.cp(act, K_s[:, g, :], KE)
                            fw.cp(act, R2[0:64, g, :, 0], XR[0:64, j, :])
                            fw.cp(act, R2[64:128, g, :, 1], XR[64:128, j, :])
                            fw.stt(KE, XR[:, j, :], RWP[:, 28 + j:29 + j], KE, ALU.mult, ALU.mult)
                            for m in range(2):
                                hh = 2 * j + m
                                fw.mm(PS[2][0:64, :], BONES[:, m * 64:(m + 1) * 64], KE)
                                fw.mm(PS[3][0:64, :], IDF[:, m * 64:(m + 1) * 64], XV[:, j, :])
                                fw.cp(act, VUt, PS[3][0:64, :])
                                fw.tt(dve, BON[:, hh, :], PS[2][0:64, :], VUt, ALU.mult)
                        fw.ld(BOND[d][:, :, tok0:tok0 + 512], BON)
                        for tt_ in range(4):
                            ps = PS[tt_ % 2]
                            for j in range(2):
                                fw.tr(ps[:, j * 128:(j + 1) * 128], XV[:, j, tt_ * 128:(tt_ + 1) * 128], IDF)
                            fw.cp(act, VTM[:, tt_, d, :, :].r("p m (j v) -> p j m v", j=2), ps[:, 0:256].r("p (j m v) -> p j m v", j=2, m=2))
                    scan_s(W_s, B_s, K_s, A16, R2, VTM, OSb, M, M16, T1, SAp, Vp, Op)
                    for d in range(2):
                        tok0 = (kb if d == 0 else 7 - kb) * 512
                        fw.ld(ODD[d][:, :, tok0:tok0 + 512], OSb[d])
                fw.end()
                fw.begin()
                RWU = fw.sb([64, 32], F32)
                fw.ld(RWU, rwu_d[l])
                OF = fw.sb([64, 4, 512], F32)
                OB = fw.sb([64, 4, 512], F32)
                B0 = fw.sb([64, 4, 512], F32)
                B1 = fw.sb([64, 4, 512], F32)
                GZ = fw.sb([64, 4, 512], F32)
                Y = fw.sb([64, 4, 512], BF16)
                SQ = fw.sb([64, 512], F32)
                RN = fw.sb([64, 512], F32)
                ONES = CON[0:64, 256:320]
                for ch in range(8):
                    tok0 = ch * 512
                    fw.ld(OF, ODD[0][:, :, tok0:tok0 + 512])
                    fw.ld(OB, ODD[1][:, :, tok0:tok0 + 512])
                    fw.ld(B0, BOND[0][:, :, tok0:tok0 + 512])
                    fw.ld(B1, BOND[1][:, :, tok0:tok0 + 512])
                    fw.ld(GZ, GZD[:, :, tok0:tok0 + 512])
                    fw.tt(pool, OF, OF, OB, ALU.add)
                    fw.tt(pool, B0, B0, B1, ALU.add)
                    for hh in range(4):
                        fw.mm(PS[0][0:64, :], ONES, OF[:, hh, :])
                        fw.tt(dve, SQ, OF[:, hh, :], PS[0][0:64, :], ALU.subtract)
                        fw.actf(RN, SQ, AF.Square)
                        fw.mm(PS[1][0:64, :], ONES, RN)
                        fw.actf(RN, PS[1][0:64, :], AF.Sqrt, bias=EPSC[0:64, 2:3])
                        fw.recip(RN, RN)
                        fw.tt(dve, SQ, SQ, RN, ALU.mult)
                        fw.ts(dve, SQ, SQ, RWU[:, 8 + hh:9 + hh], ALU.mult, RWU[:, 12 + hh:13 + hh], ALU.add)
                        fw.tt(dve, SQ, SQ, B0[:, hh, :], ALU.add)
                        fw.tt(dve, Y[:, hh, :], SQ, GZ[:, hh, :], ALU.mult)
                    y_to_mixd(Y, tok0, 256)
                fw.end()

            if sstage >= 5:
                fw.begin()
                W16 = load_w(l, CD, 1152)
                DNP = fw.sb([128, 48], F32)
                fw.ld(DNP, dnp_d[l])
                EXPM = fw.sb([128, 2, 4, 128], F32)
                fw.ld(EXPM, expm_d)
                CONVW = DNP[:, 0:30].r("p (c j) -> p c j", j=5)
                NAe = fw.sb([128, 1], F32)
                fw.actf(NAe, DNP[:, 31:32], AF.Exp)
                fw.ts(dve, NAe, NAe, -1.0, ALU.mult)
                BT1 = [fw.sb([128, 516], F32) for _ in range(2)]
                CV = fw.sb([128, 6, 512], F32)
                SQ = fw.sb([128, 512], F32)
                RN = fw.sb([128, 512], F32)
                KB = fw.sb([128, 512], F32)
                SB_ = fw.sb([128, 512], F32)
                GG = fw.sb([128, 512], F32)
                W_s = fw.sb([128, 4, 512], F32)
                B_s = fw.sb([128, 4, 512], BF16)
                K_s = fw.sb([128, 4, 512], BF16)
                A16 = fw.sb([128, 4, 512], BF16)
                R2 = fw.sb([128, 4, 512, 2], BF16)
                VTM = fw.sb([128, 4, 2, 2, 128], BF16)
                ZU = fw.sb([64, 4, 512], F32)
                OSb = [fw.sb([64, 4, 512], F32) for _ in range(2)]
                M = [fw.sb([128, 2, 64], F32) for _ in range(2)]
                M16 = [fw.sb([128, 2, 64], BF16) for _ in range(2)]
                T1 = [fw.sb([128, 2, 64], F32) for _ in range(2)]
                SAp, Vp, Op = scan_res()
                fw.memset(pool, R2, 0.0)
                for d in range(2):
                    for j in range(2):
                        fw.ld(M[d][:, j, :], V(sdn.ap[l, d, 2 * j:2 * j + 2].rearrange("m k v -> (m k) v"), sdn.res))
                    fw.cp(act, M16[d], M[d])
                for kb in range(8):
                    for d in range(2):
                        tok0 = (kb if d == 0 else 7 - kb) * 512
                        for ch in range(6):
                            ps = PS[ch % 2]
                            bt = BT1[ch % 2]
                            proj_s(ps, W16, ch * 128, 128, tok0)
                            fw.cp(act, bt[:, 2:514], ps)
                            if tok0 > 0:
                                proj_s(PS[2], W16, ch * 128, 128, tok0 - 2, 2)
                                fw.cp(act, bt[:, 0:2], PS[2][:, 0:2])
                            else:
                                fw.memset(dve, bt[:, 0:2], 0.0)
                            if tok0 + 512 < TS:
                                proj_s(PS[3], W16, ch * 128, 128, tok0 + 512, 2)
                                fw.cp(act, bt[:, 514:516], PS[3][:, 0:2])
                            else:
                                fw.memset(dve, bt[:, 514:516], 0.0)
                            fw.ts(dve, CV[:, ch, :], bt[:, 0:512], CONVW[:, ch, 0:1], ALU.mult)
                            for jj in range(1, 5):
                                fw.stt(CV[:, ch, :], bt[:, jj:jj + 512], CONVW[:, ch, jj:jj + 1], CV[:, ch, :], ALU.mult, ALU.add)
                            fw.actf(CV[:, ch, :], CV[:, ch, :], AF.Silu)
                        for ch in range(4):
                            fw.actf(SQ, CV[:, ch, :], AF.Square)
                            ps = PS[ch % 2]
                            fw.mm(ps, BONES, SQ)
                            fw.actf(RN, ps, AF.Sqrt, bias=EPSC[:, 0:1])
                            fw.recip(RN, RN)
                            fw.tt(dve, CV[:, ch, :], CV[:, ch, :], RN, ALU.mult)
                        ps = PS[0]
                        proj_s(ps, W16, 1024, 128, tok0)
                        fw.actf(SB_, ps, AF.Sigmoid)
                        fw.actf(GG, ps, AF.Exp, bias=DNP[:, 30:31])
                        fw.actf(GG, GG, AF.Ln, bias=EPSC[:, 1:2])
                        fw.ts(dve, GG, GG, NAe[:, 0:1], ALU.mult)
                        for j in range(2):
                            g = d * 2 + j
                            fw.ts(dve, R2[0:64, g, :, 0], CV[0:64, j, :], 0.125, ALU.mult)
                            fw.ts(dve, R2[64:128, g, :, 1], CV[64:128, j, :], 0.125, ALU.mult)
                            fw.cp(act, A16[:, g, :], CV[:, 2 + j, :])
                            fw.mm(PS[2], EXPM[:, 0, g, :], SB_)
                            fw.mm(PS[3], EXPM[:, 1, g, :], GG)
                            fw.actf(W_s[:, g, :], PS[3], AF.Exp)
                            fw.tt(dve, KB, CV[:, 2 + j, :], PS[2], ALU.mult)
                            fw.cp(act, K_s[:, g, :], KB)
                            fw.stt(B_s[:, g, :], KB, -1.0, W_s[:, g, :], ALU.mult, ALU.mult)
                        for tt_ in range(4):
                            ps = PS[tt_ % 2]
                            for j in range(2):
                                fw.tr(ps[:, j * 128:(j + 1) * 128], CV[:, 4 + j, tt_ * 128:(tt_ + 1) * 128], IDF)
                            fw.cp(act, VTM[:, tt_, d, :, :].r("p m (j v) -> p j m v", j=2), ps[:, 0:256].r("p (j m v) -> p j m v", j=2, m=2))
                        if d == 0:
                            for hh in range(4):
                                ps = PS[2 + hh % 2]
                                proj_s(ps, W16, 768 + hh * 64, 64, tok0)
                                fw.actf(ZU[:, hh, :], ps[0:64, :], AF.Silu)
                            fw.ld(GZD[:, :, tok0:tok0 + 512], ZU)
                    scan_s(W_s, B_s, K_s, A16, R2, VTM, OSb, M, M16, T1, SAp, Vp, Op)
                    for d in range(2):
                        tok0 = (kb if d == 0 else 7 - kb) * 512
                        fw.ld(ODD[d][:, :, tok0:tok0 + 512], OSb[d])
                fw.end()
                fw.begin()
                DNP = fw.sb([128, 48], F32)
                fw.ld(DNP, dnp_d[l])
                OF = fw.sb([64, 4, 512], F32)
                OB = fw.sb([64, 4, 512], F32)
                GZ = fw.sb([64, 4, 512], F32)
                Y = fw.sb([64, 4, 512], BF16)
                SQ = fw.sb([64, 512], F32)
                RN = fw.sb([64, 512], F32)
                ONES = CON[0:64, 256:320]
                for ch in range(8):
                    tok0 = ch * 512
                    fw.ld(OF, ODD[0][:, :, tok0:tok0 + 512])
                    fw.ld(OB, ODD[1][:, :, tok0:tok0 + 512])
                    fw.ld(GZ, GZD[:, :, tok0:tok0 + 512])
                    fw.tt(pool, OF, OF, OB, ALU.add)
                    for hh in range(4):
                        fw.actf(SQ, OF[:, hh, :], AF.Square)
                        fw.mm(PS[hh % 2][0:64, :], ONES, SQ)
                        fw.actf(RN, PS[hh % 2][0:64, :], AF.Sqrt, bias=EPSC[0:64, 0:1])
                        fw.recip(RN, RN)
                        fw.tt(dve, RN, RN, OF[:, hh, :], ALU.mult)
                        fw.stt(Y[:, hh, :], RN, DNP[0:64, 32:33], GZ[:, hh, :], ALU.mult, ALU.mult)
                    y_to_mixd(Y, tok0, 768)
                fw.end()

            fw.begin()
            WOn = fw.sb([128, 8, D], BF16)
            WS = [fw.sb([128, D], F32) for _ in range(2)]
            for c in range(8):
                fw.ld(WS[c % 2], woutn[l, :, c, :])
                fw.cp(pool, WOn[:, c, :], WS[c % 2])
            MT = [fw.sb([128, D], BF16) for _ in range(2)]
            MF = [fw.sb([128, 8, 128], BF16) for _ in range(2)]
            XT = [fw.sb([128, D], F32) for _ in range(2)]
            TMP = fw.sb([128, 512], F32)
            for j in range(NTS):
                k = j % 2
                fw.ld(MT[k], MIXD[j * 128:(j + 1) * 128, :])
                fw.ld(XT[k], xsrc[j * 128:(j + 1) * 128, :])
                pb = PB[k]
                for c in range(8):
                    fw.tr(pb[:, c * 128:(c + 1) * 128], MT[k][:, c * 128:(c + 1) * 128], ID16)
                fw.cp(act, MF[k], pb.r("p (c t) -> p c t", c=8))
                for n in range(2):
                    ps = PS[(2 * j + n) % 4]
                    for c in range(8):
                        fw.mm(ps, MF[k][:, c, :], WOn[:, c, n * 512:(n + 1) * 512], start=(c == 0), stop=(c == 7))
                    fw.tt(dve, TMP, ps, H['G1s'][:, n * 512:(n + 1) * 512], ALU.mult)
                    fw.tt(pool, XT[k][:, n * 512:(n + 1) * 512], XT[k][:, n * 512:(n + 1) * 512], TMP, ALU.add)
                fw.ld(XS[j * 128:(j + 1) * 128, :], XT[k])
            fw.end()
            fw.sect.close()
            fw.sect = None
            if sstage <= 5:
                continue

            ring = {}

            def get_x(j):
                if "xt" not in ring:
                    ring["xt"] = [fw.sb([128, D], F32) for _ in range(2)]
                xt = ring["xt"][j % 2]
                fw.ld(xt, XS[j * 128:(j + 1) * 128, :])
                return xt

            def put_x(j, xt):
                fw.ld(XS[j * 128:(j + 1) * 128, :], xt)

            peer_phase(l, NTS, CSs, get_x, put_x)

        fw.begin()
        FN = fw.sb([128, D], F32)
        fw.ld(FN, V(fng.ap[0:1, :].partition_broadcast(128), fng.res))
        JK = fw.sb([128, D], F32)
        XT = [fw.sb([128, D], F32) for _ in range(2)]
        YT = [fw.sb([128, D], F32) for _ in range(2)]
        for j in range(NTS):
            k = j % 16
            fw.ld(XT[j % 2], XS[j * 128:(j + 1) * 128, :])
            fw.actf(JK, XT[j % 2], AF.Square, accum=SS[:, k:k + 1])
            fw.actf(RSTD[:, k:k + 1], SS[:, k:k + 1], AF.Sqrt, bias=EPSC[:, 0:1], scale=1.0 / D)
            fw.recip(RSTD[:, k:k + 1], RSTD[:, k:k + 1])
            fw.stt(YT[j % 2], XT[j % 2], RSTD[:, k:k + 1], FN, ALU.mult, ALU.mult)
            fw.ld(y_sample[j * 128:(j + 1) * 128, :], YT[j % 2])
        if dbg:
            for j in range(NTS):
                fw.ld(XT[j % 2], XS[j * 128:(j + 1) * 128, :])
                fw.ld(dbgo["xs"][j * 128:(j + 1) * 128, :], XT[j % 2])
        fw.end()

    if sstage >= 1:
        sample_section(sstage)
    fw.begin()
    for o in outs:
        if o.res.lw is not None:
            fw._wait(fw.sp, o.res.lw)
    fw.end()
    fw.close()
    return fw.nc, fw


def host_inputs(inputs, core, peer=True):
    f = lambda a: np.ascontiguousarray(a, dtype=np.float32)
    L = DEPTH
    m = {}
    m["xp"] = f(inputs["x_prompt"][4 * core:4 * core + 4].reshape(NTOK, D))
    m["cctx"] = f(inputs["c_ctx"].reshape(8, 128).T)
    m["wmod"] = f(inputs["w_mod"].reshape(L, 8, 128, 6 * D).transpose(0, 2, 1, 3))
    m["bmod"] = f(inputs["b_mod"])
    m["n1g"] = f(inputs["norm1_g"])
    m["n2g"] = f(inputs["norm2_g"])
    m["fng"] = f(inputs["final_norm_g"].reshape(1, D))
    w = inputs["w_in"]
    cols = []
    for h in (0, 2, 1, 3):
        cols += list(range(h * 64, h * 64 + 64))
    cols += list(range(256, 512))
    cols += list(range(1472, 2240))
    wperm = w[:, :, cols]
    wfull = np.zeros((L, D, NWC), np.float32)
    wfull[:, :, 0:1280] = wperm
    wfull[:, :, CB:CB + 768] = w[:, :, 512:1280]
    wfull[:, :, CB + 768:CB + 896] = w[:, :, 1280:1408]
    wfull[:, :, CB + 896:CB + 960] = w[:, :, 1408:1472]
    wfull[:, :, CD:CD + 768] = w[:, :, 2240:3008]
    wfull[:, :, CD + 768:CD + 1024] = w[:, :, 3024:3280]
    wfull[:, :, CD + 1024:CD + 1040] = w[:, :, 3008:3024]
    m["win"] = f(wfull.reshape(L, 8, 128, NWC).transpose(0, 2, 1, 3))
    wo = inputs["w_out"]
    rows = []
    for h in (0, 2, 1, 3):
        rows += list(range(h * 64, h * 64 + 64))
    rows += list(range(256, 1024))
    m["wout"] = f(wo[:, rows, :].reshape(L, 16, 64, D).transpose(0, 2, 1, 3))
    m["gqn"] = f(np.stack([inputs["gqa_q_norm"], inputs["gqa_k_norm"]], axis=1))
    con = np.zeros((128, 1024), np.float32)
    con[:, 0:128] = np.eye(128)
    p = np.arange(128)
    con[:, 128:256] = (p[:, None] // 64 == p[None, :] // 64)
    con[0:64, 256:320] = 1.0 / 64
    con[:, 320 + 128] = 1.0
    con[:, 576:592] = np.arange(16)[None, :]
    m["consts"] = con
    rwp = np.zeros((L, 128, 64), np.float32)
    mu = inputs["rw_mu"]
    st2 = lambda v: v.reshape(v.shape[:-1] + (2, 128)).swapaxes(-1, -2)
    for d in range(2):
        rwp[:, :, d * 7:d * 7 + 6] = mu[:, d, 0:768].reshape(L, 6, 128).transpose(0, 2, 1)
        rwp[:, d * 32:d * 32 + 32, d * 7 + 6] = mu[:, d, 768:800]
        rwp[:, 64 + d * 32:96 + d * 32, d * 7 + 6] = mu[:, d, 800:832]
        rwp[:, :, 14 + d * 2:16 + d * 2] = st2(inputs["rw_w0"][:, d])
        rwp[:, :, 18 + d * 2:20 + d * 2] = st2(inputs["rw_a0"][:, d])
    rwp[:, :, 22:24] = st2(inputs["rw_k_k"])
    rwp[:, :, 24:26] = st2(inputs["rw_k_a"])
    rwp[:, :, 28:30] = st2(inputs["rw_r_k"].reshape(L, 256))
    m["rwp"] = rwp
    rwu = np.zeros((L, 64, 32), np.float32)
    rwu[:, :, 8:12] = inputs["rw_ln_g"].reshape(L, 4, 64).transpose(0, 2, 1)
    rwu[:, :, 12:16] = inputs["rw_ln_b"].reshape(L, 4, 64).transpose(0, 2, 1)
    m["rwu"] = rwu
    rww = np.zeros((L, 128, 2, 2, 256), np.float32)
    for d in range(2):
        rww[:, d * 32:d * 32 + 32, d, 0, :] = inputs["rw_w2"][:, d]
        rww[:, 64 + d * 32:96 + d * 32, d, 1, :] = inputs["rw_a2"][:, d]
    m["rww"] = rww
    rwg = np.zeros((L, 128, 256), np.float32)
    rwg[:, 0:64] = inputs["rw_g2"]
    m["rwg"] = rwg
    m["xs_in"] = f(inputs["x_sample"][core])
    m["cs_in"] = f(inputs["c"][core].reshape(8, 128).T)
    m["cgk"] = f(inputs["cache_gqa_k"][core])
    m["cgv"] = f(inputs["cache_gqa_v"][core])
    m["cnk"] = f(inputs["cache_na_k"][core])
    m["cnv"] = f(inputs["cache_na_v"][core])
    m["srw"] = f(inputs["state_rwkv"][core])
    m["sdn"] = f(inputs["state_delta"][core])
    m["rcos"], m["rsin"] = _rope_tables()
    m["nab"] = _na_bias_table(inputs["na_bias"])
    m["woutn"] = f(inputs["w_out"].reshape(L, 8, 128, D).transpose(0, 2, 1, 3))
    dnp = np.zeros((L, 128, 48), np.float32)
    cw = inputs["dn_conv"]
    dnp[:, :, 0:30] = cw.reshape(L, 5, 6, 128).transpose(0, 3, 2, 1).reshape(L, 128, 30)
    dnp[:, 8:16, 30] = inputs["dn_dt_bias"].reshape(L, 8)
    dnp[:, 8:16, 31] = inputs["dn_a_log"].reshape(L, 8)
    dnp[:, 0:64, 32] = inputs["dn_norm_g"]
    m["dnp"] = dnp
    ex = np.zeros((128, 2, 4, 128), np.float32)
    for d in range(2):
        for j in range(2):
            for mm_ in range(2):
                ex[d * 4 + 2 * j + mm_, 0, d * 2 + j, mm_ * 64:(mm_ + 1) * 64] = 1.0
                ex[8 + d * 4 + 2 * j + mm_, 1, d * 2 + j, mm_ * 64:(mm_ + 1) * 64] = 1.0
    m["expm"] = ex
    if not peer:
        return m
    m["wq"] = f(inputs["peer_wq"].reshape(L, 8, 128, 2048).transpose(0, 2, 1, 3))
    m["keyt"] = f(inputs["peer_keys"].reshape(L, 16, 128, 128).transpose(0, 3, 1, 2))
    for i in range(L):
        m[f"pu{i}"] = f(inputs["peer_u"][i])
        m[f"pv{i}"] = f(inputs["peer_v"][i])
    return m


def _rope_tables():
    t = np.arange(4096)
    row, col = t // 64, t % 64
    inv = 10000.0 ** (-2.0 * np.arange(16) / 32)
    ang = np.concatenate([row[:, None] * inv[None], col[:, None] * inv[None]], axis=1)
    return np.cos(ang).astype(np.float32), np.sin(ang).astype(np.float32)


def _na_bias_table(na_bias):
    L = na_bias.shape[0]
    qc = np.arange(64)
    kc = np.arange(64)
    cstart = np.clip(qc - 8, 0, 48)
    valid = (kc[None, :] >= cstart[:, None]) & (kc[None, :] < cstart[:, None] + 16)
    dc = np.clip(kc[None, :] - qc[:, None], -15, 15) + 15
    out = np.empty((L, 8, 4, 64, 8, 64), np.float32)
    for dl in range(8):
        for w in range(8):
            dr = w + 7 - dl
            g = na_bias[:, :, dr, :][:, :, dc]
            out[:, dl, :, :, w, :] = np.where(valid[None, None], g, np.float32(-1e30))
    return out.reshape(L, 8, 4, 64, 512)


_CACHE = {}


def kernel(**inputs):
    inputs = {k: np.asarray(v) for k, v in inputs.items()}
    if "nc" not in _CACHE:
        _CACHE["nc"] = build()[0]
    nc = _CACHE["nc"]
    in_maps = [host_inputs(inputs, c) for c in range(NCORES)]
    res = run_bass_kernel_spmd(nc, in_maps, core_ids=list(range(NCORES)))
    R = res.results
    cat = lambda k: np.concatenate([np.asarray(r[k]) for r in R], axis=0)
    y_prompt = cat("y_prompt").reshape(32, T, D)
    y_sample = np.stack([np.asarray(r["y_sample"]) for r in R], axis=0)
    return (y_prompt, y_sample, cat("o_gk"), cat("o_gv"), cat("o_nk"), cat("o_nv"), cat("o_rw"), cat("o_dn"))
```
